# Optimizing a Trainium2 kernel written in Bass

```python
import math
import jax, jax.numpy as jnp
from jax import lax
import numpy as np

D_MODEL = 1024
BATCH = 8
SEQ = 2048
DEPTH = 2
DEC_BATCH = 32
DEC_SEQ = 1
PAST_LEN = 8192
PAGE_SIZE = 128

N_A_LAYERS = DEPTH // 2
N_B_LAYERS = DEPTH - N_A_LAYERS
SSM_EXPAND = 2
D_INNER = SSM_EXPAND * D_MODEL
SSM_HEAD_DIM = 64
SSM_HEADS = D_INNER // SSM_HEAD_DIM
N_GROUPS = 4
HEADS_PER_GROUP = SSM_HEADS // N_GROUPS
D_STATE = 128
D_CONV = 4
CONV_DIM = D_INNER + 2 * N_GROUPS * D_STATE
IN_A_DIM = D_INNER + CONV_DIM + SSM_HEADS
CHUNK = 128
ATT_HEADS = D_MODEL // 128
HEAD_DIM = 64
V_DIM = 2 * HEAD_DIM
K_DIM = ATT_HEADS * 2 * HEAD_DIM
VAL_WIDTH = ATT_HEADS * V_DIM
IN_B_DIM = K_DIM + VAL_WIDTH
KV_DIM = K_DIM + VAL_WIDTH
ROT_DIM = HEAD_DIM // 4
ROPE_THETA = 500000.0
Q_BLOCK = 128
EPS = 1e-6
NEG_INF = -1e30

kernel_name = 'hybrid_ssd_diffattn_yoco_step'


def _rmsnorm(x, g):
    xf = x.astype(jnp.float32)
    xf = xf * lax.rsqrt(jnp.mean(xf * xf, axis=-1, keepdims=True) + EPS)
    return xf.astype(x.dtype) * g


def _pad_time(t, total):
    pad = total - t.shape[1]
    return jnp.pad(t, [(0, 0), (0, pad)] + [(0, 0)] * (t.ndim - 2))


def _ssd(x, dt, a, bm, cm, s0):
    b, L = x.shape[:2]
    q = min(CHUNK, L)
    nc = -(-L // q)
    lp = nc * q
    x, dt, bm, cm = (_pad_time(t, lp) for t in (x, dt, bm, cm))
    x = x.reshape(b, nc, q, N_GROUPS, HEADS_PER_GROUP, SSM_HEAD_DIM)
    dt = dt.reshape(b, nc, q, N_GROUPS, HEADS_PER_GROUP)
    bm = bm.reshape(b, nc, q, N_GROUPS, D_STATE).astype(jnp.float32)
    cm = cm.reshape(b, nc, q, N_GROUPS, D_STATE).astype(jnp.float32)
    a_cs = jnp.cumsum(dt * a, axis=2)
    xdt = x.astype(jnp.float32) * dt[..., None]
    seg = a_cs[:, :, :, None] - a_cs[:, :, None, :]
    causal = jnp.tril(jnp.ones((q, q), dtype=bool))[None, None, :, :, None, None]
    decay_ls = jnp.exp(jnp.where(causal, seg, -jnp.inf))
    cb = jnp.einsum('bclgn,bcsgn->bclsg', cm, bm)
    y_diag = jnp.einsum('bclsg,bclsgr,bcsgrp->bclgrp', cb, decay_ls, xdt)
    decay_s = jnp.exp(a_cs[:, :, -1:] - a_cs)
    states = jnp.einsum('bcsgn,bcsgr,bcsgrp->bcgrpn', bm, decay_s, xdt)
    chunk_decay = jnp.exp(a_cs[:, :, -1])

    def step(s, inp):
        st, dec = inp
        return s * dec[..., None, None] + st, s

    s_final, s_in = lax.scan(step, s0, (jnp.moveaxis(states, 1, 0), jnp.moveaxis(chunk_decay, 1, 0)))
    s_in = jnp.moveaxis(s_in, 0, 1)
    y_off = jnp.einsum('bclgn,bcgrpn,bclgr->bclgrp', cm, s_in, jnp.exp(a_cs))
    y = (y_diag + y_off).reshape(b, lp, N_GROUPS, HEADS_PER_GROUP, SSM_HEAD_DIM)[:, :L]
    return y, s_final


def _ssd_layer(x, conv_init, ssm_init, norm_g, w_in, conv_w, conv_b, dt_bias, a_log, d_skip, gnorm, w_out):
    b, L, _ = x.shape
    proj = _rmsnorm(x, norm_g) @ w_in
    z = proj[..., :D_INNER]
    xbc = proj[..., D_INNER:D_INNER + CONV_DIM]
    dt_raw = proj[..., D_INNER + CONV_DIM:]
    xpad = jnp.concatenate([conv_init.astype(xbc.dtype), xbc], axis=1)
    conv = conv_b + sum(xpad[:, k:k + L] * conv_w[k] for k in range(D_CONV))
    new_conv = xpad[:, L:]
    xbc = jax.nn.silu(conv)
    xs = xbc[..., :D_INNER].reshape(b, L, N_GROUPS, HEADS_PER_GROUP, SSM_HEAD_DIM)
    bm = xbc[..., D_INNER:D_INNER + N_GROUPS * D_STATE].reshape(b, L, N_GROUPS, D_STATE)
    cm = xbc[..., D_INNER + N_GROUPS * D_STATE:].reshape(b, L, N_GROUPS, D_STATE)
    dt = jax.nn.softplus((dt_raw + dt_bias).astype(jnp.float32)).reshape(b, L, N_GROUPS, HEADS_PER_GROUP)
    a = -jnp.exp(a_log.astype(jnp.float32)).reshape(N_GROUPS, HEADS_PER_GROUP)
    s0 = ssm_init.astype(jnp.float32).reshape(b, N_GROUPS, HEADS_PER_GROUP, SSM_HEAD_DIM, D_STATE)
    y, s_final = _ssd(xs, dt, a, bm, cm, s0)
    y = y.astype(x.dtype) + d_skip.reshape(N_GROUPS, HEADS_PER_GROUP)[..., None] * xs
    yg = (y.reshape(b, L, D_INNER) * jax.nn.silu(z)).reshape(b, L, N_GROUPS, D_INNER // N_GROUPS)
    yg = _rmsnorm(yg, gnorm.reshape(N_GROUPS, D_INNER // N_GROUPS)).reshape(b, L, D_INNER)
    out = x + yg @ w_out
    return out, new_conv, s_final.reshape(b, SSM_HEADS, SSM_HEAD_DIM, D_STATE)


def _rope(t, pos):
    inv_freq = ROPE_THETA ** (-jnp.arange(0, ROT_DIM, 2, dtype=jnp.float32) / ROT_DIM)
    ang = pos[:, None] * inv_freq[None, :]
    cos = jnp.cos(ang)[None, :, None, None, :].astype(t.dtype)
    sin = jnp.sin(ang)[None, :, None, None, :].astype(t.dtype)
    half = ROT_DIM // 2
    x1, x2 = t[..., :half], t[..., half:ROT_DIM]
    return jnp.concatenate([x1 * cos - x2 * sin, x2 * cos + x1 * sin, t[..., ROT_DIM:]], axis=-1)


def _shared_kv(h, pos, norm_kv, w_kv):
    b, L, _ = h.shape
    kv = _rmsnorm(h, norm_kv) @ w_kv
    k = _rope(kv[..., :K_DIM].reshape(b, L, ATT_HEADS, 2, HEAD_DIM), pos)
    v = kv[..., K_DIM:].reshape(b, L, ATT_HEADS, V_DIM)
    return k, v


def _diff_attention(q, k, v, q_pos, k_pos, lam):
    b, L = q.shape[:2]
    qb = min(Q_BLOCK, L)
    nb = -(-L // qb)
    q = _pad_time(q, nb * qb)
    q_pos = jnp.pad(q_pos, (0, nb * qb - L), mode='edge')
    q_blocks = jnp.moveaxis(q.reshape(b, nb, qb, ATT_HEADS, 2, HEAD_DIM), 1, 0)
    pos_blocks = q_pos.reshape(nb, qb)
    scale = HEAD_DIM ** -0.5

    def block(args):
        qi, pi = args
        s = jnp.einsum('bqhjd,bkhjd->bhjqk', qi, k).astype(jnp.float32) * scale
        s = jnp.where(k_pos[None, :] <= pi[:, None], s, NEG_INF)
        p = jax.nn.softmax(s, axis=-1)
        w = p[:, :, 0] - lam * p[:, :, 1]
        return jnp.einsum('bhqk,bkhe->bqhe', w.astype(v.dtype), v)

    o = lax.map(block, (q_blocks, pos_blocks))
    return jnp.moveaxis(o, 0, 1).reshape(b, nb * qb, ATT_HEADS, V_DIM)[:, :L]


def _diff_layer(x, k, v, q_pos, k_pos, lambda_init, norm_g, w_in, lq1, lk1, lq2, lk2, subln, w_out):
    b, L, _ = x.shape
    proj = _rmsnorm(x, norm_g) @ w_in
    q = _rope(proj[..., :K_DIM].reshape(b, L, ATT_HEADS, 2, HEAD_DIM), q_pos)
    gate = proj[..., K_DIM:]
    lam = (jnp.exp(jnp.sum(lq1.astype(jnp.float32) * lk1.astype(jnp.float32)))
           - jnp.exp(jnp.sum(lq2.astype(jnp.float32) * lk2.astype(jnp.float32))) + lambda_init)
    o = _diff_attention(q, k, v, q_pos, k_pos, lam)
    o = _rmsnorm(o, subln) * (1.0 - lambda_init)
    o = o.reshape(b, L, VAL_WIDTH) * jax.nn.silu(gate)
    return x + o @ w_out


def setup_inputs(seed: int = 0) -> dict:
    key = jax.random.key(seed)
    ks = iter(jax.random.split(key, 40))

    def nrm(shape, scale):
        return jax.random.normal(next(ks), shape, jnp.float32) * scale

    def gain(shape):
        return 1.0 + nrm(shape, 0.02)

    n_pages = PAST_LEN // PAGE_SIZE
    n_used = DEC_BATCH * n_pages
    n_pool = n_used + n_used // 4
    page_table = jax.random.permutation(next(ks), n_pool)[:n_used].astype(jnp.int32).reshape(DEC_BATCH, n_pages)
    dt0 = jnp.exp(jax.random.uniform(next(ks), (N_A_LAYERS, SSM_HEADS), jnp.float32, math.log(1e-3), math.log(1e-1)))
    dt_bias = dt0 + jnp.log(-jnp.expm1(-dt0))
    a_log = jnp.log(jax.random.uniform(next(ks), (N_A_LAYERS, SSM_HEADS), jnp.float32, 1.0, 16.0))
    return {
        'x_prompt': nrm((BATCH, SEQ, D_MODEL), 1.0),
        'x_sample': nrm((DEC_BATCH, DEC_SEQ, D_MODEL), 1.0),
        'cache_k': nrm((n_pool, PAGE_SIZE, ATT_HEADS, 2, HEAD_DIM), 1.0),
        'cache_v': nrm((n_pool, PAGE_SIZE, ATT_HEADS, V_DIM), 1.0),
        'page_table': page_table,
        'state_conv': nrm((N_A_LAYERS, DEC_BATCH, D_CONV - 1, CONV_DIM), 1.0),
        'state_ssm': nrm((N_A_LAYERS, DEC_BATCH, SSM_HEADS, SSM_HEAD_DIM, D_STATE), 0.1),
        'norm_a': gain((N_A_LAYERS, D_MODEL)),
        'w_in_a': nrm((N_A_LAYERS, D_MODEL, IN_A_DIM), D_MODEL ** -0.5),
        'conv_w': nrm((N_A_LAYERS, D_CONV, CONV_DIM), D_CONV ** -0.5),
        'conv_b': nrm((N_A_LAYERS, CONV_DIM), 0.02),
        'dt_bias': dt_bias,
        'a_log': a_log,
        'd_skip': gain((N_A_LAYERS, SSM_HEADS)),
        'gnorm_a': gain((N_A_LAYERS, D_INNER)),
        'w_out_a': nrm((N_A_LAYERS, D_INNER, D_MODEL), D_INNER ** -0.5),
        'norm_kv': gain((D_MODEL,)),
        'w_kv': nrm((D_MODEL, KV_DIM), D_MODEL ** -0.5),
        'norm_b': gain((N_B_LAYERS, D_MODEL)),
        'w_in_b': nrm((N_B_LAYERS, D_MODEL, IN_B_DIM), D_MODEL ** -0.5),
        'lambda_q1': nrm((N_B_LAYERS, HEAD_DIM), 0.1),
        'lambda_k1': nrm((N_B_LAYERS, HEAD_DIM), 0.1),
        'lambda_q2': nrm((N_B_LAYERS, HEAD_DIM), 0.1),
        'lambda_k2': nrm((N_B_LAYERS, HEAD_DIM), 0.1),
        'subln_b': gain((N_B_LAYERS, V_DIM)),
        'w_out_b': nrm((N_B_LAYERS, VAL_WIDTH, D_MODEL), VAL_WIDTH ** -0.5),
        'norm_f': gain((D_MODEL,)),
    }


def reference(x_prompt, x_sample, cache_k, cache_v, page_table, state_conv, state_ssm,
              norm_a, w_in_a, conv_w, conv_b, dt_bias, a_log, d_skip, gnorm_a, w_out_a,
              norm_kv, w_kv, norm_b, w_in_b, lambda_q1, lambda_k1, lambda_q2, lambda_k2,
              subln_b, w_out_b, norm_f):
    n_pages = PAST_LEN // PAGE_SIZE
    pos_p = jnp.arange(SEQ, dtype=jnp.float32)
    pos_s = PAST_LEN + jnp.arange(DEC_SEQ, dtype=jnp.float32)
    kpos_s = jnp.arange(PAST_LEN + DEC_SEQ, dtype=jnp.float32)
    past_k = cache_k[page_table].reshape(DEC_BATCH, n_pages * PAGE_SIZE, ATT_HEADS, 2, HEAD_DIM)
    past_v = cache_v[page_table].reshape(DEC_BATCH, n_pages * PAGE_SIZE, ATT_HEADS, V_DIM)
    conv0_p = jnp.zeros((BATCH, D_CONV - 1, CONV_DIM), x_prompt.dtype)
    ssm0_p = jnp.zeros((BATCH, SSM_HEADS, SSM_HEAD_DIM, D_STATE), jnp.float32)

    hp, hs = x_prompt, x_sample
    conv_p, ssm_p, conv_s, ssm_s = [], [], [], []
    k_p = v_p = k_s = v_s = k_all = v_all = None
    for layer in range(DEPTH):
        if layer < N_A_LAYERS:
            i = layer
            wa = (norm_a[i], w_in_a[i], conv_w[i], conv_b[i], dt_bias[i], a_log[i], d_skip[i], gnorm_a[i], w_out_a[i])
            hp, c, s = _ssd_layer(hp, conv0_p, ssm0_p, *wa)
            conv_p.append(c)
            ssm_p.append(s)
            hs, c, s = _ssd_layer(hs, state_conv[i], state_ssm[i], *wa)
            conv_s.append(c)
            ssm_s.append(s)
        else:
            j = layer - N_A_LAYERS
            if j == 0:
                k_p, v_p = _shared_kv(hp, pos_p, norm_kv, w_kv)
                k_s, v_s = _shared_kv(hs, pos_s, norm_kv, w_kv)
                k_all = jnp.concatenate([past_k.astype(k_s.dtype), k_s], axis=1)
                v_all = jnp.concatenate([past_v.astype(v_s.dtype), v_s], axis=1)
            lambda_init = 0.8 - 0.6 * math.exp(-0.3 * layer)
            wb = (norm_b[j], w_in_b[j], lambda_q1[j], lambda_k1[j], lambda_q2[j], lambda_k2[j], subln_b[j], w_out_b[j])
            hp = _diff_layer(hp, k_p, v_p, pos_p, pos_p, lambda_init, *wb)
            hs = _diff_layer(hs, k_all, v_all, pos_s, kpos_s, lambda_init, *wb)

    y_prompt = _rmsnorm(hp, norm_f)
    y_sample = _rmsnorm(hs, norm_f)
    return (y_prompt, y_sample, k_p, v_p, jnp.stack(conv_p), jnp.stack(ssm_p),
            k_s, v_s, jnp.stack(conv_s), jnp.stack(ssm_s))
```

```python
import numpy as np
from contextlib import ExitStack
import concourse.bass as bass
import concourse.mybir as mybir
from concourse.bass_utils import run_bass_kernel_spmd

F32 = mybir.dt.float32
BF16 = mybir.dt.bfloat16
AF = mybir.ActivationFunctionType
ALU = mybir.AluOpType
AX = mybir.AxisListType

D = 1024
DI = 2048
CONV = 3072
INA = 5152
EPS = 1e-6
PAST = 8192


class Buf:
    __slots__ = ("w", "r", "name")

    def __init__(self, name=""):
        self.w = None
        self.r = []
        self.name = name


class KB:
    def __init__(self, nc, es, ndma=24):
        self.nc = nc
        self.eng = {"pe": nc.tensor, "act": nc.scalar, "dve": nc.vector, "pool": nc.gpsimd, "sp": nc.sync}
        self.sems = []
        self.csem = {}
        self.cnt = {}
        for e in ("pe", "act", "dve", "pool"):
            s = es.enter_context(nc.semaphore("c_" + e))
            self.csem[e] = len(self.sems)
            self.sems.append(s)
            self.cnt[e] = 0
        self.dsem = []
        self.dtot = []
        for i in range(ndma):
            s = es.enter_context(nc.semaphore("d%d" % i))
            self.dsem.append(len(self.sems))
            self.sems.append(s)
            self.dtot.append(0)
        self.dnext = 0
        self.seen = {e: {} for e in self.eng}

    def _wait(self, e, ev):
        si, v = ev
        if self.seen[e].get(si, 0) >= v:
            return
        self.seen[e][si] = v
        self.eng[e].wait_ge(self.sems[si], v)

    def _deps(self, e, reads, writes):
        for b in reads:
            if b.w is not None:
                self._wait(e, b.w)
        for b in writes:
            if b.w is not None:
                self._wait(e, b.w)
            for ev in b.r:
                self._wait(e, ev)

    def op(self, e, fn, reads=(), writes=()):
        self._deps(e, reads, writes)
        ins = fn()
        self.cnt[e] += 1
        ev = (self.csem[e], self.cnt[e])
        ins.then_inc(self.sems[ev[0]], 1)
        for b in reads:
            b.r.append(ev)
        for b in writes:
            b.w = ev
            b.r = []
        return ins

    def dma(self, q, out, in_, reads=(), writes=(), **kw):
        k = self.dnext
        self.dnext = (self.dnext + 1) % len(self.dsem)
        si = self.dsem[k]
        if self.dtot[k] > 0:
            self._wait(q, (si, self.dtot[k]))
        self._deps(q, reads, writes)
        ins = self.eng[q].dma_start(out=out, in_=in_, **kw)
        self.dtot[k] += 16
        ev = (si, self.dtot[k])
        ins.then_inc(self.sems[si], 16)
        for b in reads:
            b.r.append(ev)
        for b in writes:
            b.w = ev
            b.r = []
        return ev

    def barrier(self, engines=("pe", "act", "dve", "pool", "sp")):
        for e in engines:
            for o in ("pe", "act", "dve", "pool"):
                if self.cnt[o] > 0:
                    self._wait(e, (self.csem[o], self.cnt[o]))
            for k, si in enumerate(self.dsem):
                if self.dtot[k] > 0:
                    self._wait(e, (si, self.dtot[k]))


def build(T, NS, NP=64, NPOOL=2560):
    NCH = T // 128
    NT = NP + 1
    nc = bass.Bass("TRN2", target_bir_lowering=False)

    def din(name, shape, dt=F32):
        return nc.dram_tensor(name, list(shape), dt, kind="ExternalInput").ap()

    def dout(name, shape, dt=F32):
        return nc.dram_tensor(name, list(shape), dt, kind="ExternalOutput").ap()

    x_d = din("x", [T, D])
    xs_d = din("xsmp", [NS, D])
    sconv_d = din("sconv", [NS, 128, 24, 3])
    sssm_d = din("sssm", [NS, 128, DI])
    wina_d = din("w_in_a", [D, INA])
    wouta_d = din("w_out_a", [DI, D])
    wkv_d = din("w_kv", [D, 2 * D])
    convw_d = din("convw", [128, 24, 4])
    convb_d = din("convb", [128, 24])
    na_d = din("na", [128, 8])
    gn_d = din("gn", [128, 16])
    nkv_d = din("nkv", [128, 8])
    dtb_d = din("dtb", [32])
    alog_d = din("alog", [32])
    dsk_d = din("dsk", [32])
    cst_d = din("cst", [128, 384])
    cosp_d = din("cosp", [T, 8])
    sinp_d = din("sinp", [T, 8])
    coss_d = din("coss", [128, 8])
    sins_d = din("sins", [128, 8])
    winb_d = din("w_in_b", [D, 2 * D])
    woutb_d = din("w_out_b", [D, D])
    nb_d = din("nb", [128, 8])
    lam_d = din("lamv", [4, 64])
    subln_d = din("subln", [128])
    nf_d = din("nf", [D])
    poolk_d = din("poolk", [NPOOL * 128, D])
    poolv_d = din("poolv", [NPOOL * 128, D])
    pt_d = nc.dram_tensor("ptab", [NS * NP], mybir.dt.int32, kind="ExternalInput").ap()
    cst2_d = din("cst2", [128, 652])

    kp_o = dout("k_p", [T, D])
    vp_o = dout("v_p", [T, D])
    convp_o = dout("conv_p", [128, 24, 3])
    ssmp_o = dout("ssm_p", [128, DI])
    ks_o = dout("k_s", [NS, D])
    vs_o = dout("v_s", [NS, D])
    convs_o = dout("conv_s", [NS, 128, 24, 3])
    ssms_o = dout("ssm_s", [NS, 128, DI])
    yp_o = dout("y_p", [T, D])
    ys_o = dout("y_s", [NS, D])

    NSEQ = 1 + NS
    h1_d = nc.dram_tensor("h1d", [NCH + NS, 128, D], F32, kind="Internal").ap()

    with ExitStack() as es:
        E = es.enter_context
        kb = KB(nc, es)

        def sb(name, shape, dt=F32):
            return E(nc.sbuf_tensor("s_" + name, list(shape), dt))

        cst = sb("cst", [128, 384])
        b_cst = Buf()
        kb.dma("sp", cst[:], cst_d[:, :], writes=[b_cst])
        tri = cst[:, 0:128]
        astr = cst[:, 128:256]
        identf = cst[:, 256:384]
        identb = sb("identb", [128, 128], BF16)
        onesf = sb("onesf", [128, 128])
        b_misc = Buf()
        kb.op("dve", lambda: nc.vector.tensor_copy(out=identb[:], in_=identf), [b_cst], [b_misc])
        kb.op("dve", lambda: nc.vector.memset(onesf[:], 1.0), [], [b_misc])
        epsc = sb("epsc", [128, 1])
        kb.op("dve", lambda: nc.vector.memset(epsc[:], EPS), [], [b_misc])

        convw = sb("convw", [128, 24, 4])
        convb = sb("convb", [128, 24])
        na = sb("na", [128, 8])
        gn = sb("gn", [128, 16])
        nkv = sb("nkv", [128, 8])
        prm = sb("prm", [128, 96])
        b_prm = Buf()
        kb.dma("sp", convw[:], convw_d[:, :, :], writes=[b_prm])
        kb.dma("sp", convb[:], convb_d[:, :], writes=[b_prm])
        kb.dma("sp", na[:], na_d[:, :], writes=[b_prm])
        kb.dma("sp", gn[:], gn_d[:, :], writes=[b_prm])
        kb.dma("sp", nkv[:], nkv_d[:, :], writes=[b_prm])
        kb.dma("sp", prm[:, 0:32], dtb_d.partition_broadcast(128), writes=[b_prm])
        kb.dma("sp", prm[:, 32:64], alog_d.partition_broadcast(128), writes=[b_prm])
        kb.dma("sp", prm[:, 64:96], dsk_d.partition_broadcast(128), writes=[b_prm])
        dtb = prm[:, 0:32]
        dsk = prm[:, 64:96]
        aneg = sb("aneg", [128, 32])
        kb.op("act", lambda: nc.scalar.activation(out=aneg[:], in_=prm[:, 32:64], func=AF.Exp), [b_prm], [b_misc])
        kb.op("dve", lambda: nc.vector.tensor_scalar(out=aneg[:], in0=aneg[:], scalar1=-1.0, scalar2=None, op0=ALU.mult),
              [b_misc], [b_misc])

        NB = 8
        ps = [E(nc.psum_tensor("ps%d" % i, [128, 512], F32)) for i in range(NB)]
        b_ps = [Buf("ps%d" % i) for i in range(NB)]
        psi = [0]

        def bank():
            i = psi[0]
            psi[0] = (i + 1) % NB
            return ps[i], b_ps[i]

        with ExitStack() as esA:
            EA = esA.enter_context

            def sa(name, shape, dt=F32):
                return EA(nc.sbuf_tensor("a_" + name, list(shape), dt))

            wina = sa("wina", [128, 8, INA], BF16)
            wouta = sa("wouta", [128, 16, D], BF16)
            b_wina = Buf()
            b_wouta = Buf()
            for k in range(8):
                kb.dma("pool", wina[:, k, :], wina_d[k * 128:(k + 1) * 128, :], writes=[b_wina])
            for k in range(16):
                kb.dma("pool", wouta[:, k, :], wouta_d[k * 128:(k + 1) * 128, :], writes=[b_wouta])
            for k in range(8):
                kb.op("dve", lambda k=k: nc.vector.tensor_scalar(out=wina[:, k, :], in0=wina[:, k, :], scalar1=na[:, k:k + 1],
                                                               scalar2=None, op0=ALU.mult), [b_wina, b_prm], [b_wina])
            for k in range(16):
                kb.op("dve", lambda k=k: nc.vector.tensor_scalar(out=wouta[:, k, :], in0=wouta[:, k, :], scalar1=gn[:, k:k + 1],
                                                               scalar2=None, op0=ALU.mult), [b_wouta, b_prm], [b_wouta])

            xt = sa("xt", [128, D])
            xsb = sa("xsb", [128, D], BF16)
            xnT = sa("xnT", [128, 8, 128], BF16)
            zs = sa("zs", [128, DI], BF16)
            xpad = sa("xpad", [128, 24, 131])
            halo = sa("halo", [128, 24, 3])
            acc = sa("acc", [128, 12, 128])
            xc = sa("xc", [128, 24, 128], BF16)
            xstok = sa("xstok", [128, 32, 64], BF16)
            btok = sa("btok", [128, 4, 128], BF16)
            xdt = sa("xdt", [128, 32, 64], BF16)
            xdtd = sa("xdtd", [128, 8, 64], BF16)
            sm = sa("sm", [128, 16])
            dt = sa("dt", [128, 32])
            dta = sa("dta", [128, 32])
            acs = sa("acs", [128, 32])
            eacs = sa("eacs", [128, 32])
            decs = sa("decs", [128, 32])
            cd = sa("cd", [128, 32])
            vmask = sa("vmask", [128, 2])
            Zg2 = [sa("Zg%d" % i, [128, 8, 128]) for i in range(2)]
            Eg2 = [sa("Eg%d" % i, [128, 8, 128], BF16) for i in range(2)]
            Gg2 = [sa("Gg%d" % i, [128, 8, 128], BF16) for i in range(2)]
            cbm = sa("cbm", [128, 4, 128])
            yg = sa("yg", [128, 512])
            tg = sa("tg", [128, 512])
            ygb = sa("ygb", [128, DI], BF16)
            ygT = sa("ygT", [128, 16, 128], BF16)
            S = sa("S", [128, DI])
            Sb = sa("Sb", [128, DI], BF16)
            B = {n: Buf(n) for n in ["xt", "junk", "xsb", "xnT", "zs", "xpad", "halo", "acc", "xc", "xstok", "btok", "xdt",
                                     "xdtd", "sm", "dt", "dta", "acs", "eacs", "decs", "cd", "vmask", "Zg0", "Eg0", "Gg0", "Zg1", "Eg1", "Gg1", "cbm",
                                     "yg", "tg", "ygb", "ygT", "S", "Sb"]}
            kb.op("dve", lambda: nc.vector.memset(vmask[:], 0.0), [], [B["vmask"]])
            kb.op("dve", lambda: nc.vector.memset(vmask[:, 0:1], 1.0), [], [B["vmask"]])
            kb.op("dve", lambda: nc.vector.memset(vmask[0:1, 1:2], 1.0), [], [B["vmask"]])

            def rstd_from(ssq_ap, out_ap, n, bufs):
                kb.op("act", lambda: nc.scalar.activation(out=out_ap, in_=ssq_ap, func=AF.Ln, scale=1.0 / n, bias=epsc[:, 0:1]), bufs + [b_misc], bufs)
                kb.op("act", lambda: nc.scalar.activation(out=out_ap, in_=out_ap, func=AF.Exp, scale=-0.5), bufs, bufs)

            def chunk(seq, c, first, last):
                is_s = seq > 0
                vm = vmask[:, 1:2] if is_s else vmask[:, 0:1]
                if is_s:
                    kb.op("dve", lambda: nc.vector.memset(xt[:], 0.0), [], [B["xt"]])
                    kb.dma("sp", xt[0:1, :], xs_d[seq - 1:seq, :], writes=[B["xt"]])
                else:
                    kb.dma("sp", xt[:], x_d[c * 128:(c + 1) * 128, :], writes=[B["xt"]])
                if first:
                    if is_s:
                        kb.dma("sp", halo[:], sconv_d[seq - 1], writes=[B["halo"]])
                        kb.dma("sp", S[:], sssm_d[seq - 1], writes=[B["S"]])
                    else:
                        kb.op("pool", lambda: nc.gpsimd.memset(halo[:], 0.0), [], [B["halo"]])
                        kb.op("pool", lambda: nc.gpsimd.memset(S[:], 0.0), [], [B["S"]])
                    kb.op("act", lambda: nc.scalar.copy(out=Sb[:], in_=S[:]), [B["S"]], [B["Sb"]])
                kb.op("act", lambda: nc.scalar.activation(out=acc[:].rearrange("p c t -> p (c t)")[:, 0:D], in_=xt[:], func=AF.Square, accum_out=sm[:, 0:1]),
                      [B["xt"]], [B["acc"], B["sm"]])
                rstd_from(sm[:, 0:1], sm[:, 1:2], D, [B["sm"]])
                kb.op("dve", lambda: nc.vector.tensor_scalar(out=xsb[:], in0=xt[:], scalar1=sm[:, 1:2], scalar2=None, op0=ALU.mult),
                      [B["xt"], B["sm"]], [B["xsb"]])
                p, bp = bank()
                pb = p[:].bitcast(BF16)
                for k in range(8):
                    kb.op("pe", lambda k=k: nc.tensor.transpose(out=pb[:, k * 128:(k + 1) * 128], in_=xsb[:, k * 128:(k + 1) * 128],
                                                              identity=identb[:]), [B["xsb"], b_misc], [bp])
                kb.op("act", lambda: nc.scalar.copy(out=xnT[:].rearrange("p k t -> p (k t)"), in_=pb), [bp], [B["xnT"]])
                for blk in range(4):
                    p, bp = bank()
                    for k in range(8):
                        kb.op("pe", lambda k=k, blk=blk, p=p: nc.tensor.matmul(p[:], lhsT=xnT[:, k, :], rhs=wina[:, k, blk * 512:(blk + 1) * 512],
                                                                           start=(k == 0), stop=(k == 7)), [B["xnT"], b_wina], [bp])
                    kb.op("act", lambda blk=blk, p=p: nc.scalar.activation(out=zs[:, blk * 512:(blk + 1) * 512], in_=p[:], func=AF.Silu),
                          [bp], [B["zs"]])
                kb.op("pool", lambda: nc.gpsimd.tensor_copy(out=xpad[:, :, 0:3], in_=halo[:]), [B["halo"]], [B["xpad"]])
                for q4 in range(6):
                    p, bp = bank()
                    for j in range(4):
                        ct = q4 * 4 + j
                        for k in range(8):
                            kb.op("pe", lambda k=k, ct=ct, j=j, p=p: nc.tensor.matmul(
                                p[:, j * 128:(j + 1) * 128], lhsT=wina[:, k, DI + ct * 128:DI + (ct + 1) * 128], rhs=xnT[:, k, :],
                                start=(k == 0), stop=(k == 7)), [B["xnT"], b_wina], [bp])
                    kb.op("act", lambda q4=q4, p=p: nc.scalar.copy(out=xpad[:, q4 * 4:(q4 + 1) * 4, 3:131],
                                                                  in_=p[:].rearrange("p (j t) -> p j t", j=4)), [bp], [B["xpad"]])
                if is_s:
                    kb.op("pool", lambda: nc.gpsimd.tensor_copy(out=halo[:], in_=xpad[:, :, 1:4]), [B["xpad"]], [B["halo"]])
                else:
                    kb.op("pool", lambda: nc.gpsimd.tensor_copy(out=halo[:], in_=xpad[:, :, 128:131]), [B["xpad"]], [B["halo"]])
                if last:
                    kb.dma("sp", convs_o[seq - 1] if is_s else convp_o[:, :, :], halo[:], reads=[B["halo"]])
                p, bp = bank()
                for k in range(8):
                    kb.op("pe", lambda k=k, p=p: nc.tensor.matmul(p[:, 0:32], lhsT=xnT[:, k, :], rhs=wina[:, k, DI + CONV:INA],
                                                               start=(k == 0), stop=(k == 7)), [B["xnT"], b_wina], [bp])
                kb.op("dve", lambda p=p: nc.vector.tensor_tensor(out=dt[:], in0=p[:, 0:32], in1=dtb, op=ALU.add), [bp, b_prm], [B["dt"]])
                kb.op("act", lambda: nc.scalar.activation(out=dt[:], in_=dt[:], func=AF.Exp), [B["dt"]], [B["dt"]])
                kb.op("act", lambda: nc.scalar.activation(out=dt[:], in_=dt[:], func=AF.Ln, bias=1.0), [B["dt"]], [B["dt"]])
                kb.op("dve", lambda: nc.vector.tensor_scalar(out=dt[:], in0=dt[:], scalar1=vm, scalar2=None, op0=ALU.mult),
                      [B["dt"], B["vmask"]], [B["dt"]])
                kb.op("dve", lambda: nc.vector.tensor_tensor(out=dta[:], in0=dt[:], in1=aneg[:], op=ALU.mult), [B["dt"], b_misc], [B["dta"]])
                for hf in range(2):
                    for c12 in range(12):
                        ct = hf * 12 + c12
                        kb.op("act", lambda ct=ct, c12=c12: nc.scalar.activation(out=acc[:, c12, :], in_=xpad[:, ct, 3:131], func=AF.Identity,
                                                                               scale=convw[:, ct, 3:4], bias=convb[:, ct:ct + 1]),
                              [B["xpad"], b_prm], [B["acc"]])
                    for c12 in range(12):
                        ct = hf * 12 + c12
                        for kk in range(3):
                            kb.op("dve", lambda ct=ct, c12=c12, kk=kk: nc.vector.scalar_tensor_tensor(
                                out=acc[:, c12, :], in0=xpad[:, ct, kk:kk + 128], scalar=convw[:, ct, kk:kk + 1], in1=acc[:, c12, :],
                                op0=ALU.mult, op1=ALU.add), [B["xpad"], B["acc"], b_prm], [B["acc"]])
                    kb.op("act", lambda hf=hf: nc.scalar.activation(out=xc[:, hf * 12:(hf + 1) * 12, :].rearrange("p c t -> p (c t)"),
                                                                   in_=acc[:].rearrange("p c t -> p (c t)"), func=AF.Silu), [B["acc"]], [B["xc"]])
                for half in range(2):
                    p, bp = bank()
                    pb = p[:].bitcast(BF16)
                    for i in range(8):
                        ct = half * 8 + i
                        kb.op("pe", lambda i=i, ct=ct, pb=pb: nc.tensor.transpose(out=pb[:, i * 128:(i + 1) * 128], in_=xc[:, ct, :],
                                                                                identity=identb[:]), [B["xc"], b_misc], [bp])
                    kb.op("act", lambda half=half, pb=pb: nc.scalar.copy(
                        out=xstok[:, half * 16:(half + 1) * 16, :].rearrange("p r d -> p (r d)"), in_=pb), [bp], [B["xstok"]])
                p, bp = bank()
                pb = p[:].bitcast(BF16)
                for g in range(4):
                    kb.op("pe", lambda g=g, pb=pb: nc.tensor.transpose(out=pb[:, g * 128:(g + 1) * 128], in_=xc[:, 16 + g, :],
                                                                     identity=identb[:]), [B["xc"], b_misc], [bp])
                kb.op("act", lambda pb=pb: nc.scalar.copy(out=btok[:].rearrange("p g n -> p (g n)"), in_=pb[:, 0:512]), [bp], [B["btok"]])
                kb.op("dve", lambda: nc.vector.tensor_tensor(out=xdt[:], in0=xstok[:], in1=dt[:].unsqueeze(2).to_broadcast([128, 32, 64]),
                                                             op=ALU.mult), [B["xstok"], B["dt"]], [B["xdt"]])
                p, bp = bank()
                kb.op("pe", lambda p=p: nc.tensor.matmul(p[:, 0:32], lhsT=tri, rhs=dta[:], start=True, stop=True), [b_cst, B["dta"]], [bp])
                kb.op("pe", lambda p=p: nc.tensor.matmul(p[:, 32:64], lhsT=onesf[:], rhs=dta[:], start=True, stop=True),
                      [b_misc, B["dta"]], [bp])
                kb.op("act", lambda p=p: nc.scalar.copy(out=acs[:], in_=p[:, 0:32]), [bp], [B["acs"]])
                kb.op("act", lambda p=p: nc.scalar.activation(out=eacs[:], in_=p[:, 0:32], func=AF.Exp), [bp], [B["eacs"]])
                kb.op("act", lambda p=p: nc.scalar.activation(out=cd[:], in_=p[:, 32:64], func=AF.Exp), [bp], [B["cd"]])
                kb.op("dve", lambda p=p: nc.vector.tensor_tensor(out=decs[:], in0=p[:, 32:64], in1=acs[:], op=ALU.subtract),
                      [bp, B["acs"]], [B["decs"]])
                kb.op("act", lambda: nc.scalar.activation(out=decs[:], in_=decs[:], func=AF.Exp), [B["decs"]], [B["decs"]])
                p, bp = bank()
                for g in range(4):
                    kb.op("pe", lambda g=g, p=p: nc.tensor.matmul(p[:, g * 128:(g + 1) * 128], lhsT=xc[:, 16 + g, :], rhs=xc[:, 20 + g, :],
                                                               start=True, stop=True), [B["xc"]], [bp])
                kb.op("dve", lambda p=p: nc.vector.tensor_tensor(out=cbm[:], in0=p[:].rearrange("p (g l) -> p g l", g=4),
                                                                in1=tri.unsqueeze(1).to_broadcast([128, 4, 128]), op=ALU.mult),
                      [bp, b_cst], [B["cbm"]])
                for g in range(4):
                    Zg, Eg, Gg = Zg2[g % 2], Eg2[g % 2], Gg2[g % 2]
                    zk, ek, gk = "Zg%d" % (g % 2), "Eg%d" % (g % 2), "Gg%d" % (g % 2)
                    kb.op("pool", lambda g=g, Zg=Zg: nc.gpsimd.tensor_tensor(
                        out=Zg[:], in0=tri.unsqueeze(1).to_broadcast([128, 8, 128]),
                        in1=dta[:, g * 8:(g + 1) * 8].unsqueeze(2).to_broadcast([128, 8, 128]), op=ALU.mult),
                        [b_cst, B["dta"]], [B[zk]])
                    for hh in range(2):
                        p, bp = bank()
                        kb.op("pe", lambda hh=hh, p=p, Zg=Zg: nc.tensor.matmul(p[:], lhsT=astr, rhs=Zg[:, hh * 4:(hh + 1) * 4, :].rearrange("p r l -> p (r l)"),
                                                                     start=True, stop=True), [b_cst, B[zk]], [bp])
                        kb.op("act", lambda hh=hh, p=p, Eg=Eg: nc.scalar.activation(out=Eg[:, hh * 4:(hh + 1) * 4, :].rearrange("p r l -> p (r l)"),
                                                                            in_=p[:], func=AF.Exp), [bp], [B[ek]])
                    kb.op("dve", lambda g=g, Gg=Gg, Eg=Eg: nc.vector.tensor_tensor(out=Gg[:], in0=Eg[:], in1=cbm[:, g, :].unsqueeze(1).to_broadcast([128, 8, 128]),
                                                                     op=ALU.mult), [B[ek], B["cbm"]], [B[gk]])
                    pA, bA = bank()
                    for r in range(8):
                        h = g * 8 + r
                        kb.op("pe", lambda r=r, h=h, pA=pA, Gg=Gg: nc.tensor.matmul(pA[:, r * 64:(r + 1) * 64], lhsT=Gg[:, r, :], rhs=xdt[:, h, :],
                                                                          start=True, stop=True), [B[gk], B["xdt"]], [bA])
                    pB, bB = bank()
                    kb.op("pe", lambda g=g, pB=pB: nc.tensor.matmul(pB[:], lhsT=xc[:, 20 + g, :], rhs=Sb[:, g * 512:(g + 1) * 512],
                                                                 start=True, stop=True), [B["xc"], B["Sb"]], [bB])
                    kb.op("dve", lambda g=g, pB=pB: nc.vector.tensor_tensor(
                        out=tg[:].rearrange("p (r d) -> p r d", r=8), in0=pB[:].rearrange("p (r d) -> p r d", r=8),
                        in1=eacs[:, g * 8:(g + 1) * 8].unsqueeze(2).to_broadcast([128, 8, 64]), op=ALU.mult), [bB, B["eacs"]], [B["tg"]])
                    kb.op("dve", lambda pA=pA: nc.vector.tensor_tensor(out=yg[:], in0=pA[:], in1=tg[:], op=ALU.add), [bA, B["tg"]], [B["yg"]])
                    kb.op("pool", lambda g=g: nc.gpsimd.tensor_tensor(
                        out=tg[:].rearrange("p (r d) -> p r d", r=8), in0=xstok[:, g * 8:(g + 1) * 8, :],
                        in1=dsk[:, g * 8:(g + 1) * 8].unsqueeze(2).to_broadcast([128, 8, 64]), op=ALU.mult),
                        [B["xstok"], b_prm, B["yg"]], [B["tg"]])
                    kb.op("dve", lambda: nc.vector.tensor_tensor(out=yg[:], in0=yg[:], in1=tg[:], op=ALU.add), [B["tg"], B["yg"]], [B["yg"]])
                    kb.op("dve", lambda g=g: nc.vector.tensor_tensor(out=yg[:], in0=yg[:], in1=zs[:, g * 512:(g + 1) * 512], op=ALU.mult),
                          [B["zs"], B["yg"]], [B["yg"]])
                    kb.op("act", lambda g=g: nc.scalar.activation(out=tg[:], in_=yg[:], func=AF.Square, accum_out=sm[:, 4 + g:5 + g]),
                          [B["yg"]], [B["tg"], B["sm"]])
                    rstd_from(sm[:, 4 + g:5 + g], sm[:, 8 + g:9 + g], 512, [B["sm"]])
                    kb.op("dve", lambda g=g: nc.vector.tensor_scalar(out=ygb[:, g * 512:(g + 1) * 512], in0=yg[:], scalar1=sm[:, 8 + g:9 + g],
                                                                   scalar2=None, op0=ALU.mult), [B["yg"], B["sm"]], [B["ygb"]])
                for g in range(4):
                    kb.op("dve", lambda g=g: nc.vector.tensor_tensor(out=xdtd[:], in0=xdt[:, g * 8:(g + 1) * 8, :],
                                                                     in1=decs[:, g * 8:(g + 1) * 8].unsqueeze(2).to_broadcast([128, 8, 64]),
                                                                     op=ALU.mult), [B["xdt"], B["decs"]], [B["xdtd"]])
                    p, bp = bank()
                    kb.op("pe", lambda g=g, p=p: nc.tensor.matmul(p[:], lhsT=btok[:, g, :], rhs=xdtd[:].rearrange("p r d -> p (r d)"),
                                                               start=True, stop=True), [B["btok"], B["xdtd"]], [bp])
                    kb.op("dve", lambda g=g: nc.vector.tensor_tensor(
                        out=S[:, g * 512:(g + 1) * 512].rearrange("p (r d) -> p r d", r=8),
                        in0=S[:, g * 512:(g + 1) * 512].rearrange("p (r d) -> p r d", r=8),
                        in1=cd[:, g * 8:(g + 1) * 8].unsqueeze(2).to_broadcast([128, 8, 64]), op=ALU.mult), [B["S"], B["cd"], B["Sb"]], [B["S"]])
                    kb.op("dve", lambda g=g, p=p: nc.vector.tensor_tensor(out=S[:, g * 512:(g + 1) * 512], in0=S[:, g * 512:(g + 1) * 512],
                                                                        in1=p[:], op=ALU.add), [bp, B["S"]], [B["S"]])
                kb.op("act", lambda: nc.scalar.copy(out=Sb[:], in_=S[:]), [B["S"]], [B["Sb"]])
                if last:
                    kb.dma("sp", ssms_o[seq - 1] if is_s else ssmp_o[:, :], S[:], reads=[B["S"]])
                for half in range(2):
                    p, bp = bank()
                    pb = p[:].bitcast(BF16)
                    for i in range(8):
                        kk = half * 8 + i
                        kb.op("pe", lambda i=i, kk=kk, pb=pb: nc.tensor.transpose(out=pb[:, i * 128:(i + 1) * 128], in_=ygb[:, kk * 128:(kk + 1) * 128],
                                                                                identity=identb[:]), [B["ygb"], b_misc], [bp])
                    kb.op("act", lambda half=half, pb=pb: nc.scalar.copy(out=ygT[:, half * 8:(half + 1) * 8, :].rearrange("p k t -> p (k t)"), in_=pb),
                          [bp], [B["ygT"]])
                for blk in range(2):
                    p, bp = bank()
                    for k in range(16):
                        kb.op("pe", lambda k=k, blk=blk, p=p: nc.tensor.matmul(p[:], lhsT=ygT[:, k, :], rhs=wouta[:, k, blk * 512:(blk + 1) * 512],
                                                                           start=(k == 0), stop=(k == 15)), [B["ygT"], b_wouta], [bp])
                    kb.op("dve", lambda blk=blk, p=p: nc.vector.tensor_tensor(out=xt[:, blk * 512:(blk + 1) * 512], in0=xt[:, blk * 512:(blk + 1) * 512],
                                                                            in1=p[:], op=ALU.add), [bp, B["xt"]], [B["xt"]])
                idx = c if not is_s else NCH + seq - 1
                kb.dma("sp", h1_d[idx], xt[:], reads=[B["xt"]])

            for c in range(NCH):
                chunk(0, c, c == 0, c == NCH - 1)
            for s in range(NS):
                chunk(1 + s, 0, True, True)
            kb.barrier()
        import os
        if os.environ.get("DBG_STOP") == "A":
            return nc

        esR = ExitStack()
        KT_all = esR.enter_context(nc.sbuf_tensor("r_KT", [128, NCH, 8, 128], BF16))
        V_all = esR.enter_context(nc.sbuf_tensor("r_V", [128, NCH, D], BF16))
        ksqr = esR.enter_context(nc.sbuf_tensor("r_ksq", [128, 16], F32))
        b_KT = [Buf() for _ in range(NCH)]
        b_V = [Buf() for _ in range(NCH)]
        b_ksqr = Buf()
        kb.op("dve", lambda: nc.vector.memset(ksqr[:], 0.0), [], [b_ksqr])
        with ExitStack() as esB:
            EB = esB.enter_context

            def sbb(name, shape, dt=F32):
                return EB(nc.sbuf_tensor("b_" + name, list(shape), dt))

            wkv = sbb("wkv", [128, 8, 2 * D], BF16)
            b_wkv = Buf()
            for k in range(8):
                kb.dma("pool", wkv[:, k, :], wkv_d[k * 128:(k + 1) * 128, :], writes=[b_wkv])
            for k in range(8):
                kb.op("dve", lambda k=k: nc.vector.tensor_scalar(out=wkv[:, k, :], in0=wkv[:, k, :], scalar1=nkv[:, k:k + 1],
                                                               scalar2=None, op0=ALU.mult), [b_wkv, b_prm], [b_wkv])
            ht = sbb("ht", [128, D])
            junk2 = sbb("junk2", [128, D])
            hsb = sbb("hsb", [128, D], BF16)
            hnT = sbb("hnT", [128, 8, 128], BF16)
            kt = sbb("kt", [128, 16, 64])
            vt = sbb("vt", [128, D])
            cs = sbb("cs", [128, 16])
            rt = sbb("rt", [128, 4, 16, 8])
            sm2 = sbb("sm2", [128, 4])
            zt = sbb("zt", [128, D])
            ktb = sbb("ktb", [128, D], BF16)
            ksq = sbb("ksq", [128, 16])
            BB = {n: Buf(n) for n in ["ht", "junk2", "hsb", "hnT", "kt", "vt", "cs", "rt", "sm2", "zt", "ktb", "ksq"]}
            kb.op("dve", lambda: nc.vector.memset(zt[:], 0.0), [], [BB["zt"]])

            def kvchunk(idx, is_s, c):
                kb.dma("sp", ht[:], h1_d[idx], writes=[BB["ht"]])
                if is_s:
                    kb.dma("sp", cs[:, 0:8], coss_d[:, :], writes=[BB["cs"]])
                    kb.dma("sp", cs[:, 8:16], sins_d[:, :], writes=[BB["cs"]])
                else:
                    kb.dma("sp", cs[:, 0:8], cosp_d[c * 128:(c + 1) * 128, :], writes=[BB["cs"]])
                    kb.dma("sp", cs[:, 8:16], sinp_d[c * 128:(c + 1) * 128, :], writes=[BB["cs"]])
                kb.op("act", lambda: nc.scalar.activation(out=junk2[:], in_=ht[:], func=AF.Square, accum_out=sm2[:, 0:1]),
                      [BB["ht"]], [BB["junk2"], BB["sm2"]])
                kb.op("act", lambda: nc.scalar.activation(out=sm2[:, 1:2], in_=sm2[:, 0:1], func=AF.Ln, scale=1.0 / D, bias=epsc[:, 0:1]),
                      [BB["sm2"], b_misc], [BB["sm2"]])
                kb.op("act", lambda: nc.scalar.activation(out=sm2[:, 1:2], in_=sm2[:, 1:2], func=AF.Exp, scale=-0.5), [BB["sm2"]], [BB["sm2"]])
                kb.op("dve", lambda: nc.vector.tensor_scalar(out=hsb[:], in0=ht[:], scalar1=sm2[:, 1:2], scalar2=None, op0=ALU.mult),
                      [BB["ht"], BB["sm2"]], [BB["hsb"]])
                p, bp = bank()
                pb = p[:].bitcast(BF16)
                for k in range(8):
                    kb.op("pe", lambda k=k, pb=pb: nc.tensor.transpose(out=pb[:, k * 128:(k + 1) * 128], in_=hsb[:, k * 128:(k + 1) * 128],
                                                                     identity=identb[:]), [BB["hsb"], b_misc], [bp])
                kb.op("act", lambda pb=pb: nc.scalar.copy(out=hnT[:].rearrange("p k t -> p (k t)"), in_=pb), [bp], [BB["hnT"]])
                for blk in range(4):
                    p, bp = bank()
                    for k in range(8):
                        kb.op("pe", lambda k=k, blk=blk, p=p: nc.tensor.matmul(p[:], lhsT=hnT[:, k, :], rhs=wkv[:, k, blk * 512:(blk + 1) * 512],
                                                                           start=(k == 0), stop=(k == 7)), [BB["hnT"], b_wkv], [bp])
                    if blk < 2:
                        kb.op("act", lambda blk=blk, p=p: nc.scalar.copy(out=kt[:, blk * 8:(blk + 1) * 8, :].rearrange("p a d -> p (a d)"), in_=p[:]),
                              [bp], [BB["kt"]])
                    else:
                        kb.op("act", lambda blk=blk, p=p: nc.scalar.copy(out=vt[:, (blk - 2) * 512:(blk - 1) * 512], in_=p[:]), [bp], [BB["vt"]])
                cosb = cs[:, 0:8].unsqueeze(1).to_broadcast([128, 16, 8])
                sinb = cs[:, 8:16].unsqueeze(1).to_broadcast([128, 16, 8])
                x1 = kt[:, :, 0:8]
                x2 = kt[:, :, 8:16]
                kb.op("dve", lambda: nc.vector.tensor_tensor(out=rt[:, 0], in0=x1, in1=cosb, op=ALU.mult), [BB["kt"], BB["cs"]], [BB["rt"]])
                kb.op("dve", lambda: nc.vector.tensor_tensor(out=rt[:, 1], in0=x2, in1=sinb, op=ALU.mult), [BB["kt"], BB["cs"]], [BB["rt"]])
                kb.op("dve", lambda: nc.vector.tensor_tensor(out=rt[:, 2], in0=x2, in1=cosb, op=ALU.mult), [BB["kt"], BB["cs"]], [BB["rt"]])
                kb.op("dve", lambda: nc.vector.tensor_tensor(out=rt[:, 3], in0=x1, in1=sinb, op=ALU.mult), [BB["kt"], BB["cs"]], [BB["rt"]])
                kb.op("dve", lambda: nc.vector.tensor_tensor(out=x1, in0=rt[:, 0], in1=rt[:, 1], op=ALU.subtract), [BB["rt"]], [BB["kt"]])
                kb.op("dve", lambda: nc.vector.tensor_tensor(out=x2, in0=rt[:, 2], in1=rt[:, 3], op=ALU.add), [BB["rt"]], [BB["kt"]])
                ktf = kt[:].rearrange("p a d -> p (a d)")
                if is_s:
                    s = idx - NCH
                    kb.dma("sp", ks_o[s:s + 1, :], ktf[0:1, :], reads=[BB["kt"]])
                    kb.dma("sp", vs_o[s:s + 1, :], vt[0:1, :], reads=[BB["vt"]])
                else:
                    kb.dma("sp", kp_o[c * 128:(c + 1) * 128, :], ktf, reads=[BB["kt"]])
                    kb.dma("sp", vp_o[c * 128:(c + 1) * 128, :], vt[:], reads=[BB["vt"]])
                    kb.op("pool", lambda: nc.gpsimd.tensor_copy(out=V_all[:, c, :], in_=vt[:]), [BB["vt"]], [b_V[c]])
                    kb.op("act", lambda: nc.scalar.copy(out=ktb[:], in_=ktf), [BB["kt"]], [BB["ktb"]])
                    kb.op("dve", lambda: nc.vector.tensor_tensor(out=junk2[:], in0=ktf, in1=ktf, op=ALU.mult), [BB["kt"]], [BB["junk2"]])
                    kb.op("dve", lambda: nc.vector.tensor_reduce(out=ksq[:], in_=junk2[:].rearrange("p (a d) -> p a d", a=16), axis=AX.X, op=ALU.add),
                          [BB["junk2"]], [BB["ksq"]])
                    kb.op("dve", lambda: nc.vector.tensor_tensor(out=ksqr[:], in0=ksqr[:], in1=ksq[:], op=ALU.max), [BB["ksq"], b_ksqr], [b_ksqr])
                    p, bp = bank()
                    pb = p[:].bitcast(BF16)
                    for h in range(8):
                        kb.op("pe", lambda h=h, pb=pb: nc.tensor.transpose(out=pb[:, h * 128:(h + 1) * 128], in_=ktb[:, h * 128:(h + 1) * 128],
                                                                         identity=identb[:]), [BB["ktb"], b_misc], [bp])
                    kb.op("act", lambda pb=pb: nc.scalar.copy(out=KT_all[:, c, :, :].rearrange("p h t -> p (h t)"), in_=pb), [bp], [b_KT[c]])

            for c in range(NCH):
                kvchunk(c, False, c)
            for s in range(NS):
                kvchunk(NCH + s, True, 0)
            kb.barrier()

        if os.environ.get("DBG_STOP") == "B":
            return nc
        LAM_INIT = 0.8 - 0.6 * float(np.exp(-0.3 * 1))
        def phaseC(mode):
          is_sm = mode == "sample"
          with ExitStack() as esC:
            EC = esC.enter_context

            def sc(name, shape, dt=F32):
                return EC(nc.sbuf_tensor(("d_" if is_sm else "c_") + name, list(shape), dt))

            winb = sc("winb", [128, 8, 2 * D], BF16)
            woutb = sc("woutb", [128, 8, D], BF16)
            nbt = sc("nbt", [128, 8])
            b_wb = Buf()
            kb.dma("sp", nbt[:], nb_d[:, :], writes=[b_wb])
            for k in range(8):
                kb.dma("pool", winb[:, k, :], winb_d[k * 128:(k + 1) * 128, :], writes=[b_wb])
            for k in range(8):
                kb.dma("pool", woutb[:, k, :], woutb_d[k * 128:(k + 1) * 128, :], writes=[b_wb])
            for k in range(8):
                kb.op("dve", lambda k=k: nc.vector.tensor_scalar(out=winb[:, k, :], in0=winb[:, k, :], scalar1=nbt[:, k:k + 1],
                                                               scalar2=None, op0=ALU.mult), [b_wb], [b_wb])
            lamt = sc("lamt", [128, 4, 64])
            lsm = sc("lsm", [128, 8])
            sgb = sc("sgb", [128, 128])
            nfb = sc("nfb", [128, D])
            onesb = sc("onesb", [128, 2], BF16)
            b_c0 = Buf()
            for i in range(4):
                kb.dma("sp", lamt[:, i, :], lam_d[i].partition_broadcast(128), writes=[b_c0])
            kb.dma("sp", sgb[:], subln_d.partition_broadcast(128), writes=[b_c0])
            kb.dma("sp", nfb[:], nf_d.partition_broadcast(128), writes=[b_c0])
            kb.op("dve", lambda: nc.vector.memset(onesb[:], 1.0), [], [b_c0])
            kb.op("dve", lambda: nc.vector.tensor_tensor(out=lamt[:, 0, :], in0=lamt[:, 0, :], in1=lamt[:, 1, :], op=ALU.mult), [b_c0], [b_c0])
            kb.op("dve", lambda: nc.vector.tensor_tensor(out=lamt[:, 2, :], in0=lamt[:, 2, :], in1=lamt[:, 3, :], op=ALU.mult), [b_c0], [b_c0])
            kb.op("dve", lambda: nc.vector.tensor_reduce(out=lsm[:, 0:1], in_=lamt[:, 0, :], axis=AX.X, op=ALU.add), [b_c0], [b_c0])
            kb.op("dve", lambda: nc.vector.tensor_reduce(out=lsm[:, 1:2], in_=lamt[:, 2, :], axis=AX.X, op=ALU.add), [b_c0], [b_c0])
            kb.op("act", lambda: nc.scalar.activation(out=lsm[:, 2:4], in_=lsm[:, 0:2], func=AF.Exp), [b_c0], [b_c0])
            kb.op("dve", lambda: nc.vector.tensor_tensor(out=lsm[:, 4:5], in0=lsm[:, 3:4], in1=lsm[:, 2:3], op=ALU.subtract), [b_c0], [b_c0])
            kb.op("dve", lambda: nc.vector.tensor_scalar(out=lsm[:, 4:5], in0=lsm[:, 4:5], scalar1=-LAM_INIT, scalar2=None, op0=ALU.add), [b_c0], [b_c0])
            kb.op("dve", lambda: nc.vector.tensor_scalar(out=sgb[:], in0=sgb[:], scalar1=1.0 - LAM_INIT, scalar2=None, op0=ALU.mult), [b_c0], [b_c0])
            neglam = lsm[:, 4:5]

            ht = sc("ht", [128, D])
            jk = sc("jk", [128, D])
            hsb = sc("hsb", [128, D], BF16)
            hnT = sc("hnT", [128, 8, 128], BF16)
            qt = sc("qt", [128, 16, 64])
            gate = sc("gate", [128, D])
            cs = sc("cs", [128, 16])
            rt = sc("rt", [128, 4, 16, 8])
            qb = sc("qb", [128, D], BF16)
            qT = sc("qT", [128, 8, 128], BF16)
            sq = sc("sq", [128, 64])
            cb16 = sc("cb16", [16, 24])
            dg = sc("dg", [16, 16])
            cbias = sc("cbias", [128, 16])
            NPB = 3
            Pb = [sc("P%d" % i, [128, 512], BF16) for i in range(NPB)]
            b_P = [Buf() for _ in range(NPB)]
            pidx = [0]
            osb = sc("osb", [128, 8, 128])
            t1 = sc("t1", [128, 8, 128])
            ob = sc("ob", [128, D], BF16)
            oT = sc("oT", [128, 8, 128], BF16)
            CB = {n: Buf(n) for n in ["ht", "jk", "hsb", "hnT", "qt", "gate", "cs", "rt", "qb", "qT", "sq", "cb16", "dg", "cbias",
                                      "osb", "t1", "ob", "oT"]}
            rot = [5]

            def rbank():
                i = rot[0]
                rot[0] = 5 + (i - 5 + 1) % 3
                return ps[i], b_ps[i]

            if not is_sm:
                p, bp = rbank()
                kb.op("pe", lambda p=p: nc.tensor.transpose(out=p[0:16, 0:128], in_=ksqr[:], identity=identf), [b_ksqr, b_cst], [bp])
                kb.op("dve", lambda p=p: nc.vector.tensor_reduce(out=cb16[:, 0:1], in_=p[0:16, 0:128], axis=AX.X, op=ALU.max), [bp], [CB["cb16"]])
            else:
                I32 = mybir.dt.int32
                c2 = sc("c2", [128, 652])
                b_c2 = Buf()
                kb.dma("sp", c2[:], cst2_d[:, :], writes=[b_c2])
                E40R = c2[:, 0:640].rearrange("p (h c) -> p h c", h=8)
                M40 = c2[0:40, 640:648]
                iotac = c2[:, 648:649]
                A40 = c2[0:40, 649:650]
                B40 = c2[0:40, 650:651]
                pti = sc("pti", [128, NS * NP], I32)
                ptf = sc("ptf", [128, NS * NP])
                idx = sc("idx", [128, NS * NP], I32)
                b_idx = Buf()
                kb.dma("sp", pti[:], pt_d.partition_broadcast(128), writes=[b_idx])
                kb.op("dve", lambda: nc.vector.tensor_copy(out=ptf[:], in_=pti[:]), [b_idx], [b_idx])
                kb.op("dve", lambda: nc.vector.tensor_scalar(out=ptf[:], in0=ptf[:], scalar1=128.0, scalar2=iotac, op0=ALU.mult, op1=ALU.add),
                      [b_idx, b_c2], [b_idx])
                kb.op("dve", lambda: nc.vector.tensor_copy(out=idx[:], in_=ptf[:]), [b_idx], [b_idx])
                lamcol = sc("lamcol", [40, 1])
                kb.op("dve", lambda: nc.vector.scalar_tensor_tensor(out=lamcol[:], in0=B40, scalar=lsm[0:40, 4:5], in1=A40, op0=ALU.mult, op1=ALU.add),
                      [b_c2, b_c0], [b_c2])
                qcol = sc("qcol", [128, 8])
                QB = sc("QB", [128, 8, 80])
                Sall = sc("Sall", [40, NT * 128])
                PnT = sc("PnT", [128, NT, 40])
                GP = 2
                NPG = 4
                pg = [sc("pg%d" % i, [128, GP, D]) for i in range(NPG)]
                b_pg = [[Buf() for _ in range(GP)] for _ in range(NPG)]
                pgb = [sc("pgb%d" % i, [128, GP, D], BF16) for i in range(2)]
                b_pgb = [Buf(), Buf()]
                QBb = sc("QBb", [128, 8, 80], BF16)
                KTs = sc("KTs", [128, 8, 128])
                Vs = sc("Vs", [128, D])
                tmp40 = sc("tmp40", [40, D])
                st40 = sc("st40", [40, 8])
                SB = {n: Buf(n) for n in ["qcol", "QB", "QBb", "Sall", "PnT", "KTs", "Vs", "tmp40", "st40"]}

            def rstd_c(ssq_ap, out_ap, n, bufs):
                kb.op("act", lambda: nc.scalar.activation(out=out_ap, in_=ssq_ap, func=AF.Ln, scale=1.0 / n, bias=epsc[:, 0:1]), bufs + [b_misc], bufs)
                kb.op("act", lambda: nc.scalar.activation(out=out_ap, in_=out_ap, func=AF.Exp, scale=-0.5), bufs, bufs)

            def qtile(i):
                kb.dma("sp", ht[:], h1_d[(NCH + i) if is_sm else i], writes=[CB["ht"]])
                if is_sm:
                    kb.dma("sp", cs[:, 0:8], coss_d[:, :], writes=[CB["cs"]])
                    kb.dma("sp", cs[:, 8:16], sins_d[:, :], writes=[CB["cs"]])
                else:
                    kb.dma("sp", cs[:, 0:8], cosp_d[i * 128:(i + 1) * 128, :], writes=[CB["cs"]])
                    kb.dma("sp", cs[:, 8:16], sinp_d[i * 128:(i + 1) * 128, :], writes=[CB["cs"]])
                kb.op("act", lambda: nc.scalar.activation(out=jk[:], in_=ht[:], func=AF.Square, accum_out=sq[:, 0:1]), [CB["ht"]], [CB["jk"], CB["sq"]])
                rstd_c(sq[:, 0:1], sq[:, 1:2], D, [CB["sq"]])
                kb.op("dve", lambda: nc.vector.tensor_scalar(out=hsb[:], in0=ht[:], scalar1=sq[:, 1:2], scalar2=None, op0=ALU.mult),
                      [CB["ht"], CB["sq"]], [CB["hsb"]])
                p, bp = rbank()
                pb = p[:].bitcast(BF16)
                for k in range(8):
                    kb.op("pe", lambda k=k, pb=pb: nc.tensor.transpose(out=pb[:, k * 128:(k + 1) * 128], in_=hsb[:, k * 128:(k + 1) * 128],
                                                                     identity=identb[:]), [CB["hsb"], b_misc], [bp])
                kb.op("act", lambda pb=pb: nc.scalar.copy(out=hnT[:].rearrange("p k t -> p (k t)"), in_=pb), [bp], [CB["hnT"]])
                for blk in range(4):
                    p, bp = rbank()
                    for k in range(8):
                        kb.op("pe", lambda k=k, blk=blk, p=p: nc.tensor.matmul(p[:], lhsT=hnT[:, k, :], rhs=winb[:, k, blk * 512:(blk + 1) * 512],
                                                                           start=(k == 0), stop=(k == 7)), [CB["hnT"], b_wb], [bp])
                    if blk < 2:
                        kb.op("act", lambda blk=blk, p=p: nc.scalar.copy(out=qt[:, blk * 8:(blk + 1) * 8, :].rearrange("p a d -> p (a d)"), in_=p[:]),
                              [bp], [CB["qt"]])
                    else:
                        kb.op("act", lambda blk=blk, p=p: nc.scalar.activation(out=gate[:, (blk - 2) * 512:(blk - 1) * 512], in_=p[:], func=AF.Silu),
                              [bp], [CB["gate"]])
                cosb = cs[:, 0:8].unsqueeze(1).to_broadcast([128, 16, 8])
                sinb = cs[:, 8:16].unsqueeze(1).to_broadcast([128, 16, 8])
                x1 = qt[:, :, 0:8]
                x2 = qt[:, :, 8:16]
                kb.op("dve", lambda: nc.vector.tensor_tensor(out=rt[:, 0], in0=x1, in1=cosb, op=ALU.mult), [CB["qt"], CB["cs"]], [CB["rt"]])
                kb.op("dve", lambda: nc.vector.tensor_tensor(out=rt[:, 1], in0=x2, in1=sinb, op=ALU.mult), [CB["qt"], CB["cs"]], [CB["rt"]])
                kb.op("dve", lambda: nc.vector.tensor_tensor(out=rt[:, 2], in0=x2, in1=cosb, op=ALU.mult), [CB["qt"], CB["cs"]], [CB["rt"]])
                kb.op("dve", lambda: nc.vector.tensor_tensor(out=rt[:, 3], in0=x1, in1=sinb, op=ALU.mult), [CB["qt"], CB["cs"]], [CB["rt"]])
                kb.op("dve", lambda: nc.vector.tensor_tensor(out=x1, in0=rt[:, 0], in1=rt[:, 1], op=ALU.subtract), [CB["rt"]], [CB["qt"]])
                kb.op("dve", lambda: nc.vector.tensor_tensor(out=x2, in0=rt[:, 2], in1=rt[:, 3], op=ALU.add), [CB["rt"]], [CB["qt"]])
                qtf = qt[:].rearrange("p a d -> p (a d)")
                if is_sm:
                    sample_attn(i, qtf)
                else:
                    prompt_attn(i, qtf)
                tail(i)

            def prompt_attn(i, qtf):
                kb.op("dve", lambda: nc.vector.tensor_tensor(out=jk[:], in0=qtf, in1=qtf, op=ALU.mult), [CB["qt"]], [CB["jk"]])
                kb.op("dve", lambda: nc.vector.tensor_reduce(out=sq[:, 16:32], in_=jk[:].rearrange("p (a d) -> p a d", a=16), axis=AX.X, op=ALU.add),
                      [CB["jk"]], [CB["sq"]])
                p, bp = rbank()
                kb.op("pe", lambda p=p: nc.tensor.transpose(out=p[0:16, 0:128], in_=sq[:, 16:32], identity=identf), [CB["sq"], b_cst], [bp])
                kb.op("dve", lambda p=p: nc.vector.tensor_reduce(out=cb16[:, 1:2], in_=p[0:16, 0:128], axis=AX.X, op=ALU.max), [bp], [CB["cb16"]])
                kb.op("dve", lambda: nc.vector.tensor_tensor(out=cb16[:, 2:3], in0=cb16[:, 0:1], in1=cb16[:, 1:2], op=ALU.mult), [CB["cb16"]], [CB["cb16"]])
                kb.op("act", lambda: nc.scalar.activation(out=cb16[:, 3:4], in_=cb16[:, 2:3], func=AF.Ln), [CB["cb16"]], [CB["cb16"]])
                kb.op("act", lambda: nc.scalar.activation(out=cb16[:, 3:4], in_=cb16[:, 3:4], func=AF.Exp, scale=0.5), [CB["cb16"]], [CB["cb16"]])
                kb.op("dve", lambda: nc.vector.tensor_scalar(out=dg[:], in0=identf[0:16, 0:16], scalar1=cb16[:, 3:4], scalar2=-0.125,
                                                            op0=ALU.mult, op1=ALU.mult), [CB["cb16"], b_cst], [CB["dg"]])
                p, bp = rbank()
                kb.op("pe", lambda p=p: nc.tensor.matmul(p[:, 0:16], lhsT=onesf[0:16, :], rhs=dg[:], start=True, stop=True), [CB["dg"], b_misc], [bp])
                kb.op("act", lambda p=p: nc.scalar.copy(out=cbias[:], in_=p[:, 0:16]), [bp], [CB["cbias"]])
                kb.op("act", lambda: nc.scalar.copy(out=qb[:], in_=qtf), [CB["qt"]], [CB["qb"]])
                p, bp = rbank()
                pb = p[:].bitcast(BF16)
                for h in range(8):
                    kb.op("pe", lambda h=h, pb=pb: nc.tensor.transpose(out=pb[:, h * 128:(h + 1) * 128], in_=qb[:, h * 128:(h + 1) * 128],
                                                                     identity=identb[:]), [CB["qb"], b_misc], [bp])
                kb.op("act", lambda pb=pb: nc.scalar.copy(out=qT[:].rearrange("p h t -> p (h t)"), in_=pb), [bp], [CB["qT"]])
                for b in range(5):
                    kb.op("dve", lambda b=b: nc.vector.memset(ps[b][:], 0.0), [], [b_ps[b]])
                tiles = list(range(i + 1))
                for g0 in range(0, len(tiles), 4):
                    grp = tiles[g0:g0 + 4]
                    n = len(grp)
                    for hj in range(16):
                        h, j = hj // 2, hj % 2
                        p, bp = rbank()
                        for tt, t in enumerate(grp):
                            kb.op("pe", lambda tt=tt, t=t, p=p, h=h, j=j: nc.tensor.matmul(
                                p[:, tt * 128:(tt + 1) * 128], lhsT=KT_all[64 * j:64 * j + 64, t, h, :], rhs=qT[64 * j:64 * j + 64, h, :],
                                start=True, stop=True), [b_KT[t], CB["qT"]], [bp])
                        pi = pidx[0]
                        pidx[0] = (pi + 1) % NPB
                        P, bP = Pb[pi], b_P[pi]
                        kb.op("act", lambda p=p, P=P, n=n, hj=hj: nc.scalar.activation(out=P[:, 0:n * 128], in_=p[:, 0:n * 128], func=AF.Exp,
                                                                                    scale=0.125, bias=cbias[:, hj:hj + 1]), [bp, CB["cbias"]], [bP])
                        if grp[-1] == i:
                            tt = n - 1
                            kb.op("dve", lambda P=P, tt=tt: nc.vector.tensor_tensor(out=P[:, tt * 128:(tt + 1) * 128], in0=P[:, tt * 128:(tt + 1) * 128],
                                                                                 in1=tri, op=ALU.mult), [bP, b_cst], [bP])
                        ob_, oc = hj // 4, (hj % 4) * 128
                        for tt, t in enumerate(grp):
                            kb.op("pe", lambda tt=tt, t=t, P=P, h=h, ob_=ob_, oc=oc: nc.tensor.matmul(
                                ps[ob_][:, oc:oc + 128], lhsT=P[:, tt * 128:(tt + 1) * 128], rhs=V_all[:, t, h * 128:(h + 1) * 128],
                                start=False, stop=False, skip_group_check=True), [bP, b_V[t]], [b_ps[ob_]])
                            kb.op("pe", lambda tt=tt, P=P, hj=hj: nc.tensor.matmul(
                                ps[4][:, hj:hj + 1], lhsT=P[:, tt * 128:(tt + 1) * 128], rhs=onesb[:, 0:1],
                                start=False, stop=False, skip_group_check=True), [bP, b_c0], [b_ps[4]])
                kb.op("dve", lambda: nc.vector.reciprocal(out=sq[:, 32:48], in_=ps[4][:, 0:16]), [b_ps[4]], [CB["sq"]])
                rz = sq[:, 32:48].rearrange("p (h j) -> p h j", j=2)
                kb.op("dve", lambda: nc.vector.tensor_scalar(out=rz[:, :, 1], in0=rz[:, :, 1], scalar1=neglam, scalar2=None, op0=ALU.mult),
                      [CB["sq"], b_c0], [CB["sq"]])
                for b in range(4):
                    ov = ps[b][:].rearrange("p (h j d) -> p h j d", h=2, j=2)
                    kb.op("dve", lambda b=b, ov=ov: nc.vector.tensor_tensor(out=osb[:, 2 * b:2 * b + 2, :], in0=ov[:, :, 0, :],
                                                                          in1=rz[:, 2 * b:2 * b + 2, 0:1].to_broadcast([128, 2, 128]), op=ALU.mult),
                          [b_ps[b], CB["sq"]], [CB["osb"]])
                    kb.op("dve", lambda b=b, ov=ov: nc.vector.tensor_tensor(out=t1[:, 2 * b:2 * b + 2, :], in0=ov[:, :, 1, :],
                                                                          in1=rz[:, 2 * b:2 * b + 2, 1:2].to_broadcast([128, 2, 128]), op=ALU.mult),
                          [b_ps[b], CB["sq"]], [CB["t1"]])
                kb.op("dve", lambda: nc.vector.tensor_tensor(out=osb[:], in0=osb[:], in1=t1[:], op=ALU.add), [CB["osb"], CB["t1"]], [CB["osb"]])

            def sample_attn(b, qtf):
                osbf = osb[:].rearrange("p h d -> p (h d)")
                for half in range(2):
                    p, bp = rbank()
                    for hh in range(4):
                        h = half * 4 + hh
                        kb.op("pe", lambda hh=hh, h=h, p=p: nc.tensor.transpose(out=p[:, hh * 128:(hh + 1) * 128], in_=qtf[:, h * 128:(h + 1) * 128],
                                                                               identity=identf), [CB["qt"], b_cst], [bp])
                    kb.op("act", lambda half=half, p=p: nc.scalar.copy(out=qcol[:, half * 4:(half + 1) * 4],
                                                                      in_=p[:].rearrange("p (h t) -> p h t", h=4)[:, :, 0]), [bp], [SB["qcol"]])
                kb.op("dve", lambda: nc.vector.tensor_tensor(out=QB[:], in0=E40R, in1=qcol[:].unsqueeze(2).to_broadcast([128, 8, 80]), op=ALU.mult),
                      [SB["qcol"], b_c2], [SB["QB"]])
                kb.op("pool", lambda: nc.gpsimd.memset(KTs[:], 0.0), [], [SB["KTs"]])
                kb.op("pool", lambda: nc.gpsimd.memset(Vs[:], 0.0), [], [SB["Vs"]])
                with nc.allow_non_contiguous_dma(reason="tiny feature-major column load"):
                    kb.dma("sp", KTs[:, :, 0], ks_o[b].rearrange("(h p) -> p h", p=128), writes=[SB["KTs"]])
                kb.dma("sp", Vs[0:1, :], vs_o[b:b + 1, :], writes=[SB["Vs"]])
                kb.op("act", lambda: nc.scalar.copy(out=QBb[:], in_=QB[:]), [SB["QB"]], [SB["QBb"]])
                groups = [list(range(g0, min(g0 + GP, NP))) for g0 in range(0, NP, GP)]
                import os
                if os.environ.get('DBG_NOPAGES'):
                    groups = []
                    kb.op('dve', lambda: nc.vector.memset(Sall[:], -1e30), [], [SB['Sall']])
                    kb.op('dve', lambda: nc.vector.memset(PnT[:], 0.0), [], [SB['PnT']])
                gi = [0]

                def load_group(grp, src_d):
                    k = gi[0] % NPG
                    gi[0] += 1
                    for tt, pgi in enumerate(grp):
                        col = b * NP + pgi
                        kb._deps("pool", [b_idx], [b_pg[k][tt]])
                        k2 = kb.dnext
                        kb.dnext = (kb.dnext + 1) % len(kb.dsem)
                        si = kb.dsem[k2]
                        if kb.dtot[k2] > 0:
                            kb._wait("pool", (si, kb.dtot[k2]))
                        ins = nc.gpsimd.indirect_dma_start(out=pg[k][:, tt, :], out_offset=None, in_=src_d[:, :],
                                                           in_offset=bass.IndirectOffsetOnAxis(ap=idx[:, col:col + 1], axis=0))
                        kb.dtot[k2] += 16
                        ev = (si, kb.dtot[k2])
                        ins.then_inc(kb.sems[si], 16)
                        b_idx.r.append(ev)
                        b_pg[k][tt].w = ev
                        b_pg[k][tt].r = []
                    return pg[k], b_pg[k]

                def score_mm(p, bp, tt, rhs_tile, rbuf):
                    for h in range(8):
                        for j in range(2):
                            kb.op("pe", lambda h=h, j=j: nc.tensor.matmul(p[0:40, tt * 128:(tt + 1) * 128], lhsT=QB[:, h, j * 40:(j + 1) * 40],
                                                                       rhs=rhs_tile[:, h * 128:(h + 1) * 128],
                                                                       start=(h == 0 and j == 0), stop=(h == 7 and j == 1)), [SB["QB"], rbuf], [bp])
                for g, grp in enumerate(groups):
                    buf, bbuf = load_group(grp, poolk_d)
                    n = len(grp)
                    cb_, bcb = pgb[g % 2], b_pgb[g % 2]
                    kb.op("act", lambda buf=buf, cb_=cb_, n=n: nc.scalar.copy(out=cb_[:, 0:n, :], in_=buf[:, 0:n, :]), bbuf[0:n], [bcb])
                    p, bp = rbank()
                    for h in range(8):
                        for j in range(2):
                            kb.op("pe", lambda h=h, j=j, p=p, cb_=cb_, n=n: nc.tensor.matmul(
                                p[0:40, 0:n * 128].rearrange("p (t k) -> p t k", t=n), lhsT=QBb[:, h, j * 40:(j + 1) * 40],
                                rhs=cb_[:, 0:n, h * 128:(h + 1) * 128], start=(h == 0 and j == 0), stop=(h == 7 and j == 1)),
                                [SB["QBb"], bcb], [bp])
                    kb.op("dve", lambda p=p, g=g, n=n: nc.vector.tensor_copy(out=Sall[:, g * GP * 128:g * GP * 128 + n * 128], in_=p[0:40, 0:n * 128]),
                          [bp], [SB["Sall"]])
                p, bp = rbank()
                score_mm(p, bp, 0, KTs[:].rearrange("p h t -> p (h t)"), SB["KTs"])
                kb.op("act", lambda p=p: nc.scalar.copy(out=Sall[:, NP * 128:NT * 128], in_=p[0:40, 0:128]), [bp], [SB["Sall"]])
                kb.op("dve", lambda: nc.vector.memset(Sall[:, NP * 128 + 1:NT * 128], -1e30), [], [SB["Sall"]])
                kb.op("dve", lambda: nc.vector.tensor_reduce(out=st40[:, 0:1], in_=Sall[:], axis=AX.X, op=ALU.max), [SB["Sall"]], [SB["st40"]])
                kb.op("dve", lambda: nc.vector.tensor_scalar(out=st40[:, 1:2], in0=st40[:, 0:1], scalar1=-0.125, scalar2=None, op0=ALU.mult),
                      [SB["st40"]], [SB["st40"]])
                kb.op("act", lambda: nc.scalar.activation(out=Sall[:], in_=Sall[:], func=AF.Exp, scale=0.125, bias=st40[:, 1:2], accum_out=st40[:, 2:3]),
                      [SB["Sall"], SB["st40"]], [SB["Sall"], SB["st40"]])
                kb.op("dve", lambda: nc.vector.reciprocal(out=st40[:, 3:4], in_=st40[:, 2:3]), [SB["st40"]], [SB["st40"]])
                kb.op("dve", lambda: nc.vector.tensor_tensor(out=st40[:, 4:5], in0=st40[:, 3:4], in1=lamcol[:], op=ALU.mult), [SB["st40"], b_c2], [SB["st40"]])
                kb.op("dve", lambda: nc.vector.tensor_scalar(out=Sall[:], in0=Sall[:], scalar1=st40[:, 4:5], scalar2=None, op0=ALU.mult),
                      [SB["Sall"], SB["st40"]], [SB["Sall"]])
                for t0 in range(0, NT, 12):
                    ts_ = list(range(t0, min(t0 + 12, NT)))
                    p, bp = rbank()
                    for sl, t in enumerate(ts_):
                        kb.op("pe", lambda sl=sl, t=t, p=p: nc.tensor.transpose(out=p[:, sl * 40:(sl + 1) * 40], in_=Sall[:, t * 128:(t + 1) * 128],
                                                                               identity=identf[0:40, 0:40]), [SB["Sall"], b_cst], [bp])
                    n = len(ts_)
                    kb.op("act", lambda p=p, t0=t0, n=n: nc.scalar.copy(out=PnT[:, t0:t0 + n, :].rearrange("p t c -> p (t c)"), in_=p[:, 0:n * 40]),
                          [bp], [SB["PnT"]])
                for g, grp in enumerate(groups if not os.environ.get('DBG_NOPV') else []):
                    buf, bbuf = load_group(grp, poolv_d)
                    for tt, t in enumerate(grp if not os.environ.get('DBG_NOPVMM') else []):
                        for half in range(2):
                            kb.op("pe", lambda tt=tt, t=t, half=half, buf=buf: nc.tensor.matmul(
                                ps[half][0:40, :], lhsT=PnT[:, t, :], rhs=buf[:, tt, half * 512:(half + 1) * 512],
                                start=(t == 0), stop=False), [SB["PnT"], bbuf[tt]], [b_ps[half]])
                for half in range(2):
                    kb.op("pe", lambda half=half: nc.tensor.matmul(ps[half][0:40, :], lhsT=PnT[:, NP, :], rhs=Vs[:, half * 512:(half + 1) * 512],
                                                                  start=False, stop=True), [SB["PnT"], SB["Vs"]], [b_ps[half]])
                for half in range(2):
                    kb.op("dve", lambda half=half: nc.vector.tensor_tensor(
                        out=tmp40[:, half * 512:(half + 1) * 512].rearrange("p (h d) -> p h d", h=4),
                        in0=ps[half][0:40, :].rearrange("p (h d) -> p h d", h=4),
                        in1=M40[:, half * 4:(half + 1) * 4].unsqueeze(2).to_broadcast([40, 4, 128]), op=ALU.mult), [b_ps[half], b_c2], [SB["tmp40"]])
                kb.op("dve", lambda: nc.vector.memset(osb[:], 0.0), [], [CB["osb"]])
                for half in range(2):
                    p, bp = rbank()
                    kb.op("pe", lambda half=half, p=p: nc.tensor.matmul(p[0:1, :], lhsT=onesf[0:40, 0:1], rhs=tmp40[:, half * 512:(half + 1) * 512],
                                                                       start=True, stop=True), [SB["tmp40"], b_misc], [bp])
                    kb.op("act", lambda half=half, p=p: nc.scalar.copy(out=osbf[0:1, half * 512:(half + 1) * 512], in_=p[0:1, :]), [bp], [CB["osb"]])

            def tail(i):
                kb.op("dve", lambda: nc.vector.tensor_tensor(out=t1[:], in0=osb[:], in1=osb[:], op=ALU.mult), [CB["osb"]], [CB["t1"]])
                kb.op("dve", lambda: nc.vector.tensor_reduce(out=sq[:, 48:56], in_=t1[:], axis=AX.X, op=ALU.add), [CB["t1"]], [CB["sq"]])
                rstd_c(sq[:, 48:56], sq[:, 56:64], 128, [CB["sq"]])
                kb.op("dve", lambda: nc.vector.tensor_tensor(out=osb[:], in0=osb[:], in1=sq[:, 56:64].unsqueeze(2).to_broadcast([128, 8, 128]), op=ALU.mult),
                      [CB["osb"], CB["sq"]], [CB["osb"]])
                kb.op("dve", lambda: nc.vector.tensor_tensor(out=osb[:], in0=osb[:], in1=sgb[:].unsqueeze(1).to_broadcast([128, 8, 128]), op=ALU.mult),
                      [CB["osb"], b_c0], [CB["osb"]])
                kb.op("dve", lambda: nc.vector.tensor_tensor(out=ob[:], in0=osb[:].rearrange("p h d -> p (h d)"), in1=gate[:], op=ALU.mult),
                      [CB["osb"], CB["gate"]], [CB["ob"]])
                p, bp = rbank()
                pb = p[:].bitcast(BF16)
                for h in range(8):
                    kb.op("pe", lambda h=h, pb=pb: nc.tensor.transpose(out=pb[:, h * 128:(h + 1) * 128], in_=ob[:, h * 128:(h + 1) * 128],
                                                                     identity=identb[:]), [CB["ob"], b_misc], [bp])
                kb.op("act", lambda pb=pb: nc.scalar.copy(out=oT[:].rearrange("p h t -> p (h t)"), in_=pb), [bp], [CB["oT"]])
                for blk in range(2):
                    p, bp = rbank()
                    for k in range(8):
                        kb.op("pe", lambda k=k, blk=blk, p=p: nc.tensor.matmul(p[:], lhsT=oT[:, k, :], rhs=woutb[:, k, blk * 512:(blk + 1) * 512],
                                                                           start=(k == 0), stop=(k == 7)), [CB["oT"], b_wb], [bp])
                    kb.op("dve", lambda blk=blk, p=p: nc.vector.tensor_tensor(out=ht[:, blk * 512:(blk + 1) * 512], in0=ht[:, blk * 512:(blk + 1) * 512],
                                                                            in1=p[:], op=ALU.add), [bp, CB["ht"]], [CB["ht"]])
                kb.op("act", lambda: nc.scalar.activation(out=jk[:], in_=ht[:], func=AF.Square, accum_out=sq[:, 2:3]), [CB["ht"]], [CB["jk"], CB["sq"]])
                rstd_c(sq[:, 2:3], sq[:, 3:4], D, [CB["sq"]])
                kb.op("dve", lambda: nc.vector.scalar_tensor_tensor(out=jk[:], in0=ht[:], scalar=sq[:, 3:4], in1=nfb[:], op0=ALU.mult, op1=ALU.mult),
                      [CB["ht"], CB["sq"], b_c0], [CB["jk"]])
                if is_sm:
                    kb.dma("sp", ys_o[i:i + 1, :], jk[0:1, :], reads=[CB["jk"]])
                else:
                    kb.dma("sp", yp_o[i * 128:(i + 1) * 128, :], jk[:], reads=[CB["jk"]])

            for i in range(NS if is_sm else NCH):
                qtile(i)
            kb.barrier()

        phaseC("prompt")
        if os.environ.get("DBG_STOP") == "C":
            return nc
        esR.close()
        phaseC("sample")
        kb.barrier(engines=("sp",))
    return nc


def _rope_tables(pos):
    inv = (500000.0 ** (-(np.arange(0, 16, 2, dtype=np.float32)) / np.float32(16))).astype(np.float32)
    ang = pos.astype(np.float32)[:, None] * inv[None, :]
    return np.cos(ang).astype(np.float32), np.sin(ang).astype(np.float32)


def _consts():
    i = np.arange(128)
    tri = (i[:, None] <= i[None, :]).astype(np.float32)
    astr = (i[:, None] > i[None, :]).astype(np.float32)
    ident = np.eye(128, dtype=np.float32)
    return np.ascontiguousarray(np.concatenate([tri, astr, ident], axis=1))


def _consts2():
    c = np.zeros((128, 652), np.float32)
    e = np.zeros((128, 8, 2, 40), np.float32)
    for h in range(8):
        e[0:64, h, 0, h] = 1.0
        e[64:128, h, 1, 32 + h] = 1.0
    c[:, 0:640] = e.reshape(128, 640)
    for h in range(8):
        c[h, 640 + h] = 1.0
        c[32 + h, 640 + h] = 1.0
    c[:, 648] = np.arange(128)
    c[0:8, 649] = 1.0
    c[32:40, 650] = 1.0
    return c


def make_in_maps(inp, ncores, T, NS, NP=64, past=PAST):
    f = lambda a: np.ascontiguousarray(np.asarray(a, dtype=np.float32))
    cosp, sinp = _rope_tables(np.arange(T))
    coss, sins = _rope_tables(np.full((128,), past))
    convw = f(np.asarray(inp["conv_w"])[0].reshape(4, 24, 128).transpose(2, 1, 0))
    convb = f(np.asarray(inp["conv_b"])[0].reshape(24, 128).T)
    na = f(np.asarray(inp["norm_a"])[0].reshape(8, 128).T)
    gn = f(np.asarray(inp["gnorm_a"])[0].reshape(16, 128).T)
    nkv = f(np.asarray(inp["norm_kv"]).reshape(8, 128).T)
    shared = dict(w_in_a=f(inp["w_in_a"][0]), w_out_a=f(inp["w_out_a"][0]), w_kv=f(inp["w_kv"]), convw=convw, convb=convb, na=na, gn=gn,
                  nkv=nkv, dtb=f(inp["dt_bias"][0]), alog=f(inp["a_log"][0]), dsk=f(inp["d_skip"][0]), cst=_consts(),
                  cosp=f(cosp), sinp=f(sinp), coss=f(coss), sins=f(sins),
                  w_in_b=f(inp["w_in_b"][0]), w_out_b=f(inp["w_out_b"][0]), nb=f(np.asarray(inp["norm_b"])[0].reshape(8, 128).T),
                  lamv=f(np.stack([np.asarray(inp["lambda_q1"])[0], np.asarray(inp["lambda_k1"])[0],
                                   np.asarray(inp["lambda_q2"])[0], np.asarray(inp["lambda_k2"])[0]])),
                  subln=f(np.asarray(inp["subln_b"])[0]), nf=f(inp["norm_f"]))
    ck = np.asarray(inp["cache_k"])
    npool = ck.shape[0]
    poolk = np.ascontiguousarray(ck.reshape(npool, 128, 8, 128).transpose(0, 3, 2, 1)).reshape(npool * 128, D)
    poolv = np.ascontiguousarray(np.asarray(inp["cache_v"]).reshape(npool * 128, D))
    shared["poolk"] = poolk
    shared["poolv"] = poolv
    shared["cst2"] = _consts2()
    ptab = np.asarray(inp["page_table"]).astype(np.int32)
    maps = []
    for c in range(ncores):
        m = dict(shared)
        m["ptab"] = np.ascontiguousarray(ptab[c * NS:(c + 1) * NS, :NP].reshape(-1))
        m["x"] = f(inp["x_prompt"][c][:T])
        m["xsmp"] = f(np.asarray(inp["x_sample"])[c * NS:(c + 1) * NS, 0])
        sc = np.asarray(inp["state_conv"])[0, c * NS:(c + 1) * NS]
        m["sconv"] = f(sc.reshape(NS, 3, 24, 128).transpose(0, 3, 2, 1))
        ss = np.asarray(inp["state_ssm"])[0, c * NS:(c + 1) * NS]
        m["sssm"] = f(ss.reshape(NS, DI, 128).transpose(0, 2, 1))
        maps.append(m)
    return maps


_NC_CACHE = {}


def kernel(**inp):
    ncores, T, NS = 8, 2048, 4
    key = (T, NS)
    if key not in _NC_CACHE:
        _NC_CACHE[key] = build(T, NS)
    nc = _NC_CACHE[key]
    maps = make_in_maps(inp, ncores, T, NS)
    res = run_bass_kernel_spmd(nc, maps, core_ids=list(range(ncores))).results
    cat = lambda k: np.stack([np.asarray(r[k]) for r in res])
    y_p = cat("y_p")
    y_s = cat("y_s").reshape(ncores * NS, 1, D)
    k_p = cat("k_p").reshape(ncores, T, 8, 2, 64)
    v_p = cat("v_p").reshape(ncores, T, 8, 128)
    conv_p = cat("conv_p").transpose(0, 3, 2, 1).reshape(1, ncores, 3, CONV)
    ssm_p = cat("ssm_p").transpose(0, 2, 1).reshape(1, ncores, 32, 64, 128)
    k_s = cat("k_s").reshape(ncores * NS, 1, 8, 2, 64)
    v_s = cat("v_s").reshape(ncores * NS, 1, 8, 128)
    conv_s = cat("conv_s").reshape(ncores * NS, 128, 24, 3).transpose(0, 3, 2, 1).reshape(1, ncores * NS, 3, CONV)
    ssm_s = cat("ssm_s").reshape(ncores * NS, 128, DI).transpose(0, 2, 1).reshape(1, ncores * NS, 32, 64, 128)
    f = lambda a: np.ascontiguousarray(a, dtype=np.float32)
    return (f(y_p), f(y_s), f(k_p), f(v_p), f(conv_p), f(ssm_p), f(k_s), f(v_s), f(conv_s), f(ssm_s))
```

```python
import numpy as np
from contextlib import ExitStack
import concourse.bass as bass
import concourse.mybir as mybir
from concourse.bass_utils import run_bass_kernel_spmd

F32 = mybir.dt.float32
BF16 = mybir.dt.bfloat16
AF = mybir.ActivationFunctionType
ALU = mybir.AluOpType
AX = mybir.AxisListType

D = 1024
DI = 2048
CONV = 3072
INA = 5152
EPS = 1e-6
PAST = 8192


class Buf:
    __slots__ = ("w", "r", "name")

    def __init__(self, name=""):
        self.w = None
        self.r = []
        self.name = name


class KB:
    def __init__(self, nc, es, ndma=24):
        self.nc = nc
        self.eng = {"pe": nc.tensor, "act": nc.scalar, "dve": nc.vector, "pool": nc.gpsimd, "sp": nc.sync}
        self.sems = []
        self.csem = {}
        self.cnt = {}
        for e in ("pe", "act", "dve", "pool"):
            s = es.enter_context(nc.semaphore("c_" + e))
            self.csem[e] = len(self.sems)
            self.sems.append(s)
            self.cnt[e] = 0
        self.dsem = []
        self.dtot = []
        for i in range(ndma):
            s = es.enter_context(nc.semaphore("d%d" % i))
            self.dsem.append(len(self.sems))
            self.sems.append(s)
            self.dtot.append(0)
        self.dnext = 0
        self.seen = {e: {} for e in self.eng}

    def _wait(self, e, ev):
        si, v = ev
        if self.seen[e].get(si, 0) >= v:
            return
        self.seen[e][si] = v
        self.eng[e].wait_ge(self.sems[si], v)

    def _deps(self, e, reads, writes):
        for b in reads:
            if b.w is not None:
                self._wait(e, b.w)
        for b in writes:
            if b.w is not None:
                self._wait(e, b.w)
            for ev in b.r:
                self._wait(e, ev)

    def op(self, e, fn, reads=(), writes=()):
        self._deps(e, reads, writes)
        ins = fn()
        self.cnt[e] += 1
        ev = (self.csem[e], self.cnt[e])
        ins.then_inc(self.sems[ev[0]], 1)
        for b in reads:
            b.r.append(ev)
        for b in writes:
            b.w = ev
            b.r = []
        return ins

    def dma(self, q, out, in_, reads=(), writes=(), **kw):
        k = self.dnext
        self.dnext = (self.dnext + 1) % len(self.dsem)
        si = self.dsem[k]
        if self.dtot[k] > 0:
            self._wait(q, (si, self.dtot[k]))
        self._deps(q, reads, writes)
        ins = self.eng[q].dma_start(out=out, in_=in_, **kw)
        self.dtot[k] += 16
        ev = (si, self.dtot[k])
        ins.then_inc(self.sems[si], 16)
        for b in reads:
            b.r.append(ev)
        for b in writes:
            b.w = ev
            b.r = []
        return ev

    def barrier(self, engines=("pe", "act", "dve", "pool", "sp")):
        for e in engines:
            for o in ("pe", "act", "dve", "pool"):
                if self.cnt[o] > 0:
                    self._wait(e, (self.csem[o], self.cnt[o]))
            for k, si in enumerate(self.dsem):
                if self.dtot[k] > 0:
                    self._wait(e, (si, self.dtot[k]))


def build(T, NS, NP=64, NPOOL=2560):
    NCH = T // 128
    NT = NP + 1
    nc = bass.Bass("TRN2", target_bir_lowering=False)

    def din(name, shape, dt=F32):
        return nc.dram_tensor(name, list(shape), dt, kind="ExternalInput").ap()

    def dout(name, shape, dt=F32):
        return nc.dram_tensor(name, list(shape), dt, kind="ExternalOutput").ap()

    x_d = din("x", [T, D])
    xs_d = din("xsmp", [NS, D])
    sconv_d = din("sconv", [NS, 128, 24, 3])
    sssm_d = din("sssm", [NS, 128, DI])
    wina_d = din("w_in_a", [D, INA])
    wouta_d = din("w_out_a", [DI, D])
    wkv_d = din("w_kv", [D, 2 * D])
    convw_d = din("convw", [128, 24, 4])
    convb_d = din("convb", [128, 24])
    na_d = din("na", [128, 8])
    gn_d = din("gn", [128, 16])
    nkv_d = din("nkv", [128, 8])
    dtb_d = din("dtb", [32])
    alog_d = din("alog", [32])
    dsk_d = din("dsk", [32])
    cst_d = din("cst", [128, 384])
    cosp_d = din("cosp", [T, 8])
    sinp_d = din("sinp", [T, 8])
    coss_d = din("coss", [128, 8])
    sins_d = din("sins", [128, 8])
    winb_d = din("w_in_b", [D, 2 * D])
    woutb_d = din("w_out_b", [D, D])
    nb_d = din("nb", [128, 8])
    lam_d = din("lamv", [4, 64])
    subln_d = din("subln", [128])
    nf_d = din("nf", [D])
    poolk_d = din("poolk", [NPOOL * 128, D])
    poolv_d = din("poolv", [NPOOL * 128, D])
    pt_d = nc.dram_tensor("ptab", [NS * NP], mybir.dt.int32, kind="ExternalInput").ap()
    cst2_d = din("cst2", [128, 652])

    kp_o = dout("k_p", [T, D])
    vp_o = dout("v_p", [T, D])
    convp_o = dout("conv_p", [128, 24, 3])
    ssmp_o = dout("ssm_p", [128, DI])
    ks_o = dout("k_s", [NS, D])
    vs_o = dout("v_s", [NS, D])
    convs_o = dout("conv_s", [NS, 128, 24, 3])
    ssms_o = dout("ssm_s", [NS, 128, DI])
    yp_o = dout("y_p", [T, D])
    ys_o = dout("y_s", [NS, D])

    NSEQ = 1 + NS
    h1_d = nc.dram_tensor("h1d", [NCH + NS, 128, D], F32, kind="Internal").ap()

    with ExitStack() as es:
        E = es.enter_context
        kb = KB(nc, es)

        def sb(name, shape, dt=F32):
            return E(nc.sbuf_tensor("s_" + name, list(shape), dt))

        cst = sb("cst", [128, 384])
        b_cst = Buf()
        kb.dma("sp", cst[:], cst_d[:, :], writes=[b_cst])
        tri = cst[:, 0:128]
        astr = cst[:, 128:256]
        identf = cst[:, 256:384]
        identb = sb("identb", [128, 128], BF16)
        onesf = sb("onesf", [128, 128])
        b_misc = Buf()
        kb.op("dve", lambda: nc.vector.tensor_copy(out=identb[:], in_=identf), [b_cst], [b_misc])
        kb.op("dve", lambda: nc.vector.memset(onesf[:], 1.0), [], [b_misc])
        dmy = sb("dmy", [128, 2])
        kb.op("dve", lambda: nc.vector.memset(dmy[:], 0.0), [], [b_misc])

        def act_fence(bufs):
            kb.op("act", lambda: nc.scalar.copy(out=dmy[:, 0:1], in_=dmy[:, 1:2]), list(bufs), list(bufs))

        epsc = sb("epsc", [128, 1])
        kb.op("dve", lambda: nc.vector.memset(epsc[:], EPS), [], [b_misc])

        convw = sb("convw", [128, 24, 4])
        convb = sb("convb", [128, 24])
        na = sb("na", [128, 8])
        gn = sb("gn", [128, 16])
        nkv = sb("nkv", [128, 8])
        prm = sb("prm", [128, 96])
        b_prm = Buf()
        kb.dma("sp", convw[:], convw_d[:, :, :], writes=[b_prm])
        kb.dma("sp", convb[:], convb_d[:, :], writes=[b_prm])
        kb.dma("sp", na[:], na_d[:, :], writes=[b_prm])
        kb.dma("sp", gn[:], gn_d[:, :], writes=[b_prm])
        kb.dma("sp", nkv[:], nkv_d[:, :], writes=[b_prm])
        kb.dma("sp", prm[:, 0:32], dtb_d.partition_broadcast(128), writes=[b_prm])
        kb.dma("sp", prm[:, 32:64], alog_d.partition_broadcast(128), writes=[b_prm])
        kb.dma("sp", prm[:, 64:96], dsk_d.partition_broadcast(128), writes=[b_prm])
        dtb = prm[:, 0:32]
        dsk = prm[:, 64:96]
        aneg = sb("aneg", [128, 32])
        kb.op("act", lambda: nc.scalar.activation(out=aneg[:], in_=prm[:, 32:64], func=AF.Exp), [b_prm], [b_misc])
        kb.op("dve", lambda: nc.vector.tensor_scalar(out=aneg[:], in0=aneg[:], scalar1=-1.0, scalar2=None, op0=ALU.mult),
              [b_misc], [b_misc])

        NB = 8
        ps = [E(nc.psum_tensor("ps%d" % i, [128, 512], F32)) for i in range(NB)]
        b_ps = [Buf("ps%d" % i) for i in range(NB)]
        psi = [0]

        def bank():
            i = psi[0]
            psi[0] = (i + 1) % NB
            return ps[i], b_ps[i]

        with ExitStack() as esA:
            EA = esA.enter_context

            def sa(name, shape, dt=F32):
                return EA(nc.sbuf_tensor("a_" + name, list(shape), dt))

            wina = sa("wina", [128, 8, INA], BF16)
            wouta = sa("wouta", [128, 16, D], BF16)
            b_wina = Buf()
            b_wouta = Buf()
            for k in range(8):
                kb.dma("pool", wina[:, k, :], wina_d[k * 128:(k + 1) * 128, :], writes=[b_wina])
            for k in range(16):
                kb.dma("pool", wouta[:, k, :], wouta_d[k * 128:(k + 1) * 128, :], writes=[b_wouta])
            for k in range(8):
                kb.op("dve", lambda k=k: nc.vector.tensor_scalar(out=wina[:, k, :], in0=wina[:, k, :], scalar1=na[:, k:k + 1],
                                                               scalar2=None, op0=ALU.mult), [b_wina, b_prm], [b_wina])
            for k in range(16):
                kb.op("dve", lambda k=k: nc.vector.tensor_scalar(out=wouta[:, k, :], in0=wouta[:, k, :], scalar1=gn[:, k:k + 1],
                                                               scalar2=None, op0=ALU.mult), [b_wouta, b_prm], [b_wouta])

            xt = sa("xt", [128, D])
            xsb = sa("xsb", [128, D], BF16)
            xnT = sa("xnT", [128, 8, 128], BF16)
            zs = sa("zs", [128, DI], BF16)
            xpad = sa("xpad", [128, 24, 131])
            halo = sa("halo", [128, 24, 3])
            acc = sa("acc", [128, 12, 128])
            xc = sa("xc", [128, 24, 128], BF16)
            xstok = sa("xstok", [128, 32, 64], BF16)
            btok = sa("btok", [128, 4, 128], BF16)
            xdt = sa("xdt", [128, 32, 64], BF16)
            xdtd = sa("xdtd", [128, 8, 64], BF16)
            sm = sa("sm", [128, 16])
            dt = sa("dt", [128, 32])
            dta = sa("dta", [128, 32])
            acs = sa("acs", [128, 32])
            eacs = sa("eacs", [128, 32])
            decs = sa("decs", [128, 32])
            cd = sa("cd", [128, 32])
            vmask = sa("vmask", [128, 2])
            Zg2 = [sa("Zg%d" % i, [128, 8, 128]) for i in range(2)]
            Eg2 = [sa("Eg%d" % i, [128, 8, 128], BF16) for i in range(2)]
            Gg2 = [sa("Gg%d" % i, [128, 8, 128], BF16) for i in range(2)]
            cbm = sa("cbm", [128, 4, 128])
            yg = sa("yg", [128, 512])
            tg = sa("tg", [128, 512])
            ygb = sa("ygb", [128, DI], BF16)
            ygT = sa("ygT", [128, 16, 128], BF16)
            S = sa("S", [128, DI])
            Sb = sa("Sb", [128, DI], BF16)
            B = {n: Buf(n) for n in ["xt", "junk", "xsb", "xnT", "zs", "xpad", "halo", "acc", "xc", "xstok", "btok", "xdt",
                                     "xdtd", "sm", "dt", "dta", "acs", "eacs", "decs", "cd", "vmask", "Zg0", "Eg0", "Gg0", "Zg1", "Eg1", "Gg1", "cbm",
                                     "yg", "tg", "ygb", "ygT", "S", "Sb"]}
            kb.op("dve", lambda: nc.vector.memset(vmask[:], 0.0), [], [B["vmask"]])
            kb.op("dve", lambda: nc.vector.memset(vmask[:, 0:1], 1.0), [], [B["vmask"]])
            kb.op("dve", lambda: nc.vector.memset(vmask[0:1, 1:2], 1.0), [], [B["vmask"]])

            def rstd_from(ssq_ap, out_ap, n, bufs):
                act_fence(bufs)
                kb.op("act", lambda: nc.scalar.activation(out=out_ap, in_=ssq_ap, func=AF.Ln, scale=1.0 / n, bias=epsc[:, 0:1]), bufs + [b_misc], bufs)
                kb.op("act", lambda: nc.scalar.activation(out=out_ap, in_=out_ap, func=AF.Exp, scale=-0.5), bufs, bufs)

            def chunk(seq, c, first, last):
                is_s = seq > 0
                vm = vmask[:, 1:2] if is_s else vmask[:, 0:1]
                if is_s:
                    kb.op("dve", lambda: nc.vector.memset(xt[:], 0.0), [], [B["xt"]])
                    kb.dma("sp", xt[0:1, :], xs_d[seq - 1:seq, :], writes=[B["xt"]])
                else:
                    kb.dma("sp", xt[:], x_d[c * 128:(c + 1) * 128, :], writes=[B["xt"]])
                if first:
                    if is_s:
                        kb.dma("sp", halo[:], sconv_d[seq - 1], writes=[B["halo"]])
                        kb.dma("sp", S[:], sssm_d[seq - 1], writes=[B["S"]])
                    else:
                        kb.op("pool", lambda: nc.gpsimd.memset(halo[:], 0.0), [], [B["halo"]])
                        kb.op("pool", lambda: nc.gpsimd.memset(S[:], 0.0), [], [B["S"]])
                    kb.op("act", lambda: nc.scalar.copy(out=Sb[:], in_=S[:]), [B["S"]], [B["Sb"]])
                kb.op("act", lambda: nc.scalar.activation(out=acc[:].rearrange("p c t -> p (c t)")[:, 0:D], in_=xt[:], func=AF.Square, accum_out=sm[:, 0:1]),
                      [B["xt"]], [B["acc"], B["sm"]])
                rstd_from(sm[:, 0:1], sm[:, 1:2], D, [B["sm"]])
                kb.op("dve", lambda: nc.vector.tensor_scalar(out=xsb[:], in0=xt[:], scalar1=sm[:, 1:2], scalar2=None, op0=ALU.mult),
                      [B["xt"], B["sm"]], [B["xsb"]])
                p, bp = bank()
                pb = p[:].bitcast(BF16)
                for k in range(8):
                    kb.op("pe", lambda k=k: nc.tensor.transpose(out=pb[:, k * 128:(k + 1) * 128], in_=xsb[:, k * 128:(k + 1) * 128],
                                                              identity=identb[:]), [B["xsb"], b_misc], [bp])
                kb.op("act", lambda: nc.scalar.copy(out=xnT[:].rearrange("p k t -> p (k t)"), in_=pb), [bp], [B["xnT"]])
                for blk in range(4):
                    p, bp = bank()
                    for k in range(8):
                        kb.op("pe", lambda k=k, blk=blk, p=p: nc.tensor.matmul(p[:], lhsT=xnT[:, k, :], rhs=wina[:, k, blk * 512:(blk + 1) * 512],
                                                                           start=(k == 0), stop=(k == 7)), [B["xnT"], b_wina], [bp])
                    kb.op("act", lambda blk=blk, p=p: nc.scalar.activation(out=zs[:, blk * 512:(blk + 1) * 512], in_=p[:], func=AF.Silu),
                          [bp], [B["zs"]])
                kb.op("pool", lambda: nc.gpsimd.tensor_copy(out=xpad[:, :, 0:3], in_=halo[:]), [B["halo"]], [B["xpad"]])
                for q4 in range(6):
                    p, bp = bank()
                    for j in range(4):
                        ct = q4 * 4 + j
                        for k in range(8):
                            kb.op("pe", lambda k=k, ct=ct, j=j, p=p: nc.tensor.matmul(
                                p[:, j * 128:(j + 1) * 128], lhsT=wina[:, k, DI + ct * 128:DI + (ct + 1) * 128], rhs=xnT[:, k, :],
                                start=(k == 0), stop=(k == 7)), [B["xnT"], b_wina], [bp])
                    kb.op("act", lambda q4=q4, p=p: nc.scalar.copy(out=xpad[:, q4 * 4:(q4 + 1) * 4, 3:131],
                                                                  in_=p[:].rearrange("p (j t) -> p j t", j=4)), [bp], [B["xpad"]])
                if is_s:
                    kb.op("pool", lambda: nc.gpsimd.tensor_copy(out=halo[:], in_=xpad[:, :, 1:4]), [B["xpad"]], [B["halo"]])
                else:
                    kb.op("pool", lambda: nc.gpsimd.tensor_copy(out=halo[:], in_=xpad[:, :, 128:131]), [B["xpad"]], [B["halo"]])
                if last:
                    kb.dma("sp", convs_o[seq - 1] if is_s else convp_o[:, :, :], halo[:], reads=[B["halo"]])
                for hf in range(2):
                    for c12 in range(12):
                        ct = hf * 12 + c12
                        kb.op("act", lambda ct=ct, c12=c12: nc.scalar.activation(out=acc[:, c12, :], in_=xpad[:, ct, 3:131], func=AF.Identity,
                                                                               scale=convw[:, ct, 3:4], bias=convb[:, ct:ct + 1]),
                              [B["xpad"], b_prm], [B["acc"]])
                    for c12 in range(12):
                        ct = hf * 12 + c12
                        for kk in range(3):
                            kb.op("dve", lambda ct=ct, c12=c12, kk=kk: nc.vector.scalar_tensor_tensor(
                                out=acc[:, c12, :], in0=xpad[:, ct, kk:kk + 128], scalar=convw[:, ct, kk:kk + 1], in1=acc[:, c12, :],
                                op0=ALU.mult, op1=ALU.add), [B["xpad"], B["acc"], b_prm], [B["acc"]])
                    kb.op("act", lambda hf=hf: nc.scalar.activation(out=xc[:, hf * 12:(hf + 1) * 12, :].rearrange("p c t -> p (c t)"),
                                                                   in_=acc[:].rearrange("p c t -> p (c t)"), func=AF.Silu), [B["acc"]], [B["xc"]])
                p, bp = bank()
                for k in range(8):
                    kb.op("pe", lambda k=k, p=p: nc.tensor.matmul(p[:, 0:32], lhsT=xnT[:, k, :], rhs=wina[:, k, DI + CONV:INA],
                                                               start=(k == 0), stop=(k == 7)), [B["xnT"], b_wina], [bp])
                kb.op("dve", lambda p=p: nc.vector.tensor_tensor(out=dt[:], in0=p[:, 0:32], in1=dtb, op=ALU.add), [bp, b_prm], [B["dt"]])
                kb.op("act", lambda: nc.scalar.activation(out=dt[:], in_=dt[:], func=AF.Exp), [B["dt"]], [B["dt"]])
                kb.op("act", lambda: nc.scalar.activation(out=dt[:], in_=dt[:], func=AF.Ln, bias=1.0), [B["dt"]], [B["dt"]])
                kb.op("dve", lambda: nc.vector.tensor_scalar(out=dt[:], in0=dt[:], scalar1=vm, scalar2=None, op0=ALU.mult),
                      [B["dt"], B["vmask"]], [B["dt"]])
                kb.op("dve", lambda: nc.vector.tensor_tensor(out=dta[:], in0=dt[:], in1=aneg[:], op=ALU.mult), [B["dt"], b_misc], [B["dta"]])
                for half in range(2):
                    p, bp = bank()
                    pb = p[:].bitcast(BF16)
                    for i in range(8):
                        ct = half * 8 + i
                        kb.op("pe", lambda i=i, ct=ct, pb=pb: nc.tensor.transpose(out=pb[:, i * 128:(i + 1) * 128], in_=xc[:, ct, :],
                                                                                identity=identb[:]), [B["xc"], b_misc], [bp])
                    kb.op("act", lambda half=half, pb=pb: nc.scalar.copy(
                        out=xstok[:, half * 16:(half + 1) * 16, :].rearrange("p r d -> p (r d)"), in_=pb), [bp], [B["xstok"]])
                p, bp = bank()
                pb = p[:].bitcast(BF16)
                for g in range(4):
                    kb.op("pe", lambda g=g, pb=pb: nc.tensor.transpose(out=pb[:, g * 128:(g + 1) * 128], in_=xc[:, 16 + g, :],
                                                                     identity=identb[:]), [B["xc"], b_misc], [bp])
                kb.op("act", lambda pb=pb: nc.scalar.copy(out=btok[:].rearrange("p g n -> p (g n)"), in_=pb[:, 0:512]), [bp], [B["btok"]])
                kb.op("dve", lambda: nc.vector.tensor_tensor(out=xdt[:], in0=xstok[:], in1=dt[:].unsqueeze(2).to_broadcast([128, 32, 64]),
                                                             op=ALU.mult), [B["xstok"], B["dt"]], [B["xdt"]])
                p, bp = bank()
                kb.op("pe", lambda p=p: nc.tensor.matmul(p[:, 0:32], lhsT=tri, rhs=dta[:], start=True, stop=True), [b_cst, B["dta"]], [bp])
                kb.op("pe", lambda p=p: nc.tensor.matmul(p[:, 32:64], lhsT=onesf[:], rhs=dta[:], start=True, stop=True),
                      [b_misc, B["dta"]], [bp])
                kb.op("act", lambda p=p: nc.scalar.copy(out=acs[:], in_=p[:, 0:32]), [bp], [B["acs"]])
                kb.op("act", lambda p=p: nc.scalar.activation(out=eacs[:], in_=p[:, 0:32], func=AF.Exp), [bp], [B["eacs"]])
                kb.op("act", lambda p=p: nc.scalar.activation(out=cd[:], in_=p[:, 32:64], func=AF.Exp), [bp], [B["cd"]])
                kb.op("dve", lambda p=p: nc.vector.tensor_tensor(out=decs[:], in0=p[:, 32:64], in1=acs[:], op=ALU.subtract),
                      [bp, B["acs"]], [B["decs"]])
                kb.op("act", lambda: nc.scalar.activation(out=decs[:], in_=decs[:], func=AF.Exp), [B["decs"]], [B["decs"]])
                p, bp = bank()
                for g in range(4):
                    kb.op("pe", lambda g=g, p=p: nc.tensor.matmul(p[:, g * 128:(g + 1) * 128], lhsT=xc[:, 16 + g, :], rhs=xc[:, 20 + g, :],
                                                               start=True, stop=True), [B["xc"]], [bp])
                kb.op("dve", lambda p=p: nc.vector.tensor_tensor(out=cbm[:], in0=p[:].rearrange("p (g l) -> p g l", g=4),
                                                                in1=tri.unsqueeze(1).to_broadcast([128, 4, 128]), op=ALU.mult),
                      [bp, b_cst], [B["cbm"]])
                for g in range(4):
                    Zg, Eg, Gg = Zg2[g % 2], Eg2[g % 2], Gg2[g % 2]
                    zk, ek, gk = "Zg%d" % (g % 2), "Eg%d" % (g % 2), "Gg%d" % (g % 2)
                    kb.op("pool", lambda g=g, Zg=Zg: nc.gpsimd.tensor_tensor(
                        out=Zg[:], in0=tri.unsqueeze(1).to_broadcast([128, 8, 128]),
                        in1=dta[:, g * 8:(g + 1) * 8].unsqueeze(2).to_broadcast([128, 8, 128]), op=ALU.mult),
                        [b_cst, B["dta"]], [B[zk]])
                    for hh in range(2):
                        p, bp = bank()
                        kb.op("pe", lambda hh=hh, p=p, Zg=Zg: nc.tensor.matmul(p[:], lhsT=astr, rhs=Zg[:, hh * 4:(hh + 1) * 4, :].rearrange("p r l -> p (r l)"),
                                                                     start=True, stop=True), [b_cst, B[zk]], [bp])
                        kb.op("act", lambda hh=hh, p=p, Eg=Eg: nc.scalar.activation(out=Eg[:, hh * 4:(hh + 1) * 4, :].rearrange("p r l -> p (r l)"),
                                                                            in_=p[:], func=AF.Exp), [bp], [B[ek]])
                    kb.op("dve", lambda g=g, Gg=Gg, Eg=Eg: nc.vector.tensor_tensor(out=Gg[:], in0=Eg[:], in1=cbm[:, g, :].unsqueeze(1).to_broadcast([128, 8, 128]),
                                                                     op=ALU.mult), [B[ek], B["cbm"]], [B[gk]])
                    pA, bA = bank()
                    for r in range(8):
                        h = g * 8 + r
                        kb.op("pe", lambda r=r, h=h, pA=pA, Gg=Gg: nc.tensor.matmul(pA[:, r * 64:(r + 1) * 64], lhsT=Gg[:, r, :], rhs=xdt[:, h, :],
                                                                          start=True, stop=True), [B[gk], B["xdt"]], [bA])
                    pB, bB = bank()
                    kb.op("pe", lambda g=g, pB=pB: nc.tensor.matmul(pB[:], lhsT=xc[:, 20 + g, :], rhs=Sb[:, g * 512:(g + 1) * 512],
                                                                 start=True, stop=True), [B["xc"], B["Sb"]], [bB])
                    kb.op("dve", lambda g=g, pB=pB: nc.vector.tensor_tensor(
                        out=tg[:].rearrange("p (r d) -> p r d", r=8), in0=pB[:].rearrange("p (r d) -> p r d", r=8),
                        in1=eacs[:, g * 8:(g + 1) * 8].unsqueeze(2).to_broadcast([128, 8, 64]), op=ALU.mult), [bB, B["eacs"]], [B["tg"]])
                    kb.op("dve", lambda pA=pA: nc.vector.tensor_tensor(out=yg[:], in0=pA[:], in1=tg[:], op=ALU.add), [bA, B["tg"]], [B["yg"]])
                    kb.op("pool", lambda g=g: nc.gpsimd.tensor_tensor(
                        out=tg[:].rearrange("p (r d) -> p r d", r=8), in0=xstok[:, g * 8:(g + 1) * 8, :],
                        in1=dsk[:, g * 8:(g + 1) * 8].unsqueeze(2).to_broadcast([128, 8, 64]), op=ALU.mult),
                        [B["xstok"], b_prm, B["yg"]], [B["tg"]])
                    kb.op("dve", lambda: nc.vector.tensor_tensor(out=yg[:], in0=yg[:], in1=tg[:], op=ALU.add), [B["tg"], B["yg"]], [B["yg"]])
                    kb.op("dve", lambda g=g: nc.vector.tensor_tensor(out=yg[:], in0=yg[:], in1=zs[:, g * 512:(g + 1) * 512], op=ALU.mult),
                          [B["zs"], B["yg"]], [B["yg"]])
                    kb.op("act", lambda g=g: nc.scalar.activation(out=tg[:], in_=yg[:], func=AF.Square, accum_out=sm[:, 4 + g:5 + g]),
                          [B["yg"]], [B["tg"], B["sm"]])
                    rstd_from(sm[:, 4 + g:5 + g], sm[:, 8 + g:9 + g], 512, [B["sm"]])
                    kb.op("dve", lambda g=g: nc.vector.tensor_scalar(out=ygb[:, g * 512:(g + 1) * 512], in0=yg[:], scalar1=sm[:, 8 + g:9 + g],
                                                                   scalar2=None, op0=ALU.mult), [B["yg"], B["sm"]], [B["ygb"]])
                for g in range(4):
                    kb.op("dve", lambda g=g: nc.vector.tensor_tensor(out=xdtd[:], in0=xdt[:, g * 8:(g + 1) * 8, :],
                                                                     in1=decs[:, g * 8:(g + 1) * 8].unsqueeze(2).to_broadcast([128, 8, 64]),
                                                                     op=ALU.mult), [B["xdt"], B["decs"]], [B["xdtd"]])
                    p, bp = bank()
                    kb.op("pe", lambda g=g, p=p: nc.tensor.matmul(p[:], lhsT=btok[:, g, :], rhs=xdtd[:].rearrange("p r d -> p (r d)"),
                                                               start=True, stop=True), [B["btok"], B["xdtd"]], [bp])
                    kb.op("dve", lambda g=g: nc.vector.tensor_tensor(
                        out=S[:, g * 512:(g + 1) * 512].rearrange("p (r d) -> p r d", r=8),
                        in0=S[:, g * 512:(g + 1) * 512].rearrange("p (r d) -> p r d", r=8),
                        in1=cd[:, g * 8:(g + 1) * 8].unsqueeze(2).to_broadcast([128, 8, 64]), op=ALU.mult), [B["S"], B["cd"], B["Sb"]], [B["S"]])
                    kb.op("dve", lambda g=g, p=p: nc.vector.tensor_tensor(out=S[:, g * 512:(g + 1) * 512], in0=S[:, g * 512:(g + 1) * 512],
                                                                        in1=p[:], op=ALU.add), [bp, B["S"]], [B["S"]])
                kb.op("act", lambda: nc.scalar.copy(out=Sb[:], in_=S[:]), [B["S"]], [B["Sb"]])
                if last:
                    kb.dma("sp", ssms_o[seq - 1] if is_s else ssmp_o[:, :], S[:], reads=[B["S"]])
                for half in range(2):
                    p, bp = bank()
                    pb = p[:].bitcast(BF16)
                    for i in range(8):
                        kk = half * 8 + i
                        kb.op("pe", lambda i=i, kk=kk, pb=pb: nc.tensor.transpose(out=pb[:, i * 128:(i + 1) * 128], in_=ygb[:, kk * 128:(kk + 1) * 128],
                                                                                identity=identb[:]), [B["ygb"], b_misc], [bp])
                    kb.op("act", lambda half=half, pb=pb: nc.scalar.copy(out=ygT[:, half * 8:(half + 1) * 8, :].rearrange("p k t -> p (k t)"), in_=pb),
                          [bp], [B["ygT"]])
                for blk in range(2):
                    p, bp = bank()
                    for k in range(16):
                        kb.op("pe", lambda k=k, blk=blk, p=p: nc.tensor.matmul(p[:], lhsT=ygT[:, k, :], rhs=wouta[:, k, blk * 512:(blk + 1) * 512],
                                                                           start=(k == 0), stop=(k == 15)), [B["ygT"], b_wouta], [bp])
                    kb.op("dve", lambda blk=blk, p=p: nc.vector.tensor_tensor(out=xt[:, blk * 512:(blk + 1) * 512], in0=xt[:, blk * 512:(blk + 1) * 512],
                                                                            in1=p[:], op=ALU.add), [bp, B["xt"]], [B["xt"]])
                idx = c if not is_s else NCH + seq - 1
                kb.dma("sp", h1_d[idx], xt[:], reads=[B["xt"]])

            for c in range(NCH):
                chunk(0, c, c == 0, c == NCH - 1)
            for s in range(NS):
                chunk(1 + s, 0, True, True)
            kb.barrier()
        import os
        if os.environ.get("DBG_STOP") == "A":
            return nc

        esR = ExitStack()
        KT_all = esR.enter_context(nc.sbuf_tensor("r_KT", [128, NCH, 8, 128], BF16))
        V_all = esR.enter_context(nc.sbuf_tensor("r_V", [128, NCH, D], BF16))
        ksqr = esR.enter_context(nc.sbuf_tensor("r_ksq", [128, 16], F32))
        b_KT = [Buf() for _ in range(NCH)]
        b_V = [Buf() for _ in range(NCH)]
        b_ksqr = Buf()
        kb.op("dve", lambda: nc.vector.memset(ksqr[:], 0.0), [], [b_ksqr])
        with ExitStack() as esB:
            EB = esB.enter_context

            def sbb(name, shape, dt=F32):
                return EB(nc.sbuf_tensor("b_" + name, list(shape), dt))

            wkv = sbb("wkv", [128, 8, 2 * D], BF16)
            b_wkv = Buf()
            for k in range(8):
                kb.dma("pool", wkv[:, k, :], wkv_d[k * 128:(k + 1) * 128, :], writes=[b_wkv])
            for k in range(8):
                kb.op("dve", lambda k=k: nc.vector.tensor_scalar(out=wkv[:, k, :], in0=wkv[:, k, :], scalar1=nkv[:, k:k + 1],
                                                               scalar2=None, op0=ALU.mult), [b_wkv, b_prm], [b_wkv])
            ht = sbb("ht", [128, D])
            junk2 = sbb("junk2", [128, D])
            hsb = sbb("hsb", [128, D], BF16)
            hnT = sbb("hnT", [128, 8, 128], BF16)
            kt = sbb("kt", [128, 16, 64])
            vt = sbb("vt", [128, D])
            cs = sbb("cs", [128, 16])
            rt = sbb("rt", [128, 4, 16, 8])
            sm2 = sbb("sm2", [128, 4])
            zt = sbb("zt", [128, D])
            ktb = sbb("ktb", [128, D], BF16)
            ksq = sbb("ksq", [128, 16])
            BB = {n: Buf(n) for n in ["ht", "junk2", "hsb", "hnT", "kt", "vt", "cs", "rt", "sm2", "zt", "ktb", "ksq"]}
            kb.op("dve", lambda: nc.vector.memset(zt[:], 0.0), [], [BB["zt"]])

            def kvchunk(idx, is_s, c):
                kb.dma("sp", ht[:], h1_d[idx], writes=[BB["ht"]])
                if is_s:
                    kb.dma("sp", cs[:, 0:8], coss_d[:, :], writes=[BB["cs"]])
                    kb.dma("sp", cs[:, 8:16], sins_d[:, :], writes=[BB["cs"]])
                else:
                    kb.dma("sp", cs[:, 0:8], cosp_d[c * 128:(c + 1) * 128, :], writes=[BB["cs"]])
                    kb.dma("sp", cs[:, 8:16], sinp_d[c * 128:(c + 1) * 128, :], writes=[BB["cs"]])
                kb.op("act", lambda: nc.scalar.activation(out=junk2[:], in_=ht[:], func=AF.Square, accum_out=sm2[:, 0:1]),
                      [BB["ht"]], [BB["junk2"], BB["sm2"]])
                act_fence([BB["sm2"]])
                kb.op("act", lambda: nc.scalar.activation(out=sm2[:, 1:2], in_=sm2[:, 0:1], func=AF.Ln, scale=1.0 / D, bias=epsc[:, 0:1]),
                      [BB["sm2"], b_misc], [BB["sm2"]])
                kb.op("act", lambda: nc.scalar.activation(out=sm2[:, 1:2], in_=sm2[:, 1:2], func=AF.Exp, scale=-0.5), [BB["sm2"]], [BB["sm2"]])
                kb.op("dve", lambda: nc.vector.tensor_scalar(out=hsb[:], in0=ht[:], scalar1=sm2[:, 1:2], scalar2=None, op0=ALU.mult),
                      [BB["ht"], BB["sm2"]], [BB["hsb"]])
                p, bp = bank()
                pb = p[:].bitcast(BF16)
                for k in range(8):
                    kb.op("pe", lambda k=k, pb=pb: nc.tensor.transpose(out=pb[:, k * 128:(k + 1) * 128], in_=hsb[:, k * 128:(k + 1) * 128],
                                                                     identity=identb[:]), [BB["hsb"], b_misc], [bp])
                kb.op("act", lambda pb=pb: nc.scalar.copy(out=hnT[:].rearrange("p k t -> p (k t)"), in_=pb), [bp], [BB["hnT"]])
                for blk in range(4):
                    p, bp = bank()
                    for k in range(8):
                        kb.op("pe", lambda k=k, blk=blk, p=p: nc.tensor.matmul(p[:], lhsT=hnT[:, k, :], rhs=wkv[:, k, blk * 512:(blk + 1) * 512],
                                                                           start=(k == 0), stop=(k == 7)), [BB["hnT"], b_wkv], [bp])
                    if blk < 2:
                        kb.op("act", lambda blk=blk, p=p: nc.scalar.copy(out=kt[:, blk * 8:(blk + 1) * 8, :].rearrange("p a d -> p (a d)"), in_=p[:]),
                              [bp], [BB["kt"]])
                    else:
                        kb.op("act", lambda blk=blk, p=p: nc.scalar.copy(out=vt[:, (blk - 2) * 512:(blk - 1) * 512], in_=p[:]), [bp], [BB["vt"]])
                cosb = cs[:, 0:8].unsqueeze(1).to_broadcast([128, 16, 8])
                sinb = cs[:, 8:16].unsqueeze(1).to_broadcast([128, 16, 8])
                x1 = kt[:, :, 0:8]
                x2 = kt[:, :, 8:16]
                kb.op("dve", lambda: nc.vector.tensor_tensor(out=rt[:, 0], in0=x1, in1=cosb, op=ALU.mult), [BB["kt"], BB["cs"]], [BB["rt"]])
                kb.op("dve", lambda: nc.vector.tensor_tensor(out=rt[:, 1], in0=x2, in1=sinb, op=ALU.mult), [BB["kt"], BB["cs"]], [BB["rt"]])
                kb.op("dve", lambda: nc.vector.tensor_tensor(out=rt[:, 2], in0=x2, in1=cosb, op=ALU.mult), [BB["kt"], BB["cs"]], [BB["rt"]])
                kb.op("dve", lambda: nc.vector.tensor_tensor(out=rt[:, 3], in0=x1, in1=sinb, op=ALU.mult), [BB["kt"], BB["cs"]], [BB["rt"]])
                kb.op("dve", lambda: nc.vector.tensor_tensor(out=x1, in0=rt[:, 0], in1=rt[:, 1], op=ALU.subtract), [BB["rt"]], [BB["kt"]])
                kb.op("dve", lambda: nc.vector.tensor_tensor(out=x2, in0=rt[:, 2], in1=rt[:, 3], op=ALU.add), [BB["rt"]], [BB["kt"]])
                ktf = kt[:].rearrange("p a d -> p (a d)")
                if is_s:
                    s = idx - NCH
                    kb.dma("sp", ks_o[s:s + 1, :], ktf[0:1, :], reads=[BB["kt"]])
                    kb.dma("sp", vs_o[s:s + 1, :], vt[0:1, :], reads=[BB["vt"]])
                else:
                    kb.dma("sp", kp_o[c * 128:(c + 1) * 128, :], ktf, reads=[BB["kt"]])
                    kb.dma("sp", vp_o[c * 128:(c + 1) * 128, :], vt[:], reads=[BB["vt"]])
                    kb.op("pool", lambda: nc.gpsimd.tensor_copy(out=V_all[:, c, :], in_=vt[:]), [BB["vt"]], [b_V[c]])
                    kb.op("act", lambda: nc.scalar.copy(out=ktb[:], in_=ktf), [BB["kt"]], [BB["ktb"]])
                    kb.op("dve", lambda: nc.vector.tensor_tensor(out=junk2[:], in0=ktf, in1=ktf, op=ALU.mult), [BB["kt"]], [BB["junk2"]])
                    kb.op("dve", lambda: nc.vector.tensor_reduce(out=ksq[:], in_=junk2[:].rearrange("p (a d) -> p a d", a=16), axis=AX.X, op=ALU.add),
                          [BB["junk2"]], [BB["ksq"]])
                    kb.op("dve", lambda: nc.vector.tensor_tensor(out=ksqr[:], in0=ksqr[:], in1=ksq[:], op=ALU.max), [BB["ksq"], b_ksqr], [b_ksqr])
                    p, bp = bank()
                    pb = p[:].bitcast(BF16)
                    for h in range(8):
                        kb.op("pe", lambda h=h, pb=pb: nc.tensor.transpose(out=pb[:, h * 128:(h + 1) * 128], in_=ktb[:, h * 128:(h + 1) * 128],
                                                                         identity=identb[:]), [BB["ktb"], b_misc], [bp])
                    kb.op("act", lambda pb=pb: nc.scalar.copy(out=KT_all[:, c, :, :].rearrange("p h t -> p (h t)"), in_=pb), [bp], [b_KT[c]])

            for c in range(NCH):
                kvchunk(c, False, c)
            for s in range(NS):
                kvchunk(NCH + s, True, 0)
            kb.barrier()

        if os.environ.get("DBG_STOP") == "B":
            return nc
        LAM_INIT = 0.8 - 0.6 * float(np.exp(-0.3 * 1))
        def phaseC(mode):
          is_sm = mode == "sample"
          with ExitStack() as esC:
            EC = esC.enter_context

            def sc(name, shape, dt=F32):
                return EC(nc.sbuf_tensor(("d_" if is_sm else "c_") + name, list(shape), dt))

            winb = sc("winb", [128, 8, 2 * D], BF16)
            woutb = sc("woutb", [128, 8, D], BF16)
            nbt = sc("nbt", [128, 8])
            b_wb = Buf()
            kb.dma("sp", nbt[:], nb_d[:, :], writes=[b_wb])
            for k in range(8):
                kb.dma("pool", winb[:, k, :], winb_d[k * 128:(k + 1) * 128, :], writes=[b_wb])
            for k in range(8):
                kb.dma("pool", woutb[:, k, :], woutb_d[k * 128:(k + 1) * 128, :], writes=[b_wb])
            for k in range(8):
                kb.op("dve", lambda k=k: nc.vector.tensor_scalar(out=winb[:, k, :], in0=winb[:, k, :], scalar1=nbt[:, k:k + 1],
                                                               scalar2=None, op0=ALU.mult), [b_wb], [b_wb])
            lamt = sc("lamt", [128, 4, 64])
            lsm = sc("lsm", [128, 8])
            sgb = sc("sgb", [128, 128])
            nfb = sc("nfb", [128, D])
            onesb = sc("onesb", [128, 2], BF16)
            b_c0 = Buf()
            for i in range(4):
                kb.dma("sp", lamt[:, i, :], lam_d[i].partition_broadcast(128), writes=[b_c0])
            kb.dma("sp", sgb[:], subln_d.partition_broadcast(128), writes=[b_c0])
            kb.dma("sp", nfb[:], nf_d.partition_broadcast(128), writes=[b_c0])
            kb.op("dve", lambda: nc.vector.memset(onesb[:], 1.0), [], [b_c0])
            kb.op("dve", lambda: nc.vector.tensor_tensor(out=lamt[:, 0, :], in0=lamt[:, 0, :], in1=lamt[:, 1, :], op=ALU.mult), [b_c0], [b_c0])
            kb.op("dve", lambda: nc.vector.tensor_tensor(out=lamt[:, 2, :], in0=lamt[:, 2, :], in1=lamt[:, 3, :], op=ALU.mult), [b_c0], [b_c0])
            kb.op("dve", lambda: nc.vector.tensor_reduce(out=lsm[:, 0:1], in_=lamt[:, 0, :], axis=AX.X, op=ALU.add), [b_c0], [b_c0])
            kb.op("dve", lambda: nc.vector.tensor_reduce(out=lsm[:, 1:2], in_=lamt[:, 2, :], axis=AX.X, op=ALU.add), [b_c0], [b_c0])
            kb.op("act", lambda: nc.scalar.activation(out=lsm[:, 2:4], in_=lsm[:, 0:2], func=AF.Exp), [b_c0], [b_c0])
            kb.op("dve", lambda: nc.vector.tensor_tensor(out=lsm[:, 4:5], in0=lsm[:, 3:4], in1=lsm[:, 2:3], op=ALU.subtract), [b_c0], [b_c0])
            kb.op("dve", lambda: nc.vector.tensor_scalar(out=lsm[:, 4:5], in0=lsm[:, 4:5], scalar1=-LAM_INIT, scalar2=None, op0=ALU.add), [b_c0], [b_c0])
            kb.op("dve", lambda: nc.vector.tensor_scalar(out=sgb[:], in0=sgb[:], scalar1=1.0 - LAM_INIT, scalar2=None, op0=ALU.mult), [b_c0], [b_c0])
            neglam = lsm[:, 4:5]

            ht = sc("ht", [128, D])
            jk = sc("jk", [128, D])
            hsb = sc("hsb", [128, D], BF16)
            hnT = sc("hnT", [128, 8, 128], BF16)
            qt = sc("qt", [128, 16, 64])
            gate = sc("gate", [128, D])
            cs = sc("cs", [128, 16])
            rt = sc("rt", [128, 4, 16, 8])
            qb = sc("qb", [128, D], BF16)
            qT = sc("qT", [128, 8, 128], BF16)
            sq = sc("sq", [128, 64])
            cb16 = sc("cb16", [16, 24])
            dg = sc("dg", [16, 16])
            cbias = sc("cbias", [128, 16])
            NPB = 3
            Pb = [sc("P%d" % i, [128, 512], BF16) for i in range(NPB)]
            b_P = [Buf() for _ in range(NPB)]
            pidx = [0]
            osb = sc("osb", [128, 8, 128])
            t1 = sc("t1", [128, 8, 128])
            ob = sc("ob", [128, D], BF16)
            oT = sc("oT", [128, 8, 128], BF16)
            CB = {n: Buf(n) for n in ["ht", "jk", "hsb", "hnT", "qt", "gate", "cs", "rt", "qb", "qT", "sq", "cb16", "dg", "cbias",
                                      "osb", "t1", "ob", "oT"]}
            NAMES = ["ht", "jk", "hsb", "hnT", "qt", "gate", "cs", "rt", "qb", "qT", "sq", "cb16", "dg", "cbias", "osb", "t1", "ob", "oT"]
            SET0 = dict(zip(NAMES, [ht, jk, hsb, hnT, qt, gate, cs, rt, qb, qT, sq, cb16, dg, cbias, osb, t1, ob, oT]))
            if is_sm:
                SETS = [SET0, SET0]
                CBS = [CB, CB]
            else:
                SHP = dict(ht=([128, D], F32), jk=([128, D], F32), hsb=([128, D], BF16), hnT=([128, 8, 128], BF16), qt=([128, 16, 64], F32),
                           gate=([128, D], F32), cs=([128, 16], F32), rt=([128, 4, 16, 8], F32), qb=([128, D], BF16), qT=([128, 8, 128], BF16),
                           sq=([128, 64], F32), cb16=([16, 24], F32), dg=([16, 16], F32), cbias=([128, 16], F32), osb=([128, 8, 128], F32),
                           t1=([128, 8, 128], F32), ob=([128, D], BF16), oT=([128, 8, 128], BF16))
                SET1 = {n_: sc(n_ + "_2", SHP[n_][0], SHP[n_][1]) for n_ in NAMES}
                SETS = [SET0, SET1]
                CBS = [CB, {n_: Buf(n_) for n_ in NAMES}]
            rot = [5]

            def rbank():
                i = rot[0]
                rot[0] = 5 + (i - 5 + 1) % 3
                return ps[i], b_ps[i]

            if not is_sm:
                p, bp = rbank()
                kb.op("pe", lambda p=p: nc.tensor.transpose(out=p[0:16, 0:128], in_=ksqr[:], identity=identf), [b_ksqr, b_cst], [bp])
                for par_ in range(2):
                    kb.op("dve", lambda p=p, par_=par_: nc.vector.tensor_reduce(out=SETS[par_]["cb16"][:, 0:1], in_=p[0:16, 0:128], axis=AX.X, op=ALU.max),
                          [bp], [CBS[par_]["cb16"]])
            else:
                I32 = mybir.dt.int32
                c2 = sc("c2", [128, 652])
                b_c2 = Buf()
                kb.dma("sp", c2[:], cst2_d[:, :], writes=[b_c2])
                E40R = c2[:, 0:640].rearrange("p (h c) -> p h c", h=8)
                M40 = c2[0:40, 640:648]
                iotac = c2[:, 648:649]
                A40 = c2[0:40, 649:650]
                B40 = c2[0:40, 650:651]
                pti = sc("pti", [128, NS * NP], I32)
                ptf = sc("ptf", [128, NS * NP])
                idx = sc("idx", [128, NS * NP], I32)
                b_idx = Buf()
                kb.dma("sp", pti[:], pt_d.partition_broadcast(128), writes=[b_idx])
                kb.op("dve", lambda: nc.vector.tensor_copy(out=ptf[:], in_=pti[:]), [b_idx], [b_idx])
                kb.op("dve", lambda: nc.vector.tensor_scalar(out=ptf[:], in0=ptf[:], scalar1=128.0, scalar2=iotac, op0=ALU.mult, op1=ALU.add),
                      [b_idx, b_c2], [b_idx])
                kb.op("dve", lambda: nc.vector.tensor_copy(out=idx[:], in_=ptf[:]), [b_idx], [b_idx])
                lamcol = sc("lamcol", [40, 1])
                kb.op("dve", lambda: nc.vector.scalar_tensor_tensor(out=lamcol[:], in0=B40, scalar=lsm[0:40, 4:5], in1=A40, op0=ALU.mult, op1=ALU.add),
                      [b_c2, b_c0], [b_c2])
                qcol = sc("qcol", [128, 8])
                QB = sc("QB", [128, 8, 80])
                Sall = sc("Sall", [40, NT * 128])
                PnT = sc("PnT", [128, NT, 40])
                GP = 2
                NPG = 4
                pg = [sc("pg%d" % i, [128, GP, D]) for i in range(NPG)]
                b_pg = [[Buf() for _ in range(GP)] for _ in range(NPG)]
                pgb = [sc("pgb%d" % i, [128, GP, D], BF16) for i in range(2)]
                b_pgb = [Buf(), Buf()]
                QBb = sc("QBb", [128, 8, 80], BF16)
                KTs = sc("KTs", [128, 8, 128])
                Vs = sc("Vs", [128, D])
                tmp40 = sc("tmp40", [40, D])
                st40 = sc("st40", [40, 8])
                SB = {n: Buf(n) for n in ["qcol", "QB", "QBb", "Sall", "PnT", "KTs", "Vs", "tmp40", "st40"]}

            def rstd_c(ssq_ap, out_ap, n, bufs):
                act_fence(bufs)
                kb.op("act", lambda: nc.scalar.activation(out=out_ap, in_=ssq_ap, func=AF.Ln, scale=1.0 / n, bias=epsc[:, 0:1]), bufs + [b_misc], bufs)
                kb.op("act", lambda: nc.scalar.activation(out=out_ap, in_=out_ap, func=AF.Exp, scale=-0.5), bufs, bufs)

            def part1(i):
                S_ = SETS[i % 2]
                CB = CBS[i % 2]
                ht, jk, hsb, hnT, qt, gate, cs, rt, qb, qT, sq, cb16, dg, cbias, osb, t1, ob, oT = [S_[n_] for n_ in NAMES]
                kb.dma("sp", ht[:], h1_d[(NCH + i) if is_sm else i], writes=[CB["ht"]])
                if is_sm:
                    kb.dma("sp", cs[:, 0:8], coss_d[:, :], writes=[CB["cs"]])
                    kb.dma("sp", cs[:, 8:16], sins_d[:, :], writes=[CB["cs"]])
                else:
                    kb.dma("sp", cs[:, 0:8], cosp_d[i * 128:(i + 1) * 128, :], writes=[CB["cs"]])
                    kb.dma("sp", cs[:, 8:16], sinp_d[i * 128:(i + 1) * 128, :], writes=[CB["cs"]])
                yield
                kb.op("act", lambda: nc.scalar.activation(out=jk[:], in_=ht[:], func=AF.Square, accum_out=sq[:, 0:1]), [CB["ht"]], [CB["jk"], CB["sq"]])
                rstd_c(sq[:, 0:1], sq[:, 1:2], D, [CB["sq"]])
                yield
                kb.op("dve", lambda: nc.vector.tensor_scalar(out=hsb[:], in0=ht[:], scalar1=sq[:, 1:2], scalar2=None, op0=ALU.mult),
                      [CB["ht"], CB["sq"]], [CB["hsb"]])
                p, bp = rbank()
                pb = p[:].bitcast(BF16)
                for k in range(8):
                    kb.op("pe", lambda k=k, pb=pb: nc.tensor.transpose(out=pb[:, k * 128:(k + 1) * 128], in_=hsb[:, k * 128:(k + 1) * 128],
                                                                     identity=identb[:]), [CB["hsb"], b_misc], [bp])
                kb.op("act", lambda pb=pb: nc.scalar.copy(out=hnT[:].rearrange("p k t -> p (k t)"), in_=pb), [bp], [CB["hnT"]])
                yield
                for blk in range(4):
                    yield
                    p, bp = rbank()
                    for k in range(8):
                        kb.op("pe", lambda k=k, blk=blk, p=p: nc.tensor.matmul(p[:], lhsT=hnT[:, k, :], rhs=winb[:, k, blk * 512:(blk + 1) * 512],
                                                                           start=(k == 0), stop=(k == 7)), [CB["hnT"], b_wb], [bp])
                    if blk < 2:
                        kb.op("act", lambda blk=blk, p=p: nc.scalar.copy(out=qt[:, blk * 8:(blk + 1) * 8, :].rearrange("p a d -> p (a d)"), in_=p[:]),
                              [bp], [CB["qt"]])
                    else:
                        kb.op("act", lambda blk=blk, p=p: nc.scalar.activation(out=gate[:, (blk - 2) * 512:(blk - 1) * 512], in_=p[:], func=AF.Silu),
                              [bp], [CB["gate"]])
                yield
                cosb = cs[:, 0:8].unsqueeze(1).to_broadcast([128, 16, 8])
                sinb = cs[:, 8:16].unsqueeze(1).to_broadcast([128, 16, 8])
                x1 = qt[:, :, 0:8]
                x2 = qt[:, :, 8:16]
                kb.op("dve", lambda: nc.vector.tensor_tensor(out=rt[:, 0], in0=x1, in1=cosb, op=ALU.mult), [CB["qt"], CB["cs"]], [CB["rt"]])
                kb.op("dve", lambda: nc.vector.tensor_tensor(out=rt[:, 1], in0=x2, in1=sinb, op=ALU.mult), [CB["qt"], CB["cs"]], [CB["rt"]])
                kb.op("dve", lambda: nc.vector.tensor_tensor(out=rt[:, 2], in0=x2, in1=cosb, op=ALU.mult), [CB["qt"], CB["cs"]], [CB["rt"]])
                kb.op("dve", lambda: nc.vector.tensor_tensor(out=rt[:, 3], in0=x1, in1=sinb, op=ALU.mult), [CB["qt"], CB["cs"]], [CB["rt"]])
                kb.op("dve", lambda: nc.vector.tensor_tensor(out=x1, in0=rt[:, 0], in1=rt[:, 1], op=ALU.subtract), [CB["rt"]], [CB["qt"]])
                kb.op("dve", lambda: nc.vector.tensor_tensor(out=x2, in0=rt[:, 2], in1=rt[:, 3], op=ALU.add), [CB["rt"]], [CB["qt"]])
                yield

            def prompt_attn(i):
                S_ = SETS[i % 2]
                CB = CBS[i % 2]
                ht, jk, hsb, hnT, qt, gate, cs, rt, qb, qT, sq, cb16, dg, cbias, osb, t1, ob, oT = [S_[n_] for n_ in NAMES]
                qtf = qt[:].rearrange("p a d -> p (a d)")
                kb.op("dve", lambda: nc.vector.tensor_tensor(out=jk[:], in0=qtf, in1=qtf, op=ALU.mult), [CB["qt"]], [CB["jk"]])
                kb.op("dve", lambda: nc.vector.tensor_reduce(out=sq[:, 16:32], in_=jk[:].rearrange("p (a d) -> p a d", a=16), axis=AX.X, op=ALU.add),
                      [CB["jk"]], [CB["sq"]])
                p, bp = rbank()
                kb.op("pe", lambda p=p: nc.tensor.transpose(out=p[0:16, 0:128], in_=sq[:, 16:32], identity=identf), [CB["sq"], b_cst], [bp])
                kb.op("dve", lambda p=p: nc.vector.tensor_reduce(out=cb16[:, 1:2], in_=p[0:16, 0:128], axis=AX.X, op=ALU.max), [bp], [CB["cb16"]])
                kb.op("dve", lambda: nc.vector.tensor_tensor(out=cb16[:, 2:3], in0=cb16[:, 0:1], in1=cb16[:, 1:2], op=ALU.mult), [CB["cb16"]], [CB["cb16"]])
                kb.op("act", lambda: nc.scalar.activation(out=cb16[:, 3:4], in_=cb16[:, 2:3], func=AF.Ln), [CB["cb16"]], [CB["cb16"]])
                kb.op("act", lambda: nc.scalar.activation(out=cb16[:, 3:4], in_=cb16[:, 3:4], func=AF.Exp, scale=0.5), [CB["cb16"]], [CB["cb16"]])
                kb.op("dve", lambda: nc.vector.tensor_scalar(out=dg[:], in0=identf[0:16, 0:16], scalar1=cb16[:, 3:4], scalar2=-0.125,
                                                            op0=ALU.mult, op1=ALU.mult), [CB["cb16"], b_cst], [CB["dg"]])
                p, bp = rbank()
                kb.op("pe", lambda p=p: nc.tensor.matmul(p[:, 0:16], lhsT=onesf[0:16, :], rhs=dg[:], start=True, stop=True), [CB["dg"], b_misc], [bp])
                kb.op("act", lambda p=p: nc.scalar.copy(out=cbias[:], in_=p[:, 0:16]), [bp], [CB["cbias"]])
                kb.op("act", lambda: nc.scalar.copy(out=qb[:], in_=qtf), [CB["qt"]], [CB["qb"]])
                p, bp = rbank()
                pb = p[:].bitcast(BF16)
                for h in range(8):
                    kb.op("pe", lambda h=h, pb=pb: nc.tensor.transpose(out=pb[:, h * 128:(h + 1) * 128], in_=qb[:, h * 128:(h + 1) * 128],
                                                                     identity=identb[:]), [CB["qb"], b_misc], [bp])
                kb.op("act", lambda pb=pb: nc.scalar.copy(out=qT[:].rearrange("p h t -> p (h t)"), in_=pb), [bp], [CB["qT"]])
                for b in range(5):
                    kb.op("dve", lambda b=b: nc.vector.memset(ps[b][:], 0.0), [], [b_ps[b]])
                tiles = list(range(i + 1))
                for g0 in range(0, len(tiles), 4):
                    grp = tiles[g0:g0 + 4]
                    n = len(grp)
                    for hj in range(16):
                        h, j = hj // 2, hj % 2
                        p, bp = rbank()
                        for tt, t in enumerate(grp):
                            kb.op("pe", lambda tt=tt, t=t, p=p, h=h, j=j: nc.tensor.matmul(
                                p[:, tt * 128:(tt + 1) * 128], lhsT=KT_all[64 * j:64 * j + 64, t, h, :], rhs=qT[64 * j:64 * j + 64, h, :],
                                start=True, stop=True), [b_KT[t], CB["qT"]], [bp])
                        pi = pidx[0]
                        pidx[0] = (pi + 1) % NPB
                        P, bP = Pb[pi], b_P[pi]
                        kb.op("act", lambda p=p, P=P, n=n, hj=hj: nc.scalar.activation(out=P[:, 0:n * 128], in_=p[:, 0:n * 128], func=AF.Exp,
                                                                                    scale=0.125, bias=cbias[:, hj:hj + 1]), [bp, CB["cbias"]], [bP])
                        if grp[-1] == i:
                            tt = n - 1
                            kb.op("dve", lambda P=P, tt=tt: nc.vector.tensor_tensor(out=P[:, tt * 128:(tt + 1) * 128], in0=P[:, tt * 128:(tt + 1) * 128],
                                                                                 in1=tri, op=ALU.mult), [bP, b_cst], [bP])
                        ob_, oc = hj // 4, (hj % 4) * 128
                        for tt, t in enumerate(grp):
                            kb.op("pe", lambda tt=tt, t=t, P=P, h=h, ob_=ob_, oc=oc: nc.tensor.matmul(
                                ps[ob_][:, oc:oc + 128], lhsT=P[:, tt * 128:(tt + 1) * 128], rhs=V_all[:, t, h * 128:(h + 1) * 128],
                                start=False, stop=False, skip_group_check=True), [bP, b_V[t]], [b_ps[ob_]])
                            kb.op("pe", lambda tt=tt, P=P, hj=hj: nc.tensor.matmul(
                                ps[4][:, hj:hj + 1], lhsT=P[:, tt * 128:(tt + 1) * 128], rhs=onesb[:, 0:1],
                                start=False, stop=False, skip_group_check=True), [bP, b_c0], [b_ps[4]])
                        yield
                kb.op("dve", lambda: nc.vector.reciprocal(out=sq[:, 32:48], in_=ps[4][:, 0:16]), [b_ps[4]], [CB["sq"]])
                rz = sq[:, 32:48].rearrange("p (h j) -> p h j", j=2)
                kb.op("dve", lambda: nc.vector.tensor_scalar(out=rz[:, :, 1], in0=rz[:, :, 1], scalar1=neglam, scalar2=None, op0=ALU.mult),
                      [CB["sq"], b_c0], [CB["sq"]])
                for b in range(4):
                    ov = ps[b][:].rearrange("p (h j d) -> p h j d", h=2, j=2)
                    kb.op("dve", lambda b=b, ov=ov: nc.vector.tensor_tensor(out=osb[:, 2 * b:2 * b + 2, :], in0=ov[:, :, 0, :],
                                                                          in1=rz[:, 2 * b:2 * b + 2, 0:1].to_broadcast([128, 2, 128]), op=ALU.mult),
                          [b_ps[b], CB["sq"]], [CB["osb"]])
                    kb.op("dve", lambda b=b, ov=ov: nc.vector.tensor_tensor(out=t1[:, 2 * b:2 * b + 2, :], in0=ov[:, :, 1, :],
                                                                          in1=rz[:, 2 * b:2 * b + 2, 1:2].to_broadcast([128, 2, 128]), op=ALU.mult),
                          [b_ps[b], CB["sq"]], [CB["t1"]])
                kb.op("dve", lambda: nc.vector.tensor_tensor(out=osb[:], in0=osb[:], in1=t1[:], op=ALU.add), [CB["osb"], CB["t1"]], [CB["osb"]])

            def sample_attn(b):
                S_ = SETS[b % 2]
                CB = CBS[b % 2]
                ht, jk, hsb, hnT, qt, gate, cs, rt, qb, qT, sq, cb16, dg, cbias, osb, t1, ob, oT = [S_[n_] for n_ in NAMES]
                qtf = qt[:].rearrange("p a d -> p (a d)")
                osbf = osb[:].rearrange("p h d -> p (h d)")
                for half in range(2):
                    p, bp = rbank()
                    for hh in range(4):
                        h = half * 4 + hh
                        kb.op("pe", lambda hh=hh, h=h, p=p: nc.tensor.transpose(out=p[:, hh * 128:(hh + 1) * 128], in_=qtf[:, h * 128:(h + 1) * 128],
                                                                               identity=identf), [CB["qt"], b_cst], [bp])
                    kb.op("act", lambda half=half, p=p: nc.scalar.copy(out=qcol[:, half * 4:(half + 1) * 4],
                                                                      in_=p[:].rearrange("p (h t) -> p h t", h=4)[:, :, 0]), [bp], [SB["qcol"]])
                kb.op("dve", lambda: nc.vector.tensor_tensor(out=QB[:], in0=E40R, in1=qcol[:].unsqueeze(2).to_broadcast([128, 8, 80]), op=ALU.mult),
                      [SB["qcol"], b_c2], [SB["QB"]])
                kb.op("pool", lambda: nc.gpsimd.memset(KTs[:], 0.0), [], [SB["KTs"]])
                kb.op("pool", lambda: nc.gpsimd.memset(Vs[:], 0.0), [], [SB["Vs"]])
                with nc.allow_non_contiguous_dma(reason="tiny feature-major column load"):
                    kb.dma("sp", KTs[:, :, 0], ks_o[b].rearrange("(h p) -> p h", p=128), writes=[SB["KTs"]])
                kb.dma("sp", Vs[0:1, :], vs_o[b:b + 1, :], writes=[SB["Vs"]])
                kb.op("act", lambda: nc.scalar.copy(out=QBb[:], in_=QB[:]), [SB["QB"]], [SB["QBb"]])
                groups = [list(range(g0, min(g0 + GP, NP))) for g0 in range(0, NP, GP)]
                import os
                if os.environ.get('DBG_NOPAGES'):
                    groups = []
                    kb.op('dve', lambda: nc.vector.memset(Sall[:], -1e30), [], [SB['Sall']])
                    kb.op('dve', lambda: nc.vector.memset(PnT[:], 0.0), [], [SB['PnT']])
                gi = [0]

                def load_group(grp, src_d):
                    k = gi[0] % NPG
                    gi[0] += 1
                    for tt, pgi in enumerate(grp):
                        col = b * NP + pgi
                        kb._deps("pool", [b_idx], [b_pg[k][tt]])
                        k2 = kb.dnext
                        kb.dnext = (kb.dnext + 1) % len(kb.dsem)
                        si = kb.dsem[k2]
                        if kb.dtot[k2] > 0:
                            kb._wait("pool", (si, kb.dtot[k2]))
                        ins = nc.gpsimd.indirect_dma_start(out=pg[k][:, tt, :], out_offset=None, in_=src_d[:, :],
                                                           in_offset=bass.IndirectOffsetOnAxis(ap=idx[:, col:col + 1], axis=0))
                        kb.dtot[k2] += 16
                        ev = (si, kb.dtot[k2])
                        ins.then_inc(kb.sems[si], 16)
                        b_idx.r.append(ev)
                        b_pg[k][tt].w = ev
                        b_pg[k][tt].r = []
                    return pg[k], b_pg[k]

                def score_mm(p, bp, tt, rhs_tile, rbuf):
                    for h in range(8):
                        for j in range(2):
                            kb.op("pe", lambda h=h, j=j: nc.tensor.matmul(p[0:40, tt * 128:(tt + 1) * 128], lhsT=QB[:, h, j * 40:(j + 1) * 40],
                                                                       rhs=rhs_tile[:, h * 128:(h + 1) * 128],
                                                                       start=(h == 0 and j == 0), stop=(h == 7 and j == 1)), [SB["QB"], rbuf], [bp])
                for g, grp in enumerate(groups):
                    buf, bbuf = load_group(grp, poolk_d)
                    n = len(grp)
                    cb_, bcb = pgb[g % 2], b_pgb[g % 2]
                    kb.op("act", lambda buf=buf, cb_=cb_, n=n: nc.scalar.copy(out=cb_[:, 0:n, :], in_=buf[:, 0:n, :]), bbuf[0:n], [bcb])
                    p, bp = rbank()
                    for h in range(8):
                        for j in range(2):
                            kb.op("pe", lambda h=h, j=j, p=p, cb_=cb_, n=n: nc.tensor.matmul(
                                p[0:40, 0:n * 128].rearrange("p (t k) -> p t k", t=n), lhsT=QBb[:, h, j * 40:(j + 1) * 40],
                                rhs=cb_[:, 0:n, h * 128:(h + 1) * 128], start=(h == 0 and j == 0), stop=(h == 7 and j == 1)),
                                [SB["QBb"], bcb], [bp])
                    kb.op("dve", lambda p=p, g=g, n=n: nc.vector.tensor_copy(out=Sall[:, g * GP * 128:g * GP * 128 + n * 128], in_=p[0:40, 0:n * 128]),
                          [bp], [SB["Sall"]])
                p, bp = rbank()
                score_mm(p, bp, 0, KTs[:].rearrange("p h t -> p (h t)"), SB["KTs"])
                kb.op("act", lambda p=p: nc.scalar.copy(out=Sall[:, NP * 128:NT * 128], in_=p[0:40, 0:128]), [bp], [SB["Sall"]])
                kb.op("dve", lambda: nc.vector.memset(Sall[:, NP * 128 + 1:NT * 128], -1e30), [], [SB["Sall"]])
                kb.op("dve", lambda: nc.vector.tensor_reduce(out=st40[:, 0:1], in_=Sall[:], axis=AX.X, op=ALU.max), [SB["Sall"]], [SB["st40"]])
                kb.op("dve", lambda: nc.vector.tensor_scalar(out=st40[:, 1:2], in0=st40[:, 0:1], scalar1=-0.125, scalar2=None, op0=ALU.mult),
                      [SB["st40"]], [SB["st40"]])
                kb.op("act", lambda: nc.scalar.activation(out=Sall[:], in_=Sall[:], func=AF.Exp, scale=0.125, bias=st40[:, 1:2], accum_out=st40[:, 2:3]),
                      [SB["Sall"], SB["st40"]], [SB["Sall"], SB["st40"]])
                act_fence([SB["st40"]])
                kb.op("dve", lambda: nc.vector.reciprocal(out=st40[:, 3:4], in_=st40[:, 2:3]), [SB["st40"]], [SB["st40"]])
                kb.op("dve", lambda: nc.vector.tensor_tensor(out=st40[:, 4:5], in0=st40[:, 3:4], in1=lamcol[:], op=ALU.mult), [SB["st40"], b_c2], [SB["st40"]])
                kb.op("dve", lambda: nc.vector.tensor_scalar(out=Sall[:], in0=Sall[:], scalar1=st40[:, 4:5], scalar2=None, op0=ALU.mult),
                      [SB["Sall"], SB["st40"]], [SB["Sall"]])
                for t0 in range(0, NT, 12):
                    ts_ = list(range(t0, min(t0 + 12, NT)))
                    p, bp = rbank()
                    for sl, t in enumerate(ts_):
                        kb.op("pe", lambda sl=sl, t=t, p=p: nc.tensor.transpose(out=p[:, sl * 40:(sl + 1) * 40], in_=Sall[:, t * 128:(t + 1) * 128],
                                                                               identity=identf[0:40, 0:40]), [SB["Sall"], b_cst], [bp])
                    n = len(ts_)
                    kb.op("act", lambda p=p, t0=t0, n=n: nc.scalar.copy(out=PnT[:, t0:t0 + n, :].rearrange("p t c -> p (t c)"), in_=p[:, 0:n * 40]),
                          [bp], [SB["PnT"]])
                for g, grp in enumerate(groups if not os.environ.get('DBG_NOPV') else []):
                    buf, bbuf = load_group(grp, poolv_d)
                    for tt, t in enumerate(grp if not os.environ.get('DBG_NOPVMM') else []):
                        for half in range(2):
                            kb.op("pe", lambda tt=tt, t=t, half=half, buf=buf: nc.tensor.matmul(
                                ps[half][0:40, :], lhsT=PnT[:, t, :], rhs=buf[:, tt, half * 512:(half + 1) * 512],
                                start=(t == 0), stop=False), [SB["PnT"], bbuf[tt]], [b_ps[half]])
                for half in range(2):
                    kb.op("pe", lambda half=half: nc.tensor.matmul(ps[half][0:40, :], lhsT=PnT[:, NP, :], rhs=Vs[:, half * 512:(half + 1) * 512],
                                                                  start=False, stop=True), [SB["PnT"], SB["Vs"]], [b_ps[half]])
                for half in range(2):
                    kb.op("dve", lambda half=half: nc.vector.tensor_tensor(
                        out=tmp40[:, half * 512:(half + 1) * 512].rearrange("p (h d) -> p h d", h=4),
                        in0=ps[half][0:40, :].rearrange("p (h d) -> p h d", h=4),
                        in1=M40[:, half * 4:(half + 1) * 4].unsqueeze(2).to_broadcast([40, 4, 128]), op=ALU.mult), [b_ps[half], b_c2], [SB["tmp40"]])
                kb.op("dve", lambda: nc.vector.memset(osb[:], 0.0), [], [CB["osb"]])
                for half in range(2):
                    p, bp = rbank()
                    kb.op("pe", lambda half=half, p=p: nc.tensor.matmul(p[0:1, :], lhsT=onesf[0:40, 0:1], rhs=tmp40[:, half * 512:(half + 1) * 512],
                                                                       start=True, stop=True), [SB["tmp40"], b_misc], [bp])
                    kb.op("act", lambda half=half, p=p: nc.scalar.copy(out=osbf[0:1, half * 512:(half + 1) * 512], in_=p[0:1, :]), [bp], [CB["osb"]])

            def tail(i):
                S_ = SETS[i % 2]
                CB = CBS[i % 2]
                ht, jk, hsb, hnT, qt, gate, cs, rt, qb, qT, sq, cb16, dg, cbias, osb, t1, ob, oT = [S_[n_] for n_ in NAMES]
                yield
                kb.op("dve", lambda: nc.vector.tensor_tensor(out=t1[:], in0=osb[:], in1=osb[:], op=ALU.mult), [CB["osb"]], [CB["t1"]])
                kb.op("dve", lambda: nc.vector.tensor_reduce(out=sq[:, 48:56], in_=t1[:], axis=AX.X, op=ALU.add), [CB["t1"]], [CB["sq"]])
                rstd_c(sq[:, 48:56], sq[:, 56:64], 128, [CB["sq"]])
                kb.op("dve", lambda: nc.vector.tensor_tensor(out=osb[:], in0=osb[:], in1=sq[:, 56:64].unsqueeze(2).to_broadcast([128, 8, 128]), op=ALU.mult),
                      [CB["osb"], CB["sq"]], [CB["osb"]])
                kb.op("dve", lambda: nc.vector.tensor_tensor(out=osb[:], in0=osb[:], in1=sgb[:].unsqueeze(1).to_broadcast([128, 8, 128]), op=ALU.mult),
                      [CB["osb"], b_c0], [CB["osb"]])
                kb.op("dve", lambda: nc.vector.tensor_tensor(out=ob[:], in0=osb[:].rearrange("p h d -> p (h d)"), in1=gate[:], op=ALU.mult),
                      [CB["osb"], CB["gate"]], [CB["ob"]])
                p, bp = rbank()
                pb = p[:].bitcast(BF16)
                for h in range(8):
                    kb.op("pe", lambda h=h, pb=pb: nc.tensor.transpose(out=pb[:, h * 128:(h + 1) * 128], in_=ob[:, h * 128:(h + 1) * 128],
                                                                     identity=identb[:]), [CB["ob"], b_misc], [bp])
                kb.op("act", lambda pb=pb: nc.scalar.copy(out=oT[:].rearrange("p h t -> p (h t)"), in_=pb), [bp], [CB["oT"]])
                for blk in range(2):
                    yield
                    p, bp = rbank()
                    for k in range(8):
                        kb.op("pe", lambda k=k, blk=blk, p=p: nc.tensor.matmul(p[:], lhsT=oT[:, k, :], rhs=woutb[:, k, blk * 512:(blk + 1) * 512],
                                                                           start=(k == 0), stop=(k == 7)), [CB["oT"], b_wb], [bp])
                    kb.op("dve", lambda blk=blk, p=p: nc.vector.tensor_tensor(out=ht[:, blk * 512:(blk + 1) * 512], in0=ht[:, blk * 512:(blk + 1) * 512],
                                                                            in1=p[:], op=ALU.add), [bp, CB["ht"]], [CB["ht"]])
                yield
                kb.op("act", lambda: nc.scalar.activation(out=jk[:], in_=ht[:], func=AF.Square, accum_out=sq[:, 2:3]), [CB["ht"]], [CB["jk"], CB["sq"]])
                rstd_c(sq[:, 2:3], sq[:, 3:4], D, [CB["sq"]])
                kb.op("dve", lambda: nc.vector.scalar_tensor_tensor(out=jk[:], in0=ht[:], scalar=sq[:, 3:4], in1=nfb[:], op0=ALU.mult, op1=ALU.mult),
                      [CB["ht"], CB["sq"], b_c0], [CB["jk"]])
                if is_sm:
                    kb.dma("sp", ys_o[i:i + 1, :], jk[0:1, :], reads=[CB["jk"]])
                else:
                    kb.dma("sp", yp_o[i * 128:(i + 1) * 128, :], jk[:], reads=[CB["jk"]])

            def run(gen):
                for _ in gen:
                    pass

            if is_sm:
                for i in range(NS):
                    run(part1(i))
                    sample_attn(i)
                    run(tail(i))
            else:
                run(part1(0))
                for i in range(NCH):
                    side = []
                    if i >= 1:
                        side.append(tail(i - 1))
                    if i + 1 < NCH:
                        side.append(part1(i + 1))
                    for _ in prompt_attn(i):
                        if side:
                            try:
                                next(side[0])
                            except StopIteration:
                                side.pop(0)
                    for g_ in side:
                        run(g_)
                run(tail(NCH - 1))
            kb.barrier()

        phaseC("prompt")
        if os.environ.get("DBG_STOP") == "C":
            return nc
        esR.close()
        phaseC("sample")
        kb.barrier(engines=("sp",))
    return nc


def _rope_tables(pos):
    inv = (500000.0 ** (-(np.arange(0, 16, 2, dtype=np.float32)) / np.float32(16))).astype(np.float32)
    ang = pos.astype(np.float32)[:, None] * inv[None, :]
    return np.cos(ang).astype(np.float32), np.sin(ang).astype(np.float32)


def _consts():
    i = np.arange(128)
    tri = (i[:, None] <= i[None, :]).astype(np.float32)
    astr = (i[:, None] > i[None, :]).astype(np.float32)
    ident = np.eye(128, dtype=np.float32)
    return np.ascontiguousarray(np.concatenate([tri, astr, ident], axis=1))


def _consts2():
    c = np.zeros((128, 652), np.float32)
    e = np.zeros((128, 8, 2, 40), np.float32)
    for h in range(8):
        e[0:64, h, 0, h] = 1.0
        e[64:128, h, 1, 32 + h] = 1.0
    c[:, 0:640] = e.reshape(128, 640)
    for h in range(8):
        c[h, 640 + h] = 1.0
        c[32 + h, 640 + h] = 1.0
    c[:, 648] = np.arange(128)
    c[0:8, 649] = 1.0
    c[32:40, 650] = 1.0
    return c


def make_in_maps(inp, ncores, T, NS, NP=64, past=PAST):
    f = lambda a: np.ascontiguousarray(np.asarray(a, dtype=np.float32))
    cosp, sinp = _rope_tables(np.arange(T))
    coss, sins = _rope_tables(np.full((128,), past))
    convw = f(np.asarray(inp["conv_w"])[0].reshape(4, 24, 128).transpose(2, 1, 0))
    convb = f(np.asarray(inp["conv_b"])[0].reshape(24, 128).T)
    na = f(np.asarray(inp["norm_a"])[0].reshape(8, 128).T)
    gn = f(np.asarray(inp["gnorm_a"])[0].reshape(16, 128).T)
    nkv = f(np.asarray(inp["norm_kv"]).reshape(8, 128).T)
    shared = dict(w_in_a=f(inp["w_in_a"][0]), w_out_a=f(inp["w_out_a"][0]), w_kv=f(inp["w_kv"]), convw=convw, convb=convb, na=na, gn=gn,
                  nkv=nkv, dtb=f(inp["dt_bias"][0]), alog=f(inp["a_log"][0]), dsk=f(inp["d_skip"][0]), cst=_consts(),
                  cosp=f(cosp), sinp=f(sinp), coss=f(coss), sins=f(sins),
                  w_in_b=f(inp["w_in_b"][0]), w_out_b=f(inp["w_out_b"][0]), nb=f(np.asarray(inp["norm_b"])[0].reshape(8, 128).T),
                  lamv=f(np.stack([np.asarray(inp["lambda_q1"])[0], np.asarray(inp["lambda_k1"])[0],
                                   np.asarray(inp["lambda_q2"])[0], np.asarray(inp["lambda_k2"])[0]])),
                  subln=f(np.asarray(inp["subln_b"])[0]), nf=f(inp["norm_f"]))
    ck = np.asarray(inp["cache_k"])
    npool = ck.shape[0]
    poolk = np.ascontiguousarray(ck.reshape(npool, 128, 8, 128).transpose(0, 3, 2, 1)).reshape(npool * 128, D)
    poolv = np.ascontiguousarray(np.asarray(inp["cache_v"]).reshape(npool * 128, D))
    shared["poolk"] = poolk
    shared["poolv"] = poolv
    shared["cst2"] = _consts2()
    ptab = np.asarray(inp["page_table"]).astype(np.int32)
    maps = []
    for c in range(ncores):
        m = dict(shared)
        m["ptab"] = np.ascontiguousarray(ptab[c * NS:(c + 1) * NS, :NP].reshape(-1))
        m["x"] = f(inp["x_prompt"][c][:T])
        m["xsmp"] = f(np.asarray(inp["x_sample"])[c * NS:(c + 1) * NS, 0])
        sc = np.asarray(inp["state_conv"])[0, c * NS:(c + 1) * NS]
        m["sconv"] = f(sc.reshape(NS, 3, 24, 128).transpose(0, 3, 2, 1))
        ss = np.asarray(inp["state_ssm"])[0, c * NS:(c + 1) * NS]
        m["sssm"] = f(ss.reshape(NS, DI, 128).transpose(0, 2, 1))
        maps.append(m)
    return maps


_NC_CACHE = {}


def kernel(**inp):
    ncores, T, NS = 8, 2048, 4
    key = (T, NS)
    if key not in _NC_CACHE:
        _NC_CACHE[key] = build(T, NS)
    nc = _NC_CACHE[key]
    maps = make_in_maps(inp, ncores, T, NS)
    res = run_bass_kernel_spmd(nc, maps, core_ids=list(range(ncores))).results
    cat = lambda k: np.stack([np.asarray(r[k]) for r in res])
    y_p = cat("y_p")
    y_s = cat("y_s").reshape(ncores * NS, 1, D)
    k_p = cat("k_p").reshape(ncores, T, 8, 2, 64)
    v_p = cat("v_p").reshape(ncores, T, 8, 128)
    conv_p = cat("conv_p").transpose(0, 3, 2, 1).reshape(1, ncores, 3, CONV)
    ssm_p = cat("ssm_p").transpose(0, 2, 1).reshape(1, ncores, 32, 64, 128)
    k_s = cat("k_s").reshape(ncores * NS, 1, 8, 2, 64)
    v_s = cat("v_s").reshape(ncores * NS, 1, 8, 128)
    conv_s = cat("conv_s").reshape(ncores * NS, 128, 24, 3).transpose(0, 3, 2, 1).reshape(1, ncores * NS, 3, CONV)
    ssm_s = cat("ssm_s").reshape(ncores * NS, 128, DI).transpose(0, 2, 1).reshape(1, ncores * NS, 32, 64, 128)
    f = lambda a: np.ascontiguousarray(a, dtype=np.float32)
    return (f(y_p), f(y_s), f(k_p), f(v_p), f(conv_p), f(ssm_p), f(k_s), f(v_s), f(conv_s), f(ssm_s))
```

```python
import numpy as np
from contextlib import ExitStack
import concourse.bass as bass
import concourse.mybir as mybir
from concourse.bass_utils import run_bass_kernel_spmd

F32 = mybir.dt.float32
BF16 = mybir.dt.bfloat16
AF = mybir.ActivationFunctionType
ALU = mybir.AluOpType
AX = mybir.AxisListType

D = 1024
DI = 2048
CONV = 3072
INA = 5152
EPS = 1e-6
PAST = 8192


class Buf:
    __slots__ = ("w", "r", "name")

    def __init__(self, name=""):
        self.w = None
        self.r = []
        self.name = name


class KB:
    def __init__(self, nc, es, ndma=24):
        self.nc = nc
        self.eng = {"pe": nc.tensor, "act": nc.scalar, "dve": nc.vector, "pool": nc.gpsimd, "sp": nc.sync}
        self.sems = []
        self.csem = {}
        self.cnt = {}
        for e in ("pe", "act", "dve", "pool"):
            s = es.enter_context(nc.semaphore("c_" + e))
            self.csem[e] = len(self.sems)
            self.sems.append(s)
            self.cnt[e] = 0
        self.dsem = []
        self.dtot = []
        for i in range(ndma):
            s = es.enter_context(nc.semaphore("d%d" % i))
            self.dsem.append(len(self.sems))
            self.sems.append(s)
            self.dtot.append(0)
        self.dnext = 0
        self.seen = {e: {} for e in self.eng}

    def _wait(self, e, ev):
        si, v = ev
        if self.seen[e].get(si, 0) >= v:
            return
        self.seen[e][si] = v
        self.eng[e].wait_ge(self.sems[si], v)

    def _deps(self, e, reads, writes):
        own = self.csem.get(e) if e == "pe" else None
        for b in reads:
            if b.w is not None:
                self._wait(e, b.w)
        for b in writes:
            if b.w is not None and b.w[0] != own:
                self._wait(e, b.w)
            for ev in b.r:
                if ev[0] != own:
                    self._wait(e, ev)

    def op(self, e, fn, reads=(), writes=()):
        self._deps(e, reads, writes)
        ins = fn()
        self.cnt[e] += 1
        ev = (self.csem[e], self.cnt[e])
        ins.then_inc(self.sems[ev[0]], 1)
        for b in reads:
            b.r.append(ev)
        for b in writes:
            b.w = ev
            b.r = []
        return ins

    def dma(self, q, out, in_, reads=(), writes=(), **kw):
        k = self.dnext
        self.dnext = (self.dnext + 1) % len(self.dsem)
        si = self.dsem[k]
        if self.dtot[k] > 0:
            self._wait(q, (si, self.dtot[k]))
        self._deps(q, reads, writes)
        ins = self.eng[q].dma_start(out=out, in_=in_, **kw)
        self.dtot[k] += 16
        ev = (si, self.dtot[k])
        ins.then_inc(self.sems[si], 16)
        for b in reads:
            b.r.append(ev)
        for b in writes:
            b.w = ev
            b.r = []
        return ev

    def barrier(self, engines=("pe", "act", "dve", "pool", "sp")):
        for e in engines:
            for o in ("pe", "act", "dve", "pool"):
                if self.cnt[o] > 0:
                    self._wait(e, (self.csem[o], self.cnt[o]))
            for k, si in enumerate(self.dsem):
                if self.dtot[k] > 0:
                    self._wait(e, (si, self.dtot[k]))


def build(T, NS, NP=64, NPOOL=2560):
    NCH = T // 128
    NT = NP + 1
    nc = bass.Bass("TRN2", target_bir_lowering=False)

    def din(name, shape, dt=F32):
        return nc.dram_tensor(name, list(shape), dt, kind="ExternalInput").ap()

    def dout(name, shape, dt=F32):
        return nc.dram_tensor(name, list(shape), dt, kind="ExternalOutput").ap()

    x_d = din("x", [T, D])
    xs_d = din("xsmp", [NS, D])
    sconv_d = din("sconv", [NS, 128, 24, 3])
    sssm_d = din("sssm", [NS, 128, DI])
    wina_d = din("w_in_a", [D, INA])
    wouta_d = din("w_out_a", [DI, D])
    wkv_d = din("w_kv", [D, 2 * D])
    convw_d = din("convw", [128, 24, 4])
    convb_d = din("convb", [128, 24])
    na_d = din("na", [128, 8])
    gn_d = din("gn", [128, 16])
    nkv_d = din("nkv", [128, 8])
    dtb_d = din("dtb", [32])
    alog_d = din("alog", [32])
    dsk_d = din("dsk", [32])
    cst_d = din("cst", [128, 384])
    cosp_d = din("cosp", [T, 8])
    sinp_d = din("sinp", [T, 8])
    coss_d = din("coss", [128, 8])
    sins_d = din("sins", [128, 8])
    winb_d = din("w_in_b", [D, 2 * D])
    woutb_d = din("w_out_b", [D, D])
    nb_d = din("nb", [128, 8])
    lam_d = din("lamv", [4, 64])
    subln_d = din("subln", [128])
    nf_d = din("nf", [D])
    poolk_d = din("poolk", [NPOOL * 128, D])
    poolv_d = din("poolv", [NPOOL * 128, D])
    pt_d = nc.dram_tensor("ptab", [NS * NP], mybir.dt.int32, kind="ExternalInput").ap()
    cst2_d = din("cst2", [128, 652])

    kp_o = dout("k_p", [T, D])
    vp_o = dout("v_p", [T, D])
    convp_o = dout("conv_p", [128, 24, 3])
    ssmp_o = dout("ssm_p", [128, DI])
    ks_o = dout("k_s", [NS, D])
    vs_o = dout("v_s", [NS, D])
    convs_o = dout("conv_s", [NS, 128, 24, 3])
    ssms_o = dout("ssm_s", [NS, 128, DI])
    yp_o = dout("y_p", [T, D])
    ys_o = dout("y_s", [NS, D])

    NSEQ = 1 + NS
    h1_d = nc.dram_tensor("h1d", [NCH + NS, 128, D], F32, kind="Internal").ap()

    with ExitStack() as es:
        E = es.enter_context
        kb = KB(nc, es)

        def sb(name, shape, dt=F32):
            return E(nc.sbuf_tensor("s_" + name, list(shape), dt))

        cst = sb("cst", [128, 384])
        b_cst = Buf()
        kb.dma("sp", cst[:], cst_d[:, :], writes=[b_cst])
        tri = cst[:, 0:128]
        astr = cst[:, 128:256]
        identf = cst[:, 256:384]
        identb = sb("identb", [128, 128], BF16)
        onesf = sb("onesf", [128, 128])
        b_misc = Buf()
        kb.op("dve", lambda: nc.vector.tensor_copy(out=identb[:], in_=identf), [b_cst], [b_misc])
        kb.op("dve", lambda: nc.vector.memset(onesf[:], 1.0), [], [b_misc])
        dmy = sb("dmy", [128, 2])
        kb.op("dve", lambda: nc.vector.memset(dmy[:], 0.0), [], [b_misc])

        def act_fence(bufs):
            kb.op("act", lambda: nc.scalar.copy(out=dmy[:, 0:1], in_=dmy[:, 1:2]), list(bufs), list(bufs))

        epsc = sb("epsc", [128, 1])
        kb.op("dve", lambda: nc.vector.memset(epsc[:], EPS), [], [b_misc])

        convw = sb("convw", [128, 24, 4])
        convb = sb("convb", [128, 24])
        na = sb("na", [128, 8])
        gn = sb("gn", [128, 16])
        nkv = sb("nkv", [128, 8])
        prm = sb("prm", [128, 96])
        b_prm = Buf()
        kb.dma("sp", convw[:], convw_d[:, :, :], writes=[b_prm])
        kb.dma("sp", convb[:], convb_d[:, :], writes=[b_prm])
        kb.dma("sp", na[:], na_d[:, :], writes=[b_prm])
        kb.dma("sp", gn[:], gn_d[:, :], writes=[b_prm])
        kb.dma("sp", nkv[:], nkv_d[:, :], writes=[b_prm])
        kb.dma("sp", prm[:, 0:32], dtb_d.partition_broadcast(128), writes=[b_prm])
        kb.dma("sp", prm[:, 32:64], alog_d.partition_broadcast(128), writes=[b_prm])
        kb.dma("sp", prm[:, 64:96], dsk_d.partition_broadcast(128), writes=[b_prm])
        dtb = prm[:, 0:32]
        dsk = prm[:, 64:96]
        aneg = sb("aneg", [128, 32])
        kb.op("act", lambda: nc.scalar.activation(out=aneg[:], in_=prm[:, 32:64], func=AF.Exp), [b_prm], [b_misc])
        kb.op("dve", lambda: nc.vector.tensor_scalar(out=aneg[:], in0=aneg[:], scalar1=-1.0, scalar2=None, op0=ALU.mult),
              [b_misc], [b_misc])

        NB = 8
        ps = [E(nc.psum_tensor("ps%d" % i, [128, 512], F32)) for i in range(NB)]
        b_ps = [Buf("ps%d" % i) for i in range(NB)]
        psi = [0]

        def bank():
            i = psi[0]
            psi[0] = (i + 1) % NB
            return ps[i], b_ps[i]

        with ExitStack() as esA:
            EA = esA.enter_context

            def sa(name, shape, dt=F32):
                return EA(nc.sbuf_tensor("a_" + name, list(shape), dt))

            wina = sa("wina", [128, 8, INA], BF16)
            wouta = sa("wouta", [128, 16, D], BF16)
            b_wina = Buf()
            b_wouta = Buf()
            for k in range(8):
                kb.dma("pool", wina[:, k, :], wina_d[k * 128:(k + 1) * 128, :], writes=[b_wina])
            for k in range(16):
                kb.dma("pool", wouta[:, k, :], wouta_d[k * 128:(k + 1) * 128, :], writes=[b_wouta])
            for k in range(8):
                kb.op("dve", lambda k=k: nc.vector.tensor_scalar(out=wina[:, k, :], in0=wina[:, k, :], scalar1=na[:, k:k + 1],
                                                               scalar2=None, op0=ALU.mult), [b_wina, b_prm], [b_wina])
            for k in range(16):
                kb.op("dve", lambda k=k: nc.vector.tensor_scalar(out=wouta[:, k, :], in0=wouta[:, k, :], scalar1=gn[:, k:k + 1],
                                                               scalar2=None, op0=ALU.mult), [b_wouta, b_prm], [b_wouta])

            xt = sa("xt", [128, D])
            xsb = sa("xsb", [128, D], BF16)
            xnT = sa("xnT", [128, 8, 128], BF16)
            zs = sa("zs", [128, DI], BF16)
            xpad = sa("xpad", [128, 24, 131])
            halo = sa("halo", [128, 24, 3])
            acc = sa("acc", [128, 12, 128])
            xc = sa("xc", [128, 24, 128], BF16)
            xstok = sa("xstok", [128, 32, 64], BF16)
            btok = sa("btok", [128, 4, 128], BF16)
            xdt = sa("xdt", [128, 32, 64], BF16)
            xdtd = sa("xdtd", [128, 8, 64], BF16)
            sm = sa("sm", [128, 16])
            dt = sa("dt", [128, 32])
            dta = sa("dta", [128, 32])
            acs = sa("acs", [128, 32])
            eacs = sa("eacs", [128, 32])
            decs = sa("decs", [128, 32])
            cd = sa("cd", [128, 32])
            vmask = sa("vmask", [128, 2])
            Zg2 = [sa("Zg%d" % i, [128, 8, 128]) for i in range(2)]
            Eg2 = [sa("Eg%d" % i, [128, 8, 128], BF16) for i in range(2)]
            Gg2 = [sa("Gg%d" % i, [128, 8, 128], BF16) for i in range(2)]
            cbm = sa("cbm", [128, 4, 128])
            yg = sa("yg", [128, 512])
            tg = sa("tg", [128, 512])
            ygb = sa("ygb", [128, DI], BF16)
            ygT = sa("ygT", [128, 16, 128], BF16)
            S = sa("S", [128, DI])
            Sb = sa("Sb", [128, DI], BF16)
            B = {n: Buf(n) for n in ["xt", "junk", "xsb", "xnT", "zs", "xpad", "halo", "acc", "xc", "xstok", "btok", "xdt",
                                     "xdtd", "sm", "dt", "dta", "acs", "eacs", "decs", "cd", "vmask", "Zg0", "Eg0", "Gg0", "Zg1", "Eg1", "Gg1", "cbm",
                                     "yg", "tg", "ygb", "ygT", "S", "Sb"]}
            kb.op("dve", lambda: nc.vector.memset(vmask[:], 0.0), [], [B["vmask"]])
            kb.op("dve", lambda: nc.vector.memset(vmask[:, 0:1], 1.0), [], [B["vmask"]])
            kb.op("dve", lambda: nc.vector.memset(vmask[0:1, 1:2], 1.0), [], [B["vmask"]])

            def rstd_from(ssq_ap, out_ap, n, bufs):
                act_fence(bufs)
                kb.op("act", lambda: nc.scalar.activation(out=out_ap, in_=ssq_ap, func=AF.Ln, scale=1.0 / n, bias=epsc[:, 0:1]), bufs + [b_misc], bufs)
                kb.op("act", lambda: nc.scalar.activation(out=out_ap, in_=out_ap, func=AF.Exp, scale=-0.5), bufs, bufs)

            def chunk(seq, c, first, last):
                is_s = seq > 0
                vm = vmask[:, 1:2] if is_s else vmask[:, 0:1]
                if is_s:
                    kb.op("dve", lambda: nc.vector.memset(xt[:], 0.0), [], [B["xt"]])
                    kb.dma("sp", xt[0:1, :], xs_d[seq - 1:seq, :], writes=[B["xt"]])
                else:
                    kb.dma("sp", xt[:], x_d[c * 128:(c + 1) * 128, :], writes=[B["xt"]])
                if first:
                    if is_s:
                        kb.dma("sp", halo[:], sconv_d[seq - 1], writes=[B["halo"]])
                        kb.dma("sp", S[:], sssm_d[seq - 1], writes=[B["S"]])
                    else:
                        kb.op("pool", lambda: nc.gpsimd.memset(halo[:], 0.0), [], [B["halo"]])
                        kb.op("pool", lambda: nc.gpsimd.memset(S[:], 0.0), [], [B["S"]])
                    kb.op("act", lambda: nc.scalar.copy(out=Sb[:], in_=S[:]), [B["S"]], [B["Sb"]])
                kb.op("act", lambda: nc.scalar.activation(out=acc[:].rearrange("p c t -> p (c t)")[:, 0:D], in_=xt[:], func=AF.Square, accum_out=sm[:, 0:1]),
                      [B["xt"]], [B["acc"], B["sm"]])
                rstd_from(sm[:, 0:1], sm[:, 1:2], D, [B["sm"]])
                kb.op("dve", lambda: nc.vector.tensor_scalar(out=xsb[:], in0=xt[:], scalar1=sm[:, 1:2], scalar2=None, op0=ALU.mult),
                      [B["xt"], B["sm"]], [B["xsb"]])
                p, bp = bank()
                pb = p[:].bitcast(BF16)
                for k in range(8):
                    kb.op("pe", lambda k=k: nc.tensor.transpose(out=pb[:, k * 128:(k + 1) * 128], in_=xsb[:, k * 128:(k + 1) * 128],
                                                              identity=identb[:]), [B["xsb"], b_misc], [bp])
                kb.op("act", lambda: nc.scalar.copy(out=xnT[:].rearrange("p k t -> p (k t)"), in_=pb), [bp], [B["xnT"]])
                for blk in range(4):
                    p, bp = bank()
                    for k in range(8):
                        kb.op("pe", lambda k=k, blk=blk, p=p: nc.tensor.matmul(p[:], lhsT=xnT[:, k, :], rhs=wina[:, k, blk * 512:(blk + 1) * 512],
                                                                           start=(k == 0), stop=(k == 7)), [B["xnT"], b_wina], [bp])
                    kb.op("act", lambda blk=blk, p=p: nc.scalar.activation(out=zs[:, blk * 512:(blk + 1) * 512], in_=p[:], func=AF.Silu),
                          [bp], [B["zs"]])
                kb.op("pool", lambda: nc.gpsimd.tensor_copy(out=xpad[:, :, 0:3], in_=halo[:]), [B["halo"]], [B["xpad"]])
                for q4 in range(6):
                    p, bp = bank()
                    for j in range(4):
                        ct = q4 * 4 + j
                        for k in range(8):
                            kb.op("pe", lambda k=k, ct=ct, j=j, p=p: nc.tensor.matmul(
                                p[:, j * 128:(j + 1) * 128], lhsT=wina[:, k, DI + ct * 128:DI + (ct + 1) * 128], rhs=xnT[:, k, :],
                                start=(k == 0), stop=(k == 7)), [B["xnT"], b_wina], [bp])
                    kb.op("act", lambda q4=q4, p=p: nc.scalar.copy(out=xpad[:, q4 * 4:(q4 + 1) * 4, 3:131],
                                                                  in_=p[:].rearrange("p (j t) -> p j t", j=4)), [bp], [B["xpad"]])
                if is_s:
                    kb.op("pool", lambda: nc.gpsimd.tensor_copy(out=halo[:], in_=xpad[:, :, 1:4]), [B["xpad"]], [B["halo"]])
                else:
                    kb.op("pool", lambda: nc.gpsimd.tensor_copy(out=halo[:], in_=xpad[:, :, 128:131]), [B["xpad"]], [B["halo"]])
                if last:
                    kb.dma("sp", convs_o[seq - 1] if is_s else convp_o[:, :, :], halo[:], reads=[B["halo"]])
                for hf in range(2):
                    for c12 in range(12):
                        ct = hf * 12 + c12
                        kb.op("act", lambda ct=ct, c12=c12: nc.scalar.activation(out=acc[:, c12, :], in_=xpad[:, ct, 3:131], func=AF.Identity,
                                                                               scale=convw[:, ct, 3:4], bias=convb[:, ct:ct + 1]),
                              [B["xpad"], b_prm], [B["acc"]])
                    for c12 in range(12):
                        ct = hf * 12 + c12
                        for kk in range(3):
                            kb.op("dve", lambda ct=ct, c12=c12, kk=kk: nc.vector.scalar_tensor_tensor(
                                out=acc[:, c12, :], in0=xpad[:, ct, kk:kk + 128], scalar=convw[:, ct, kk:kk + 1], in1=acc[:, c12, :],
                                op0=ALU.mult, op1=ALU.add), [B["xpad"], B["acc"], b_prm], [B["acc"]])
                    kb.op("act", lambda hf=hf: nc.scalar.activation(out=xc[:, hf * 12:(hf + 1) * 12, :].rearrange("p c t -> p (c t)"),
                                                                   in_=acc[:].rearrange("p c t -> p (c t)"), func=AF.Silu), [B["acc"]], [B["xc"]])
                p, bp = bank()
                for k in range(8):
                    kb.op("pe", lambda k=k, p=p: nc.tensor.matmul(p[:, 0:32], lhsT=xnT[:, k, :], rhs=wina[:, k, DI + CONV:INA],
                                                               start=(k == 0), stop=(k == 7)), [B["xnT"], b_wina], [bp])
                kb.op("dve", lambda p=p: nc.vector.tensor_tensor(out=dt[:], in0=p[:, 0:32], in1=dtb, op=ALU.add), [bp, b_prm], [B["dt"]])
                kb.op("act", lambda: nc.scalar.activation(out=dt[:], in_=dt[:], func=AF.Exp), [B["dt"]], [B["dt"]])
                kb.op("act", lambda: nc.scalar.activation(out=dt[:], in_=dt[:], func=AF.Ln, bias=1.0), [B["dt"]], [B["dt"]])
                kb.op("dve", lambda: nc.vector.tensor_scalar(out=dt[:], in0=dt[:], scalar1=vm, scalar2=None, op0=ALU.mult),
                      [B["dt"], B["vmask"]], [B["dt"]])
                kb.op("dve", lambda: nc.vector.tensor_tensor(out=dta[:], in0=dt[:], in1=aneg[:], op=ALU.mult), [B["dt"], b_misc], [B["dta"]])
                for half in range(2):
                    p, bp = bank()
                    pb = p[:].bitcast(BF16)
                    for i in range(8):
                        ct = half * 8 + i
                        kb.op("pe", lambda i=i, ct=ct, pb=pb: nc.tensor.transpose(out=pb[:, i * 128:(i + 1) * 128], in_=xc[:, ct, :],
                                                                                identity=identb[:]), [B["xc"], b_misc], [bp])
                    kb.op("act", lambda half=half, pb=pb: nc.scalar.copy(
                        out=xstok[:, half * 16:(half + 1) * 16, :].rearrange("p r d -> p (r d)"), in_=pb), [bp], [B["xstok"]])
                p, bp = bank()
                pb = p[:].bitcast(BF16)
                for g in range(4):
                    kb.op("pe", lambda g=g, pb=pb: nc.tensor.transpose(out=pb[:, g * 128:(g + 1) * 128], in_=xc[:, 16 + g, :],
                                                                     identity=identb[:]), [B["xc"], b_misc], [bp])
                kb.op("act", lambda pb=pb: nc.scalar.copy(out=btok[:].rearrange("p g n -> p (g n)"), in_=pb[:, 0:512]), [bp], [B["btok"]])
                kb.op("dve", lambda: nc.vector.tensor_tensor(out=xdt[:], in0=xstok[:], in1=dt[:].unsqueeze(2).to_broadcast([128, 32, 64]),
                                                             op=ALU.mult), [B["xstok"], B["dt"]], [B["xdt"]])
                p, bp = bank()
                kb.op("pe", lambda p=p: nc.tensor.matmul(p[:, 0:32], lhsT=tri, rhs=dta[:], start=True, stop=True), [b_cst, B["dta"]], [bp])
                kb.op("pe", lambda p=p: nc.tensor.matmul(p[:, 32:64], lhsT=onesf[:], rhs=dta[:], start=True, stop=True),
                      [b_misc, B["dta"]], [bp])
                kb.op("act", lambda p=p: nc.scalar.copy(out=acs[:], in_=p[:, 0:32]), [bp], [B["acs"]])
                kb.op("act", lambda p=p: nc.scalar.activation(out=eacs[:], in_=p[:, 0:32], func=AF.Exp), [bp], [B["eacs"]])
                kb.op("act", lambda p=p: nc.scalar.activation(out=cd[:], in_=p[:, 32:64], func=AF.Exp), [bp], [B["cd"]])
                kb.op("dve", lambda p=p: nc.vector.tensor_tensor(out=decs[:], in0=p[:, 32:64], in1=acs[:], op=ALU.subtract),
                      [bp, B["acs"]], [B["decs"]])
                kb.op("act", lambda: nc.scalar.activation(out=decs[:], in_=decs[:], func=AF.Exp), [B["decs"]], [B["decs"]])
                p, bp = bank()
                for g in range(4):
                    kb.op("pe", lambda g=g, p=p: nc.tensor.matmul(p[:, g * 128:(g + 1) * 128], lhsT=xc[:, 16 + g, :], rhs=xc[:, 20 + g, :],
                                                               start=True, stop=True), [B["xc"]], [bp])
                kb.op("dve", lambda p=p: nc.vector.tensor_tensor(out=cbm[:], in0=p[:].rearrange("p (g l) -> p g l", g=4),
                                                                in1=tri.unsqueeze(1).to_broadcast([128, 4, 128]), op=ALU.mult),
                      [bp, b_cst], [B["cbm"]])
                for g in range(4):
                    Zg, Eg, Gg = Zg2[g % 2], Eg2[g % 2], Gg2[g % 2]
                    zk, ek, gk = "Zg%d" % (g % 2), "Eg%d" % (g % 2), "Gg%d" % (g % 2)
                    kb.op("pool", lambda g=g, Zg=Zg: nc.gpsimd.tensor_tensor(
                        out=Zg[:], in0=tri.unsqueeze(1).to_broadcast([128, 8, 128]),
                        in1=dta[:, g * 8:(g + 1) * 8].unsqueeze(2).to_broadcast([128, 8, 128]), op=ALU.mult),
                        [b_cst, B["dta"]], [B[zk]])
                    for hh in range(2):
                        p, bp = bank()
                        kb.op("pe", lambda hh=hh, p=p, Zg=Zg: nc.tensor.matmul(p[:], lhsT=astr, rhs=Zg[:, hh * 4:(hh + 1) * 4, :].rearrange("p r l -> p (r l)"),
                                                                     start=True, stop=True), [b_cst, B[zk]], [bp])
                        kb.op("act", lambda hh=hh, p=p, Eg=Eg: nc.scalar.activation(out=Eg[:, hh * 4:(hh + 1) * 4, :].rearrange("p r l -> p (r l)"),
                                                                            in_=p[:], func=AF.Exp), [bp], [B[ek]])
                    kb.op("dve", lambda g=g, Gg=Gg, Eg=Eg: nc.vector.tensor_tensor(out=Gg[:], in0=Eg[:], in1=cbm[:, g, :].unsqueeze(1).to_broadcast([128, 8, 128]),
                                                                     op=ALU.mult), [B[ek], B["cbm"]], [B[gk]])
                    pA, bA = bank()
                    for r in range(8):
                        h = g * 8 + r
                        kb.op("pe", lambda r=r, h=h, pA=pA, Gg=Gg: nc.tensor.matmul(pA[:, r * 64:(r + 1) * 64], lhsT=Gg[:, r, :], rhs=xdt[:, h, :],
                                                                          start=True, stop=True), [B[gk], B["xdt"]], [bA])
                    pB, bB = bank()
                    kb.op("pe", lambda g=g, pB=pB: nc.tensor.matmul(pB[:], lhsT=xc[:, 20 + g, :], rhs=Sb[:, g * 512:(g + 1) * 512],
                                                                 start=True, stop=True), [B["xc"], B["Sb"]], [bB])
                    kb.op("dve", lambda g=g, pB=pB: nc.vector.tensor_tensor(
                        out=tg[:].rearrange("p (r d) -> p r d", r=8), in0=pB[:].rearrange("p (r d) -> p r d", r=8),
                        in1=eacs[:, g * 8:(g + 1) * 8].unsqueeze(2).to_broadcast([128, 8, 64]), op=ALU.mult), [bB, B["eacs"]], [B["tg"]])
                    kb.op("dve", lambda pA=pA: nc.vector.tensor_tensor(out=yg[:], in0=pA[:], in1=tg[:], op=ALU.add), [bA, B["tg"]], [B["yg"]])
                    kb.op("pool", lambda g=g: nc.gpsimd.tensor_tensor(
                        out=tg[:].rearrange("p (r d) -> p r d", r=8), in0=xstok[:, g * 8:(g + 1) * 8, :],
                        in1=dsk[:, g * 8:(g + 1) * 8].unsqueeze(2).to_broadcast([128, 8, 64]), op=ALU.mult),
                        [B["xstok"], b_prm, B["yg"]], [B["tg"]])
                    kb.op("dve", lambda: nc.vector.tensor_tensor(out=yg[:], in0=yg[:], in1=tg[:], op=ALU.add), [B["tg"], B["yg"]], [B["yg"]])
                    kb.op("dve", lambda g=g: nc.vector.tensor_tensor(out=yg[:], in0=yg[:], in1=zs[:, g * 512:(g + 1) * 512], op=ALU.mult),
                          [B["zs"], B["yg"]], [B["yg"]])
                    kb.op("act", lambda g=g: nc.scalar.activation(out=tg[:], in_=yg[:], func=AF.Square, accum_out=sm[:, 4 + g:5 + g]),
                          [B["yg"]], [B["tg"], B["sm"]])
                    rstd_from(sm[:, 4 + g:5 + g], sm[:, 8 + g:9 + g], 512, [B["sm"]])
                    kb.op("dve", lambda g=g: nc.vector.tensor_scalar(out=ygb[:, g * 512:(g + 1) * 512], in0=yg[:], scalar1=sm[:, 8 + g:9 + g],
                                                                   scalar2=None, op0=ALU.mult), [B["yg"], B["sm"]], [B["ygb"]])
                for g in range(4):
                    kb.op("dve", lambda g=g: nc.vector.tensor_tensor(out=xdtd[:], in0=xdt[:, g * 8:(g + 1) * 8, :],
                                                                     in1=decs[:, g * 8:(g + 1) * 8].unsqueeze(2).to_broadcast([128, 8, 64]),
                                                                     op=ALU.mult), [B["xdt"], B["decs"]], [B["xdtd"]])
                    p, bp = bank()
                    kb.op("pe", lambda g=g, p=p: nc.tensor.matmul(p[:], lhsT=btok[:, g, :], rhs=xdtd[:].rearrange("p r d -> p (r d)"),
                                                               start=True, stop=True), [B["btok"], B["xdtd"]], [bp])
                    kb.op("dve", lambda g=g: nc.vector.tensor_tensor(
                        out=S[:, g * 512:(g + 1) * 512].rearrange("p (r d) -> p r d", r=8),
                        in0=S[:, g * 512:(g + 1) * 512].rearrange("p (r d) -> p r d", r=8),
                        in1=cd[:, g * 8:(g + 1) * 8].unsqueeze(2).to_broadcast([128, 8, 64]), op=ALU.mult), [B["S"], B["cd"], B["Sb"]], [B["S"]])
                    kb.op("dve", lambda g=g, p=p: nc.vector.tensor_tensor(out=S[:, g * 512:(g + 1) * 512], in0=S[:, g * 512:(g + 1) * 512],
                                                                        in1=p[:], op=ALU.add), [bp, B["S"]], [B["S"]])
                kb.op("act", lambda: nc.scalar.copy(out=Sb[:], in_=S[:]), [B["S"]], [B["Sb"]])
                if last:
                    kb.dma("sp", ssms_o[seq - 1] if is_s else ssmp_o[:, :], S[:], reads=[B["S"]])
                for half in range(2):
                    p, bp = bank()
                    pb = p[:].bitcast(BF16)
                    for i in range(8):
                        kk = half * 8 + i
                        kb.op("pe", lambda i=i, kk=kk, pb=pb: nc.tensor.transpose(out=pb[:, i * 128:(i + 1) * 128], in_=ygb[:, kk * 128:(kk + 1) * 128],
                                                                                identity=identb[:]), [B["ygb"], b_misc], [bp])
                    kb.op("act", lambda half=half, pb=pb: nc.scalar.copy(out=ygT[:, half * 8:(half + 1) * 8, :].rearrange("p k t -> p (k t)"), in_=pb),
                          [bp], [B["ygT"]])
                for blk in range(2):
                    p, bp = bank()
                    for k in range(16):
                        kb.op("pe", lambda k=k, blk=blk, p=p: nc.tensor.matmul(p[:], lhsT=ygT[:, k, :], rhs=wouta[:, k, blk * 512:(blk + 1) * 512],
                                                                           start=(k == 0), stop=(k == 15)), [B["ygT"], b_wouta], [bp])
                    kb.op("dve", lambda blk=blk, p=p: nc.vector.tensor_tensor(out=xt[:, blk * 512:(blk + 1) * 512], in0=xt[:, blk * 512:(blk + 1) * 512],
                                                                            in1=p[:], op=ALU.add), [bp, B["xt"]], [B["xt"]])
                idx = c if not is_s else NCH + seq - 1
                kb.dma("sp", h1_d[idx], xt[:], reads=[B["xt"]])

            for c in range(NCH):
                chunk(0, c, c == 0, c == NCH - 1)
            for s in range(NS):
                chunk(1 + s, 0, True, True)
            kb.barrier()
        import os
        if os.environ.get("DBG_STOP") == "A":
            return nc

        esR = ExitStack()
        KT_all = esR.enter_context(nc.sbuf_tensor("r_KT", [128, NCH, 8, 128], BF16))
        V_all = esR.enter_context(nc.sbuf_tensor("r_V", [128, NCH, D], BF16))
        ksqr = esR.enter_context(nc.sbuf_tensor("r_ksq", [128, 16], F32))
        b_KT = [Buf() for _ in range(NCH)]
        b_V = [Buf() for _ in range(NCH)]
        b_ksqr = Buf()
        kb.op("dve", lambda: nc.vector.memset(ksqr[:], 0.0), [], [b_ksqr])
        with ExitStack() as esB:
            EB = esB.enter_context

            def sbb(name, shape, dt=F32):
                return EB(nc.sbuf_tensor("b_" + name, list(shape), dt))

            wkv = sbb("wkv", [128, 8, 2 * D], BF16)
            b_wkv = Buf()
            for k in range(8):
                kb.dma("pool", wkv[:, k, :], wkv_d[k * 128:(k + 1) * 128, :], writes=[b_wkv])
            for k in range(8):
                kb.op("dve", lambda k=k: nc.vector.tensor_scalar(out=wkv[:, k, :], in0=wkv[:, k, :], scalar1=nkv[:, k:k + 1],
                                                               scalar2=None, op0=ALU.mult), [b_wkv, b_prm], [b_wkv])
            ht = sbb("ht", [128, D])
            junk2 = sbb("junk2", [128, D])
            hsb = sbb("hsb", [128, D], BF16)
            hnT = sbb("hnT", [128, 8, 128], BF16)
            kt = sbb("kt", [128, 16, 64])
            vt = sbb("vt", [128, D])
            cs = sbb("cs", [128, 16])
            rt = sbb("rt", [128, 4, 16, 8])
            sm2 = sbb("sm2", [128, 4])
            zt = sbb("zt", [128, D])
            ktb = sbb("ktb", [128, D], BF16)
            ksq = sbb("ksq", [128, 16])
            BB = {n: Buf(n) for n in ["ht", "junk2", "hsb", "hnT", "kt", "vt", "cs", "rt", "sm2", "zt", "ktb", "ksq"]}
            kb.op("dve", lambda: nc.vector.memset(zt[:], 0.0), [], [BB["zt"]])

            def kvchunk(idx, is_s, c):
                kb.dma("sp", ht[:], h1_d[idx], writes=[BB["ht"]])
                if is_s:
                    kb.dma("sp", cs[:, 0:8], coss_d[:, :], writes=[BB["cs"]])
                    kb.dma("sp", cs[:, 8:16], sins_d[:, :], writes=[BB["cs"]])
                else:
                    kb.dma("sp", cs[:, 0:8], cosp_d[c * 128:(c + 1) * 128, :], writes=[BB["cs"]])
                    kb.dma("sp", cs[:, 8:16], sinp_d[c * 128:(c + 1) * 128, :], writes=[BB["cs"]])
                kb.op("act", lambda: nc.scalar.activation(out=junk2[:], in_=ht[:], func=AF.Square, accum_out=sm2[:, 0:1]),
                      [BB["ht"]], [BB["junk2"], BB["sm2"]])
                act_fence([BB["sm2"]])
                kb.op("act", lambda: nc.scalar.activation(out=sm2[:, 1:2], in_=sm2[:, 0:1], func=AF.Ln, scale=1.0 / D, bias=epsc[:, 0:1]),
                      [BB["sm2"], b_misc], [BB["sm2"]])
                kb.op("act", lambda: nc.scalar.activation(out=sm2[:, 1:2], in_=sm2[:, 1:2], func=AF.Exp, scale=-0.5), [BB["sm2"]], [BB["sm2"]])
                kb.op("dve", lambda: nc.vector.tensor_scalar(out=hsb[:], in0=ht[:], scalar1=sm2[:, 1:2], scalar2=None, op0=ALU.mult),
                      [BB["ht"], BB["sm2"]], [BB["hsb"]])
                p, bp = bank()
                pb = p[:].bitcast(BF16)
                for k in range(8):
                    kb.op("pe", lambda k=k, pb=pb: nc.tensor.transpose(out=pb[:, k * 128:(k + 1) * 128], in_=hsb[:, k * 128:(k + 1) * 128],
                                                                     identity=identb[:]), [BB["hsb"], b_misc], [bp])
                kb.op("act", lambda pb=pb: nc.scalar.copy(out=hnT[:].rearrange("p k t -> p (k t)"), in_=pb), [bp], [BB["hnT"]])
                for blk in range(4):
                    p, bp = bank()
                    for k in range(8):
                        kb.op("pe", lambda k=k, blk=blk, p=p: nc.tensor.matmul(p[:], lhsT=hnT[:, k, :], rhs=wkv[:, k, blk * 512:(blk + 1) * 512],
                                                                           start=(k == 0), stop=(k == 7)), [BB["hnT"], b_wkv], [bp])
                    if blk < 2:
                        kb.op("act", lambda blk=blk, p=p: nc.scalar.copy(out=kt[:, blk * 8:(blk + 1) * 8, :].rearrange("p a d -> p (a d)"), in_=p[:]),
                              [bp], [BB["kt"]])
                    else:
                        kb.op("act", lambda blk=blk, p=p: nc.scalar.copy(out=vt[:, (blk - 2) * 512:(blk - 1) * 512], in_=p[:]), [bp], [BB["vt"]])
                cosb = cs[:, 0:8].unsqueeze(1).to_broadcast([128, 16, 8])
                sinb = cs[:, 8:16].unsqueeze(1).to_broadcast([128, 16, 8])
                x1 = kt[:, :, 0:8]
                x2 = kt[:, :, 8:16]
                kb.op("dve", lambda: nc.vector.tensor_tensor(out=rt[:, 0], in0=x1, in1=cosb, op=ALU.mult), [BB["kt"], BB["cs"]], [BB["rt"]])
                kb.op("dve", lambda: nc.vector.tensor_tensor(out=rt[:, 1], in0=x2, in1=sinb, op=ALU.mult), [BB["kt"], BB["cs"]], [BB["rt"]])
                kb.op("dve", lambda: nc.vector.tensor_tensor(out=rt[:, 2], in0=x2, in1=cosb, op=ALU.mult), [BB["kt"], BB["cs"]], [BB["rt"]])
                kb.op("dve", lambda: nc.vector.tensor_tensor(out=rt[:, 3], in0=x1, in1=sinb, op=ALU.mult), [BB["kt"], BB["cs"]], [BB["rt"]])
                kb.op("dve", lambda: nc.vector.tensor_tensor(out=x1, in0=rt[:, 0], in1=rt[:, 1], op=ALU.subtract), [BB["rt"]], [BB["kt"]])
                kb.op("dve", lambda: nc.vector.tensor_tensor(out=x2, in0=rt[:, 2], in1=rt[:, 3], op=ALU.add), [BB["rt"]], [BB["kt"]])
                ktf = kt[:].rearrange("p a d -> p (a d)")
                if is_s:
                    s = idx - NCH
                    kb.dma("sp", ks_o[s:s + 1, :], ktf[0:1, :], reads=[BB["kt"]])
                    kb.dma("sp", vs_o[s:s + 1, :], vt[0:1, :], reads=[BB["vt"]])
                else:
                    kb.dma("sp", kp_o[c * 128:(c + 1) * 128, :], ktf, reads=[BB["kt"]])
                    kb.dma("sp", vp_o[c * 128:(c + 1) * 128, :], vt[:], reads=[BB["vt"]])
                    kb.op("pool", lambda: nc.gpsimd.tensor_copy(out=V_all[:, c, :], in_=vt[:]), [BB["vt"]], [b_V[c]])
                    kb.op("act", lambda: nc.scalar.copy(out=ktb[:], in_=ktf), [BB["kt"]], [BB["ktb"]])
                    kb.op("dve", lambda: nc.vector.tensor_tensor(out=junk2[:], in0=ktf, in1=ktf, op=ALU.mult), [BB["kt"]], [BB["junk2"]])
                    kb.op("dve", lambda: nc.vector.tensor_reduce(out=ksq[:], in_=junk2[:].rearrange("p (a d) -> p a d", a=16), axis=AX.X, op=ALU.add),
                          [BB["junk2"]], [BB["ksq"]])
                    kb.op("dve", lambda: nc.vector.tensor_tensor(out=ksqr[:], in0=ksqr[:], in1=ksq[:], op=ALU.max), [BB["ksq"], b_ksqr], [b_ksqr])
                    p, bp = bank()
                    pb = p[:].bitcast(BF16)
                    for h in range(8):
                        kb.op("pe", lambda h=h, pb=pb: nc.tensor.transpose(out=pb[:, h * 128:(h + 1) * 128], in_=ktb[:, h * 128:(h + 1) * 128],
                                                                         identity=identb[:]), [BB["ktb"], b_misc], [bp])
                    kb.op("act", lambda pb=pb: nc.scalar.copy(out=KT_all[:, c, :, :].rearrange("p h t -> p (h t)"), in_=pb), [bp], [b_KT[c]])

            for c in range(NCH):
                kvchunk(c, False, c)
            for s in range(NS):
                kvchunk(NCH + s, True, 0)
            kb.barrier()

        if os.environ.get("DBG_STOP") == "B":
            return nc
        LAM_INIT = 0.8 - 0.6 * float(np.exp(-0.3 * 1))
        def phaseC(mode):
          is_sm = mode == "sample"
          with ExitStack() as esC:
            EC = esC.enter_context

            def sc(name, shape, dt=F32):
                return EC(nc.sbuf_tensor(("d_" if is_sm else "c_") + name, list(shape), dt))

            winb = sc("winb", [128, 8, 2 * D], BF16)
            woutb = sc("woutb", [128, 8, D], BF16)
            nbt = sc("nbt", [128, 8])
            b_wb = Buf()
            kb.dma("sp", nbt[:], nb_d[:, :], writes=[b_wb])
            for k in range(8):
                kb.dma("pool", winb[:, k, :], winb_d[k * 128:(k + 1) * 128, :], writes=[b_wb])
            for k in range(8):
                kb.dma("pool", woutb[:, k, :], woutb_d[k * 128:(k + 1) * 128, :], writes=[b_wb])
            for k in range(8):
                kb.op("dve", lambda k=k: nc.vector.tensor_scalar(out=winb[:, k, :], in0=winb[:, k, :], scalar1=nbt[:, k:k + 1],
                                                               scalar2=None, op0=ALU.mult), [b_wb], [b_wb])
            lamt = sc("lamt", [128, 4, 64])
            lsm = sc("lsm", [128, 8])
            sgb = sc("sgb", [128, 128])
            nfb = sc("nfb", [128, D])
            onesb = sc("onesb", [128, 2], BF16)
            b_c0 = Buf()
            for i in range(4):
                kb.dma("sp", lamt[:, i, :], lam_d[i].partition_broadcast(128), writes=[b_c0])
            kb.dma("sp", sgb[:], subln_d.partition_broadcast(128), writes=[b_c0])
            kb.dma("sp", nfb[:], nf_d.partition_broadcast(128), writes=[b_c0])
            kb.op("dve", lambda: nc.vector.memset(onesb[:], 1.0), [], [b_c0])
            kb.op("dve", lambda: nc.vector.tensor_tensor(out=lamt[:, 0, :], in0=lamt[:, 0, :], in1=lamt[:, 1, :], op=ALU.mult), [b_c0], [b_c0])
            kb.op("dve", lambda: nc.vector.tensor_tensor(out=lamt[:, 2, :], in0=lamt[:, 2, :], in1=lamt[:, 3, :], op=ALU.mult), [b_c0], [b_c0])
            kb.op("dve", lambda: nc.vector.tensor_reduce(out=lsm[:, 0:1], in_=lamt[:, 0, :], axis=AX.X, op=ALU.add), [b_c0], [b_c0])
            kb.op("dve", lambda: nc.vector.tensor_reduce(out=lsm[:, 1:2], in_=lamt[:, 2, :], axis=AX.X, op=ALU.add), [b_c0], [b_c0])
            kb.op("act", lambda: nc.scalar.activation(out=lsm[:, 2:4], in_=lsm[:, 0:2], func=AF.Exp), [b_c0], [b_c0])
            kb.op("dve", lambda: nc.vector.tensor_tensor(out=lsm[:, 4:5], in0=lsm[:, 3:4], in1=lsm[:, 2:3], op=ALU.subtract), [b_c0], [b_c0])
            kb.op("dve", lambda: nc.vector.tensor_scalar(out=lsm[:, 4:5], in0=lsm[:, 4:5], scalar1=-LAM_INIT, scalar2=None, op0=ALU.add), [b_c0], [b_c0])
            kb.op("dve", lambda: nc.vector.tensor_scalar(out=sgb[:], in0=sgb[:], scalar1=1.0 - LAM_INIT, scalar2=None, op0=ALU.mult), [b_c0], [b_c0])
            neglam = lsm[:, 4:5]

            ht = sc("ht", [128, D])
            jk = sc("jk", [128, D])
            hsb = sc("hsb", [128, D], BF16)
            hnT = sc("hnT", [128, 8, 128], BF16)
            qt = sc("qt", [128, 16, 64])
            gate = sc("gate", [128, D])
            cs = sc("cs", [128, 16])
            rt = sc("rt", [128, 4, 16, 8])
            qb = sc("qb", [128, D], BF16)
            qT = sc("qT", [128, 8, 128], BF16)
            sq = sc("sq", [128, 64])
            cb16 = sc("cb16", [16, 24])
            dg = sc("dg", [16, 16])
            cbias = sc("cbias", [128, 16])
            NPB = 3
            Pb = [sc("P%d" % i, [128, 512], BF16) for i in range(NPB)]
            b_P = [Buf() for _ in range(NPB)]
            pidx = [0]
            osb = sc("osb", [128, 8, 128])
            t1 = sc("t1", [128, 8, 128])
            ob = sc("ob", [128, D], BF16)
            oT = sc("oT", [128, 8, 128], BF16)
            CB = {n: Buf(n) for n in ["ht", "jk", "hsb", "hnT", "qt", "gate", "cs", "rt", "qb", "qT", "sq", "cb16", "dg", "cbias",
                                      "osb", "t1", "ob", "oT"]}
            NAMES = ["ht", "jk", "hsb", "hnT", "qt", "gate", "cs", "rt", "qb", "qT", "sq", "cb16", "dg", "cbias", "osb", "t1", "ob", "oT"]
            SET0 = dict(zip(NAMES, [ht, jk, hsb, hnT, qt, gate, cs, rt, qb, qT, sq, cb16, dg, cbias, osb, t1, ob, oT]))
            if is_sm:
                SETS = [SET0, SET0]
                CBS = [CB, CB]
            else:
                SHP = dict(ht=([128, D], F32), jk=([128, D], F32), hsb=([128, D], BF16), hnT=([128, 8, 128], BF16), qt=([128, 16, 64], F32),
                           gate=([128, D], F32), cs=([128, 16], F32), rt=([128, 4, 16, 8], F32), qb=([128, D], BF16), qT=([128, 8, 128], BF16),
                           sq=([128, 64], F32), cb16=([16, 24], F32), dg=([16, 16], F32), cbias=([128, 16], F32), osb=([128, 8, 128], F32),
                           t1=([128, 8, 128], F32), ob=([128, D], BF16), oT=([128, 8, 128], BF16))
                SET1 = {n_: sc(n_ + "_2", SHP[n_][0], SHP[n_][1]) for n_ in NAMES}
                SETS = [SET0, SET1]
                CBS = [CB, {n_: Buf(n_) for n_ in NAMES}]
            rot = [5]

            def rbank():
                i = rot[0]
                rot[0] = 5 + (i - 5 + 1) % 3
                return ps[i], b_ps[i]

            if not is_sm:
                p, bp = rbank()
                kb.op("pe", lambda p=p: nc.tensor.transpose(out=p[0:16, 0:128], in_=ksqr[:], identity=identf), [b_ksqr, b_cst], [bp])
                for par_ in range(2):
                    kb.op("dve", lambda p=p, par_=par_: nc.vector.tensor_reduce(out=SETS[par_]["cb16"][:, 0:1], in_=p[0:16, 0:128], axis=AX.X, op=ALU.max),
                          [bp], [CBS[par_]["cb16"]])
            else:
                I32 = mybir.dt.int32
                c2 = sc("c2", [128, 652])
                b_c2 = Buf()
                kb.dma("sp", c2[:], cst2_d[:, :], writes=[b_c2])
                E40R = c2[:, 0:640].rearrange("p (h c) -> p h c", h=8)
                M40 = c2[0:40, 640:648]
                iotac = c2[:, 648:649]
                A40 = c2[0:40, 649:650]
                B40 = c2[0:40, 650:651]
                pti = sc("pti", [128, NS * NP], I32)
                ptf = sc("ptf", [128, NS * NP])
                idx = sc("idx", [128, NS * NP], I32)
                b_idx = Buf()
                kb.dma("sp", pti[:], pt_d.partition_broadcast(128), writes=[b_idx])
                kb.op("dve", lambda: nc.vector.tensor_copy(out=ptf[:], in_=pti[:]), [b_idx], [b_idx])
                kb.op("dve", lambda: nc.vector.tensor_scalar(out=ptf[:], in0=ptf[:], scalar1=128.0, scalar2=iotac, op0=ALU.mult, op1=ALU.add),
                      [b_idx, b_c2], [b_idx])
                kb.op("dve", lambda: nc.vector.tensor_copy(out=idx[:], in_=ptf[:]), [b_idx], [b_idx])
                lamcol = sc("lamcol", [40, 1])
                kb.op("dve", lambda: nc.vector.scalar_tensor_tensor(out=lamcol[:], in0=B40, scalar=lsm[0:40, 4:5], in1=A40, op0=ALU.mult, op1=ALU.add),
                      [b_c2, b_c0], [b_c2])
                qcol = sc("qcol", [128, 8])
                QB = sc("QB", [128, 8, 80])
                Sall = sc("Sall", [40, NT * 128])
                PnT = sc("PnT", [128, NT, 40])
                GP = 2
                NPG = 4
                pg = [sc("pg%d" % i, [128, GP, D]) for i in range(NPG)]
                b_pg = [[Buf() for _ in range(GP)] for _ in range(NPG)]
                pgb = [sc("pgb%d" % i, [128, GP, D], BF16) for i in range(2)]
                b_pgb = [Buf(), Buf()]
                QBb = sc("QBb", [128, 8, 80], BF16)
                KTs = sc("KTs", [128, 8, 128])
                Vs = sc("Vs", [128, D])
                tmp40 = sc("tmp40", [40, D])
                st40 = sc("st40", [40, 8])
                SB = {n: Buf(n) for n in ["qcol", "QB", "QBb", "Sall", "PnT", "KTs", "Vs", "tmp40", "st40"]}

            def rstd_c(ssq_ap, out_ap, n, bufs):
                act_fence(bufs)
                kb.op("act", lambda: nc.scalar.activation(out=out_ap, in_=ssq_ap, func=AF.Ln, scale=1.0 / n, bias=epsc[:, 0:1]), bufs + [b_misc], bufs)
                kb.op("act", lambda: nc.scalar.activation(out=out_ap, in_=out_ap, func=AF.Exp, scale=-0.5), bufs, bufs)

            def part1(i):
                S_ = SETS[i % 2]
                CB = CBS[i % 2]
                ht, jk, hsb, hnT, qt, gate, cs, rt, qb, qT, sq, cb16, dg, cbias, osb, t1, ob, oT = [S_[n_] for n_ in NAMES]
                kb.dma("sp", ht[:], h1_d[(NCH + i) if is_sm else i], writes=[CB["ht"]])
                if is_sm:
                    kb.dma("sp", cs[:, 0:8], coss_d[:, :], writes=[CB["cs"]])
                    kb.dma("sp", cs[:, 8:16], sins_d[:, :], writes=[CB["cs"]])
                else:
                    kb.dma("sp", cs[:, 0:8], cosp_d[i * 128:(i + 1) * 128, :], writes=[CB["cs"]])
                    kb.dma("sp", cs[:, 8:16], sinp_d[i * 128:(i + 1) * 128, :], writes=[CB["cs"]])
                yield
                kb.op("act", lambda: nc.scalar.activation(out=jk[:], in_=ht[:], func=AF.Square, accum_out=sq[:, 0:1]), [CB["ht"]], [CB["jk"], CB["sq"]])
                rstd_c(sq[:, 0:1], sq[:, 1:2], D, [CB["sq"]])
                yield
                kb.op("dve", lambda: nc.vector.tensor_scalar(out=hsb[:], in0=ht[:], scalar1=sq[:, 1:2], scalar2=None, op0=ALU.mult),
                      [CB["ht"], CB["sq"]], [CB["hsb"]])
                p, bp = rbank()
                pb = p[:].bitcast(BF16)
                for k in range(8):
                    kb.op("pe", lambda k=k, pb=pb: nc.tensor.transpose(out=pb[:, k * 128:(k + 1) * 128], in_=hsb[:, k * 128:(k + 1) * 128],
                                                                     identity=identb[:]), [CB["hsb"], b_misc], [bp])
                kb.op("act", lambda pb=pb: nc.scalar.copy(out=hnT[:].rearrange("p k t -> p (k t)"), in_=pb), [bp], [CB["hnT"]])
                yield
                for blk in range(4):
                    yield
                    p, bp = rbank()
                    for k in range(8):
                        kb.op("pe", lambda k=k, blk=blk, p=p: nc.tensor.matmul(p[:], lhsT=hnT[:, k, :], rhs=winb[:, k, blk * 512:(blk + 1) * 512],
                                                                           start=(k == 0), stop=(k == 7)), [CB["hnT"], b_wb], [bp])
                    if blk < 2:
                        kb.op("act", lambda blk=blk, p=p: nc.scalar.copy(out=qt[:, blk * 8:(blk + 1) * 8, :].rearrange("p a d -> p (a d)"), in_=p[:]),
                              [bp], [CB["qt"]])
                    else:
                        kb.op("act", lambda blk=blk, p=p: nc.scalar.activation(out=gate[:, (blk - 2) * 512:(blk - 1) * 512], in_=p[:], func=AF.Silu),
                              [bp], [CB["gate"]])
                yield
                cosb = cs[:, 0:8].unsqueeze(1).to_broadcast([128, 16, 8])
                sinb = cs[:, 8:16].unsqueeze(1).to_broadcast([128, 16, 8])
                x1 = qt[:, :, 0:8]
                x2 = qt[:, :, 8:16]
                kb.op("dve", lambda: nc.vector.tensor_tensor(out=rt[:, 0], in0=x1, in1=cosb, op=ALU.mult), [CB["qt"], CB["cs"]], [CB["rt"]])
                kb.op("dve", lambda: nc.vector.tensor_tensor(out=rt[:, 1], in0=x2, in1=sinb, op=ALU.mult), [CB["qt"], CB["cs"]], [CB["rt"]])
                kb.op("dve", lambda: nc.vector.tensor_tensor(out=rt[:, 2], in0=x2, in1=cosb, op=ALU.mult), [CB["qt"], CB["cs"]], [CB["rt"]])
                kb.op("dve", lambda: nc.vector.tensor_tensor(out=rt[:, 3], in0=x1, in1=sinb, op=ALU.mult), [CB["qt"], CB["cs"]], [CB["rt"]])
                kb.op("dve", lambda: nc.vector.tensor_tensor(out=x1, in0=rt[:, 0], in1=rt[:, 1], op=ALU.subtract), [CB["rt"]], [CB["qt"]])
                kb.op("dve", lambda: nc.vector.tensor_tensor(out=x2, in0=rt[:, 2], in1=rt[:, 3], op=ALU.add), [CB["rt"]], [CB["qt"]])
                yield

            def prompt_attn(i):
                S_ = SETS[i % 2]
                CB = CBS[i % 2]
                ht, jk, hsb, hnT, qt, gate, cs, rt, qb, qT, sq, cb16, dg, cbias, osb, t1, ob, oT = [S_[n_] for n_ in NAMES]
                qtf = qt[:].rearrange("p a d -> p (a d)")
                kb.op("dve", lambda: nc.vector.tensor_tensor(out=jk[:], in0=qtf, in1=qtf, op=ALU.mult), [CB["qt"]], [CB["jk"]])
                kb.op("dve", lambda: nc.vector.tensor_reduce(out=sq[:, 16:32], in_=jk[:].rearrange("p (a d) -> p a d", a=16), axis=AX.X, op=ALU.add),
                      [CB["jk"]], [CB["sq"]])
                p, bp = rbank()
                kb.op("pe", lambda p=p: nc.tensor.transpose(out=p[0:16, 0:128], in_=sq[:, 16:32], identity=identf), [CB["sq"], b_cst], [bp])
                kb.op("dve", lambda p=p: nc.vector.tensor_reduce(out=cb16[:, 1:2], in_=p[0:16, 0:128], axis=AX.X, op=ALU.max), [bp], [CB["cb16"]])
                kb.op("dve", lambda: nc.vector.tensor_tensor(out=cb16[:, 2:3], in0=cb16[:, 0:1], in1=cb16[:, 1:2], op=ALU.mult), [CB["cb16"]], [CB["cb16"]])
                kb.op("act", lambda: nc.scalar.activation(out=cb16[:, 3:4], in_=cb16[:, 2:3], func=AF.Ln), [CB["cb16"]], [CB["cb16"]])
                kb.op("act", lambda: nc.scalar.activation(out=cb16[:, 3:4], in_=cb16[:, 3:4], func=AF.Exp, scale=0.5), [CB["cb16"]], [CB["cb16"]])
                kb.op("dve", lambda: nc.vector.tensor_scalar(out=dg[:], in0=identf[0:16, 0:16], scalar1=cb16[:, 3:4], scalar2=-0.125,
                                                            op0=ALU.mult, op1=ALU.mult), [CB["cb16"], b_cst], [CB["dg"]])
                p, bp = rbank()
                kb.op("pe", lambda p=p: nc.tensor.matmul(p[:, 0:16], lhsT=onesf[0:16, :], rhs=dg[:], start=True, stop=True), [CB["dg"], b_misc], [bp])
                kb.op("act", lambda p=p: nc.scalar.copy(out=cbias[:], in_=p[:, 0:16]), [bp], [CB["cbias"]])
                kb.op("act", lambda: nc.scalar.copy(out=qb[:], in_=qtf), [CB["qt"]], [CB["qb"]])
                p, bp = rbank()
                pb = p[:].bitcast(BF16)
                for h in range(8):
                    kb.op("pe", lambda h=h, pb=pb: nc.tensor.transpose(out=pb[:, h * 128:(h + 1) * 128], in_=qb[:, h * 128:(h + 1) * 128],
                                                                     identity=identb[:]), [CB["qb"], b_misc], [bp])
                kb.op("act", lambda pb=pb: nc.scalar.copy(out=qT[:].rearrange("p h t -> p (h t)"), in_=pb), [bp], [CB["qT"]])
                for b in range(5):
                    kb.op("dve", lambda b=b: nc.vector.memset(ps[b][:], 0.0), [], [b_ps[b]])
                tiles = list(range(i + 1))
                for g0 in range(0, len(tiles), 4):
                    grp = tiles[g0:g0 + 4]
                    n = len(grp)
                    for hj in range(16):
                        h, j = hj // 2, hj % 2
                        p, bp = rbank()
                        for tt, t in enumerate(grp):
                            kb.op("pe", lambda tt=tt, t=t, p=p, h=h, j=j: nc.tensor.matmul(
                                p[:, tt * 128:(tt + 1) * 128], lhsT=KT_all[64 * j:64 * j + 64, t, h, :], rhs=qT[64 * j:64 * j + 64, h, :],
                                start=True, stop=True), [b_KT[t], CB["qT"]], [bp])
                        pi = pidx[0]
                        pidx[0] = (pi + 1) % NPB
                        P, bP = Pb[pi], b_P[pi]
                        kb.op("act", lambda p=p, P=P, n=n, hj=hj: nc.scalar.activation(out=P[:, 0:n * 128], in_=p[:, 0:n * 128], func=AF.Exp,
                                                                                    scale=0.125, bias=cbias[:, hj:hj + 1]), [bp, CB["cbias"]], [bP])
                        if grp[-1] == i:
                            tt = n - 1
                            kb.op("dve", lambda P=P, tt=tt: nc.vector.tensor_tensor(out=P[:, tt * 128:(tt + 1) * 128], in0=P[:, tt * 128:(tt + 1) * 128],
                                                                                 in1=tri, op=ALU.mult), [bP, b_cst], [bP])
                        ob_, oc = hj // 4, (hj % 4) * 128
                        for tt, t in enumerate(grp):
                            kb.op("pe", lambda tt=tt, t=t, P=P, h=h, ob_=ob_, oc=oc: nc.tensor.matmul(
                                ps[ob_][:, oc:oc + 128], lhsT=P[:, tt * 128:(tt + 1) * 128], rhs=V_all[:, t, h * 128:(h + 1) * 128],
                                start=False, stop=False, skip_group_check=True), [bP, b_V[t]], [b_ps[ob_]])
                            kb.op("pe", lambda tt=tt, P=P, hj=hj: nc.tensor.matmul(
                                ps[4][:, hj:hj + 1], lhsT=P[:, tt * 128:(tt + 1) * 128], rhs=onesb[:, 0:1],
                                start=False, stop=False, skip_group_check=True), [bP, b_c0], [b_ps[4]])
                        yield
                kb.op("dve", lambda: nc.vector.reciprocal(out=sq[:, 32:48], in_=ps[4][:, 0:16]), [b_ps[4]], [CB["sq"]])
                rz = sq[:, 32:48].rearrange("p (h j) -> p h j", j=2)
                kb.op("dve", lambda: nc.vector.tensor_scalar(out=rz[:, :, 1], in0=rz[:, :, 1], scalar1=neglam, scalar2=None, op0=ALU.mult),
                      [CB["sq"], b_c0], [CB["sq"]])
                for b in range(4):
                    ov = ps[b][:].rearrange("p (h j d) -> p h j d", h=2, j=2)
                    kb.op("dve", lambda b=b, ov=ov: nc.vector.tensor_tensor(out=osb[:, 2 * b:2 * b + 2, :], in0=ov[:, :, 0, :],
                                                                          in1=rz[:, 2 * b:2 * b + 2, 0:1].to_broadcast([128, 2, 128]), op=ALU.mult),
                          [b_ps[b], CB["sq"]], [CB["osb"]])
                    kb.op("dve", lambda b=b, ov=ov: nc.vector.tensor_tensor(out=t1[:, 2 * b:2 * b + 2, :], in0=ov[:, :, 1, :],
                                                                          in1=rz[:, 2 * b:2 * b + 2, 1:2].to_broadcast([128, 2, 128]), op=ALU.mult),
                          [b_ps[b], CB["sq"]], [CB["t1"]])
                kb.op("dve", lambda: nc.vector.tensor_tensor(out=osb[:], in0=osb[:], in1=t1[:], op=ALU.add), [CB["osb"], CB["t1"]], [CB["osb"]])

            def sample_attn(b):
                S_ = SETS[b % 2]
                CB = CBS[b % 2]
                ht, jk, hsb, hnT, qt, gate, cs, rt, qb, qT, sq, cb16, dg, cbias, osb, t1, ob, oT = [S_[n_] for n_ in NAMES]
                qtf = qt[:].rearrange("p a d -> p (a d)")
                osbf = osb[:].rearrange("p h d -> p (h d)")
                for half in range(2):
                    p, bp = rbank()
                    for hh in range(4):
                        h = half * 4 + hh
                        kb.op("pe", lambda hh=hh, h=h, p=p: nc.tensor.transpose(out=p[:, hh * 128:(hh + 1) * 128], in_=qtf[:, h * 128:(h + 1) * 128],
                                                                               identity=identf), [CB["qt"], b_cst], [bp])
                    kb.op("act", lambda half=half, p=p: nc.scalar.copy(out=qcol[:, half * 4:(half + 1) * 4],
                                                                      in_=p[:].rearrange("p (h t) -> p h t", h=4)[:, :, 0]), [bp], [SB["qcol"]])
                kb.op("dve", lambda: nc.vector.tensor_tensor(out=QB[:], in0=E40R, in1=qcol[:].unsqueeze(2).to_broadcast([128, 8, 80]), op=ALU.mult),
                      [SB["qcol"], b_c2], [SB["QB"]])
                kb.op("pool", lambda: nc.gpsimd.memset(KTs[:], 0.0), [], [SB["KTs"]])
                kb.op("pool", lambda: nc.gpsimd.memset(Vs[:], 0.0), [], [SB["Vs"]])
                with nc.allow_non_contiguous_dma(reason="tiny feature-major column load"):
                    kb.dma("sp", KTs[:, :, 0], ks_o[b].rearrange("(h p) -> p h", p=128), writes=[SB["KTs"]])
                kb.dma("sp", Vs[0:1, :], vs_o[b:b + 1, :], writes=[SB["Vs"]])
                kb.op("act", lambda: nc.scalar.copy(out=QBb[:], in_=QB[:]), [SB["QB"]], [SB["QBb"]])
                groups = [list(range(g0, min(g0 + GP, NP))) for g0 in range(0, NP, GP)]
                import os
                if os.environ.get('DBG_NOPAGES'):
                    groups = []
                    kb.op('dve', lambda: nc.vector.memset(Sall[:], -1e30), [], [SB['Sall']])
                    kb.op('dve', lambda: nc.vector.memset(PnT[:], 0.0), [], [SB['PnT']])
                gi = [0]

                def load_group(grp, src_d):
                    k = gi[0] % NPG
                    gi[0] += 1
                    for tt, pgi in enumerate(grp):
                        col = b * NP + pgi
                        kb._deps("pool", [b_idx], [b_pg[k][tt]])
                        k2 = kb.dnext
                        kb.dnext = (kb.dnext + 1) % len(kb.dsem)
                        si = kb.dsem[k2]
                        if kb.dtot[k2] > 0:
                            kb._wait("pool", (si, kb.dtot[k2]))
                        ins = nc.gpsimd.indirect_dma_start(out=pg[k][:, tt, :], out_offset=None, in_=src_d[:, :],
                                                           in_offset=bass.IndirectOffsetOnAxis(ap=idx[:, col:col + 1], axis=0))
                        kb.dtot[k2] += 16
                        ev = (si, kb.dtot[k2])
                        ins.then_inc(kb.sems[si], 16)
                        b_idx.r.append(ev)
                        b_pg[k][tt].w = ev
                        b_pg[k][tt].r = []
                    return pg[k], b_pg[k]

                def score_mm(p, bp, tt, rhs_tile, rbuf):
                    for h in range(8):
                        for j in range(2):
                            kb.op("pe", lambda h=h, j=j: nc.tensor.matmul(p[0:40, tt * 128:(tt + 1) * 128], lhsT=QB[:, h, j * 40:(j + 1) * 40],
                                                                       rhs=rhs_tile[:, h * 128:(h + 1) * 128],
                                                                       start=(h == 0 and j == 0), stop=(h == 7 and j == 1)), [SB["QB"], rbuf], [bp])
                for g, grp in enumerate(groups):
                    buf, bbuf = load_group(grp, poolk_d)
                    n = len(grp)
                    cb_, bcb = pgb[g % 2], b_pgb[g % 2]
                    kb.op("act", lambda buf=buf, cb_=cb_, n=n: nc.scalar.copy(out=cb_[:, 0:n, :], in_=buf[:, 0:n, :]), bbuf[0:n], [bcb])
                    p, bp = rbank()
                    for h in range(8):
                        for j in range(2):
                            kb.op("pe", lambda h=h, j=j, p=p, cb_=cb_, n=n: nc.tensor.matmul(
                                p[0:40, 0:n * 128].rearrange("p (t k) -> p t k", t=n), lhsT=QBb[:, h, j * 40:(j + 1) * 40],
                                rhs=cb_[:, 0:n, h * 128:(h + 1) * 128], start=(h == 0 and j == 0), stop=(h == 7 and j == 1)),
                                [SB["QBb"], bcb], [bp])
                    kb.op("dve", lambda p=p, g=g, n=n: nc.vector.tensor_copy(out=Sall[:, g * GP * 128:g * GP * 128 + n * 128], in_=p[0:40, 0:n * 128]),
                          [bp], [SB["Sall"]])
                p, bp = rbank()
                score_mm(p, bp, 0, KTs[:].rearrange("p h t -> p (h t)"), SB["KTs"])
                kb.op("act", lambda p=p: nc.scalar.copy(out=Sall[:, NP * 128:NT * 128], in_=p[0:40, 0:128]), [bp], [SB["Sall"]])
                kb.op("dve", lambda: nc.vector.memset(Sall[:, NP * 128 + 1:NT * 128], -1e30), [], [SB["Sall"]])
                kb.op("dve", lambda: nc.vector.tensor_reduce(out=st40[:, 0:1], in_=Sall[:], axis=AX.X, op=ALU.max), [SB["Sall"]], [SB["st40"]])
                kb.op("dve", lambda: nc.vector.tensor_scalar(out=st40[:, 1:2], in0=st40[:, 0:1], scalar1=-0.125, scalar2=None, op0=ALU.mult),
                      [SB["st40"]], [SB["st40"]])
                kb.op("act", lambda: nc.scalar.activation(out=Sall[:], in_=Sall[:], func=AF.Exp, scale=0.125, bias=st40[:, 1:2], accum_out=st40[:, 2:3]),
                      [SB["Sall"], SB["st40"]], [SB["Sall"], SB["st40"]])
                act_fence([SB["st40"]])
                kb.op("dve", lambda: nc.vector.reciprocal(out=st40[:, 3:4], in_=st40[:, 2:3]), [SB["st40"]], [SB["st40"]])
                kb.op("dve", lambda: nc.vector.tensor_tensor(out=st40[:, 4:5], in0=st40[:, 3:4], in1=lamcol[:], op=ALU.mult), [SB["st40"], b_c2], [SB["st40"]])
                kb.op("dve", lambda: nc.vector.tensor_scalar(out=Sall[:], in0=Sall[:], scalar1=st40[:, 4:5], scalar2=None, op0=ALU.mult),
                      [SB["Sall"], SB["st40"]], [SB["Sall"]])
                for t0 in range(0, NT, 12):
                    ts_ = list(range(t0, min(t0 + 12, NT)))
                    p, bp = rbank()
                    for sl, t in enumerate(ts_):
                        kb.op("pe", lambda sl=sl, t=t, p=p: nc.tensor.transpose(out=p[:, sl * 40:(sl + 1) * 40], in_=Sall[:, t * 128:(t + 1) * 128],
                                                                               identity=identf[0:40, 0:40]), [SB["Sall"], b_cst], [bp])
                    n = len(ts_)
                    kb.op("act", lambda p=p, t0=t0, n=n: nc.scalar.copy(out=PnT[:, t0:t0 + n, :].rearrange("p t c -> p (t c)"), in_=p[:, 0:n * 40]),
                          [bp], [SB["PnT"]])
                for g, grp in enumerate(groups if not os.environ.get('DBG_NOPV') else []):
                    buf, bbuf = load_group(grp, poolv_d)
                    for tt, t in enumerate(grp if not os.environ.get('DBG_NOPVMM') else []):
                        for half in range(2):
                            kb.op("pe", lambda tt=tt, t=t, half=half, buf=buf: nc.tensor.matmul(
                                ps[half][0:40, :], lhsT=PnT[:, t, :], rhs=buf[:, tt, half * 512:(half + 1) * 512],
                                start=(t == 0), stop=False), [SB["PnT"], bbuf[tt]], [b_ps[half]])
                for half in range(2):
                    kb.op("pe", lambda half=half: nc.tensor.matmul(ps[half][0:40, :], lhsT=PnT[:, NP, :], rhs=Vs[:, half * 512:(half + 1) * 512],
                                                                  start=False, stop=True), [SB["PnT"], SB["Vs"]], [b_ps[half]])
                for half in range(2):
                    kb.op("dve", lambda half=half: nc.vector.tensor_tensor(
                        out=tmp40[:, half * 512:(half + 1) * 512].rearrange("p (h d) -> p h d", h=4),
                        in0=ps[half][0:40, :].rearrange("p (h d) -> p h d", h=4),
                        in1=M40[:, half * 4:(half + 1) * 4].unsqueeze(2).to_broadcast([40, 4, 128]), op=ALU.mult), [b_ps[half], b_c2], [SB["tmp40"]])
                kb.op("dve", lambda: nc.vector.memset(osb[:], 0.0), [], [CB["osb"]])
                for half in range(2):
                    p, bp = rbank()
                    kb.op("pe", lambda half=half, p=p: nc.tensor.matmul(p[0:1, :], lhsT=onesf[0:40, 0:1], rhs=tmp40[:, half * 512:(half + 1) * 512],
                                                                       start=True, stop=True), [SB["tmp40"], b_misc], [bp])
                    kb.op("act", lambda half=half, p=p: nc.scalar.copy(out=osbf[0:1, half * 512:(half + 1) * 512], in_=p[0:1, :]), [bp], [CB["osb"]])

            def tail(i):
                S_ = SETS[i % 2]
                CB = CBS[i % 2]
                ht, jk, hsb, hnT, qt, gate, cs, rt, qb, qT, sq, cb16, dg, cbias, osb, t1, ob, oT = [S_[n_] for n_ in NAMES]
                yield
                kb.op("dve", lambda: nc.vector.tensor_tensor(out=t1[:], in0=osb[:], in1=osb[:], op=ALU.mult), [CB["osb"]], [CB["t1"]])
                kb.op("dve", lambda: nc.vector.tensor_reduce(out=sq[:, 48:56], in_=t1[:], axis=AX.X, op=ALU.add), [CB["t1"]], [CB["sq"]])
                rstd_c(sq[:, 48:56], sq[:, 56:64], 128, [CB["sq"]])
                kb.op("dve", lambda: nc.vector.tensor_tensor(out=osb[:], in0=osb[:], in1=sq[:, 56:64].unsqueeze(2).to_broadcast([128, 8, 128]), op=ALU.mult),
                      [CB["osb"], CB["sq"]], [CB["osb"]])
                kb.op("dve", lambda: nc.vector.tensor_tensor(out=osb[:], in0=osb[:], in1=sgb[:].unsqueeze(1).to_broadcast([128, 8, 128]), op=ALU.mult),
                      [CB["osb"], b_c0], [CB["osb"]])
                kb.op("dve", lambda: nc.vector.tensor_tensor(out=ob[:], in0=osb[:].rearrange("p h d -> p (h d)"), in1=gate[:], op=ALU.mult),
                      [CB["osb"], CB["gate"]], [CB["ob"]])
                p, bp = rbank()
                pb = p[:].bitcast(BF16)
                for h in range(8):
                    kb.op("pe", lambda h=h, pb=pb: nc.tensor.transpose(out=pb[:, h * 128:(h + 1) * 128], in_=ob[:, h * 128:(h + 1) * 128],
                                                                     identity=identb[:]), [CB["ob"], b_misc], [bp])
                kb.op("act", lambda pb=pb: nc.scalar.copy(out=oT[:].rearrange("p h t -> p (h t)"), in_=pb), [bp], [CB["oT"]])
                for blk in range(2):
                    yield
                    p, bp = rbank()
                    for k in range(8):
                        kb.op("pe", lambda k=k, blk=blk, p=p: nc.tensor.matmul(p[:], lhsT=oT[:, k, :], rhs=woutb[:, k, blk * 512:(blk + 1) * 512],
                                                                           start=(k == 0), stop=(k == 7)), [CB["oT"], b_wb], [bp])
                    kb.op("dve", lambda blk=blk, p=p: nc.vector.tensor_tensor(out=ht[:, blk * 512:(blk + 1) * 512], in0=ht[:, blk * 512:(blk + 1) * 512],
                                                                            in1=p[:], op=ALU.add), [bp, CB["ht"]], [CB["ht"]])
                yield
                kb.op("act", lambda: nc.scalar.activation(out=jk[:], in_=ht[:], func=AF.Square, accum_out=sq[:, 2:3]), [CB["ht"]], [CB["jk"], CB["sq"]])
                rstd_c(sq[:, 2:3], sq[:, 3:4], D, [CB["sq"]])
                kb.op("dve", lambda: nc.vector.scalar_tensor_tensor(out=jk[:], in0=ht[:], scalar=sq[:, 3:4], in1=nfb[:], op0=ALU.mult, op1=ALU.mult),
                      [CB["ht"], CB["sq"], b_c0], [CB["jk"]])
                if is_sm:
                    kb.dma("sp", ys_o[i:i + 1, :], jk[0:1, :], reads=[CB["jk"]])
                else:
                    kb.dma("sp", yp_o[i * 128:(i + 1) * 128, :], jk[:], reads=[CB["jk"]])

            def run(gen):
                for _ in gen:
                    pass

            if is_sm:
                for i in range(NS):
                    run(part1(i))
                    sample_attn(i)
                    run(tail(i))
            else:
                run(part1(0))
                for i in range(NCH):
                    side = []
                    if i >= 1:
                        side.append(tail(i - 1))
                    if i + 1 < NCH:
                        side.append(part1(i + 1))
                    for _ in prompt_attn(i):
                        if side:
                            try:
                                next(side[0])
                            except StopIteration:
                                side.pop(0)
                    for g_ in side:
                        run(g_)
                run(tail(NCH - 1))
            kb.barrier()

        phaseC("prompt")
        if os.environ.get("DBG_STOP") == "C":
            return nc
        esR.close()
        phaseC("sample")
        kb.barrier(engines=("sp",))
    return nc


def _rope_tables(pos):
    inv = (500000.0 ** (-(np.arange(0, 16, 2, dtype=np.float32)) / np.float32(16))).astype(np.float32)
    ang = pos.astype(np.float32)[:, None] * inv[None, :]
    return np.cos(ang).astype(np.float32), np.sin(ang).astype(np.float32)


def _consts():
    i = np.arange(128)
    tri = (i[:, None] <= i[None, :]).astype(np.float32)
    astr = (i[:, None] > i[None, :]).astype(np.float32)
    ident = np.eye(128, dtype=np.float32)
    return np.ascontiguousarray(np.concatenate([tri, astr, ident], axis=1))


def _consts2():
    c = np.zeros((128, 652), np.float32)
    e = np.zeros((128, 8, 2, 40), np.float32)
    for h in range(8):
        e[0:64, h, 0, h] = 1.0
        e[64:128, h, 1, 32 + h] = 1.0
    c[:, 0:640] = e.reshape(128, 640)
    for h in range(8):
        c[h, 640 + h] = 1.0
        c[32 + h, 640 + h] = 1.0
    c[:, 648] = np.arange(128)
    c[0:8, 649] = 1.0
    c[32:40, 650] = 1.0
    return c


def make_in_maps(inp, ncores, T, NS, NP=64, past=PAST):
    f = lambda a: np.ascontiguousarray(np.asarray(a, dtype=np.float32))
    cosp, sinp = _rope_tables(np.arange(T))
    coss, sins = _rope_tables(np.full((128,), past))
    convw = f(np.asarray(inp["conv_w"])[0].reshape(4, 24, 128).transpose(2, 1, 0))
    convb = f(np.asarray(inp["conv_b"])[0].reshape(24, 128).T)
    na = f(np.asarray(inp["norm_a"])[0].reshape(8, 128).T)
    gn = f(np.asarray(inp["gnorm_a"])[0].reshape(16, 128).T)
    nkv = f(np.asarray(inp["norm_kv"]).reshape(8, 128).T)
    shared = dict(w_in_a=f(inp["w_in_a"][0]), w_out_a=f(inp["w_out_a"][0]), w_kv=f(inp["w_kv"]), convw=convw, convb=convb, na=na, gn=gn,
                  nkv=nkv, dtb=f(inp["dt_bias"][0]), alog=f(inp["a_log"][0]), dsk=f(inp["d_skip"][0]), cst=_consts(),
                  cosp=f(cosp), sinp=f(sinp), coss=f(coss), sins=f(sins),
                  w_in_b=f(inp["w_in_b"][0]), w_out_b=f(inp["w_out_b"][0]), nb=f(np.asarray(inp["norm_b"])[0].reshape(8, 128).T),
                  lamv=f(np.stack([np.asarray(inp["lambda_q1"])[0], np.asarray(inp["lambda_k1"])[0],
                                   np.asarray(inp["lambda_q2"])[0], np.asarray(inp["lambda_k2"])[0]])),
                  subln=f(np.asarray(inp["subln_b"])[0]), nf=f(inp["norm_f"]))
    ck = np.asarray(inp["cache_k"])
    npool = ck.shape[0]
    poolk = np.ascontiguousarray(ck.reshape(npool, 128, 8, 128).transpose(0, 3, 2, 1)).reshape(npool * 128, D)
    poolv = np.ascontiguousarray(np.asarray(inp["cache_v"]).reshape(npool * 128, D))
    shared["poolk"] = poolk
    shared["poolv"] = poolv
    shared["cst2"] = _consts2()
    ptab = np.asarray(inp["page_table"]).astype(np.int32)
    maps = []
    for c in range(ncores):
        m = dict(shared)
        m["ptab"] = np.ascontiguousarray(ptab[c * NS:(c + 1) * NS, :NP].reshape(-1))
        m["x"] = f(inp["x_prompt"][c][:T])
        m["xsmp"] = f(np.asarray(inp["x_sample"])[c * NS:(c + 1) * NS, 0])
        sc = np.asarray(inp["state_conv"])[0, c * NS:(c + 1) * NS]
        m["sconv"] = f(sc.reshape(NS, 3, 24, 128).transpose(0, 3, 2, 1))
        ss = np.asarray(inp["state_ssm"])[0, c * NS:(c + 1) * NS]
        m["sssm"] = f(ss.reshape(NS, DI, 128).transpose(0, 2, 1))
        maps.append(m)
    return maps


_NC_CACHE = {}


def kernel(**inp):
    ncores, T, NS = 8, 2048, 4
    key = (T, NS)
    if key not in _NC_CACHE:
        _NC_CACHE[key] = build(T, NS)
    nc = _NC_CACHE[key]
    maps = make_in_maps(inp, ncores, T, NS)
    res = run_bass_kernel_spmd(nc, maps, core_ids=list(range(ncores))).results
    cat = lambda k: np.stack([np.asarray(r[k]) for r in res])
    y_p = cat("y_p")
    y_s = cat("y_s").reshape(ncores * NS, 1, D)
    k_p = cat("k_p").reshape(ncores, T, 8, 2, 64)
    v_p = cat("v_p").reshape(ncores, T, 8, 128)
    conv_p = cat("conv_p").transpose(0, 3, 2, 1).reshape(1, ncores, 3, CONV)
    ssm_p = cat("ssm_p").transpose(0, 2, 1).reshape(1, ncores, 32, 64, 128)
    k_s = cat("k_s").reshape(ncores * NS, 1, 8, 2, 64)
    v_s = cat("v_s").reshape(ncores * NS, 1, 8, 128)
    conv_s = cat("conv_s").reshape(ncores * NS, 128, 24, 3).transpose(0, 3, 2, 1).reshape(1, ncores * NS, 3, CONV)
    ssm_s = cat("ssm_s").reshape(ncores * NS, 128, DI).transpose(0, 2, 1).reshape(1, ncores * NS, 32, 64, 128)
    f = lambda a: np.ascontiguousarray(a, dtype=np.float32)
    return (f(y_p), f(y_s), f(k_p), f(v_p), f(conv_p), f(ssm_p), f(k_s), f(v_s), f(conv_s), f(ssm_s))
```
